# Optimizing a Trainium2 kernel written in Bass

```python
import jax, jax.numpy as jnp
from jax import lax
import numpy as np

D_MODEL = 1024
BATCH = 8
SEQ = 4096
DEPTH = 1

D_RNN = 1536
RNN_BLOCKS = 16
RNN_BLOCK_W = D_RNN // RNN_BLOCKS
RNN_CONV_W = 4
RNN_CONV_PAD = (2, 1)
RG_C = 8.0
D_CONV = D_MODEL
CONV_W = 31
CONV_PAD = ((CONV_W - 1) // 2, (CONV_W - 1) // 2)
D_FF = 4 * D_MODEL
N_BRANCH = 2
LN_EPS = 1e-5
DEEPNORM_ALPHA = (2.0 * DEPTH) ** 0.25
DEEPNORM_BETA = (8.0 * DEPTH) ** -0.25
SPLIT_RNN_X = D_RNN
SPLIT_RNN_Y = SPLIT_RNN_X + D_RNN
SPLIT_CONV = SPLIT_RNN_Y + 2 * D_CONV
D_IN = SPLIT_CONV + N_BRANCH * D_MODEL

kernel_name = "hybrid_rglru_conformer_encoder_layer"


def layer_norm(x, g, b):
    xf = x.astype(jnp.float32)
    mu = jnp.mean(xf, axis=-1, keepdims=True)
    var = jnp.mean(jnp.square(xf - mu), axis=-1, keepdims=True)
    return ((xf - mu) * lax.rsqrt(var + LN_EPS)).astype(x.dtype) * g + b


def depthwise_conv(x, w, b, pad):
    y = lax.conv_general_dilated(
        x, w[:, None, :], window_strides=(1,), padding=[pad],
        dimension_numbers=("NWC", "WIO", "NWC"), feature_group_count=x.shape[-1])
    return y + b


def rg_lru_bidir(u, gate_w, gate_b, a_param):
    B, T, _ = u.shape
    ub = u.reshape(B, T, RNN_BLOCKS, RNN_BLOCK_W)
    gates = jnp.einsum("btni,dgnio->dgbtno", ub, gate_w) + gate_b[:, :, None, None]
    gates = jax.nn.sigmoid(gates.astype(jnp.float32)).reshape(2, 2, B, T, D_RNN)
    r, i = gates[:, 0], gates[:, 1]
    log_a = -RG_C * r * jax.nn.softplus(-a_param.astype(jnp.float32))[:, None, None, :]
    a = jnp.exp(log_a)
    bx = jnp.sqrt(-jnp.expm1(2.0 * log_a)) * (i * u.astype(jnp.float32)[None])
    a = jnp.stack([a[0], jnp.flip(a[1], axis=1)])
    bx = jnp.stack([bx[0], jnp.flip(bx[1], axis=1)])
    a_t = jnp.moveaxis(a, 2, 0)
    b_t = jnp.moveaxis(bx, 2, 0)

    def step(h, ab):
        at, bt = ab
        h = at * h + bt
        return h, h

    _, hs = lax.scan(step, jnp.zeros((2, B, D_RNN), jnp.float32), (a_t, b_t))
    hs = jnp.moveaxis(hs, 0, 2)
    return (hs[0] + jnp.flip(hs[1], axis=1)).astype(u.dtype)


def setup_inputs(seed: int = 0) -> dict:
    key = jax.random.key(seed)
    ks = jax.random.split(key, 32)
    f32 = jnp.float32
    L = DEPTH

    def nrm(k, shape, scale):
        return jax.random.normal(k, shape, f32) * scale

    x = jax.random.normal(ks[0], (BATCH, SEQ, D_MODEL), f32)
    emb_ln_g = 1.0 + nrm(ks[1], (D_MODEL,), 0.02)
    emb_ln_b = nrm(ks[2], (D_MODEL,), 0.02)
    w_in = nrm(ks[3], (L, D_MODEL, D_IN), D_MODEL ** -0.5)
    b_in = nrm(ks[4], (L, D_IN), 0.02)
    rnn_conv_w = nrm(ks[5], (L, RNN_CONV_W, D_RNN), RNN_CONV_W ** -0.5)
    rnn_conv_b = nrm(ks[6], (L, D_RNN), 0.02)
    rg_gate_w = nrm(ks[7], (L, 2, 2, RNN_BLOCKS, RNN_BLOCK_W, RNN_BLOCK_W), RNN_BLOCK_W ** -0.5)
    rg_gate_b = nrm(ks[8], (L, 2, 2, RNN_BLOCKS, RNN_BLOCK_W), 0.02)
    a_pow = jax.random.uniform(ks[9], (L, 2, D_RNN), f32, 0.9, 0.999)
    s = a_pow ** (1.0 / RG_C)
    rg_a_param = jnp.log(s) - jnp.log1p(-s)
    w_branch_a = nrm(ks[10], (L, D_RNN, D_MODEL), D_RNN ** -0.5 * DEEPNORM_BETA)
    conv_w = nrm(ks[11], (L, CONV_W, D_CONV), CONV_W ** -0.5)
    conv_b = nrm(ks[12], (L, D_CONV), 0.02)
    conv_ln_g = 1.0 + nrm(ks[13], (L, D_CONV), 0.02)
    conv_ln_b = nrm(ks[14], (L, D_CONV), 0.02)
    w_branch_b = nrm(ks[15], (L, D_CONV, D_MODEL), D_CONV ** -0.5 * DEEPNORM_BETA)
    w_out = nrm(ks[16], (L, D_MODEL, D_MODEL), D_MODEL ** -0.5 * DEEPNORM_BETA)
    b_out = nrm(ks[17], (L, D_MODEL), 0.02)
    ln1_g = 1.0 + nrm(ks[18], (L, D_MODEL), 0.02)
    ln1_b = nrm(ks[19], (L, D_MODEL), 0.02)
    w_up = nrm(ks[20], (L, D_MODEL, D_FF), D_MODEL ** -0.5 * DEEPNORM_BETA)
    b_up = nrm(ks[21], (L, D_FF), 0.02)
    w_down = nrm(ks[22], (L, D_FF, D_MODEL), D_FF ** -0.5 * DEEPNORM_BETA)
    b_down = nrm(ks[23], (L, D_MODEL), 0.02)
    ln2_g = 1.0 + nrm(ks[24], (L, D_MODEL), 0.02)
    ln2_b = nrm(ks[25], (L, D_MODEL), 0.02)
    return {
        "x": x, "emb_ln_g": emb_ln_g, "emb_ln_b": emb_ln_b,
        "w_in": w_in, "b_in": b_in,
        "rnn_conv_w": rnn_conv_w, "rnn_conv_b": rnn_conv_b,
        "rg_gate_w": rg_gate_w, "rg_gate_b": rg_gate_b, "rg_a_param": rg_a_param,
        "w_branch_a": w_branch_a,
        "conv_w": conv_w, "conv_b": conv_b, "conv_ln_g": conv_ln_g, "conv_ln_b": conv_ln_b,
        "w_branch_b": w_branch_b,
        "w_out": w_out, "b_out": b_out, "ln1_g": ln1_g, "ln1_b": ln1_b,
        "w_up": w_up, "b_up": b_up, "w_down": w_down, "b_down": b_down,
        "ln2_g": ln2_g, "ln2_b": ln2_b,
    }


def reference(x, emb_ln_g, emb_ln_b, w_in, b_in, rnn_conv_w, rnn_conv_b, rg_gate_w, rg_gate_b,
              rg_a_param, w_branch_a, conv_w, conv_b, conv_ln_g, conv_ln_b, w_branch_b,
              w_out, b_out, ln1_g, ln1_b, w_up, b_up, w_down, b_down, ln2_g, ln2_b):
    B, T, _ = x.shape
    h = layer_norm(x, emb_ln_g, emb_ln_b)
    for l in range(DEPTH):
        proj = h @ w_in[l] + b_in[l]
        xr = proj[..., :SPLIT_RNN_X]
        yr = proj[..., SPLIT_RNN_X:SPLIT_RNN_Y]
        xc = proj[..., SPLIT_RNN_Y:SPLIT_CONV]
        gl = proj[..., SPLIT_CONV:]

        xr = depthwise_conv(xr, rnn_conv_w[l], rnn_conv_b[l], RNN_CONV_PAD)
        hr = rg_lru_bidir(xr, rg_gate_w[l], rg_gate_b[l], rg_a_param[l])
        y_a = (jax.nn.gelu(yr) * hr) @ w_branch_a[l]

        c = jax.nn.glu(xc, axis=-1)
        c = depthwise_conv(c, conv_w[l], conv_b[l], CONV_PAD)
        c = jax.nn.silu(layer_norm(c, conv_ln_g[l], conv_ln_b[l]))
        y_b = c @ w_branch_b[l]

        g = jax.nn.sigmoid(gl).reshape(B, T, N_BRANCH, D_MODEL)
        mixed = (g[:, :, 0] * y_a + g[:, :, 1] * y_b) @ w_out[l] + b_out[l]
        h = layer_norm(DEEPNORM_ALPHA * h + mixed, ln1_g[l], ln1_b[l])

        m = jnp.square(jax.nn.relu(h @ w_up[l] + b_up[l]))
        h = layer_norm(DEEPNORM_ALPHA * h + (m @ w_down[l] + b_down[l]), ln2_g[l], ln2_b[l])
    return h
```

```python
import contextlib
import numpy as np
import concourse.bass as bass
import concourse.mybir as mybir
from concourse.ap import AP
from concourse.bass_utils import run_bass_kernel_spmd

F32 = mybir.dt.float32
BF16 = mybir.dt.bfloat16
AF = mybir.ActivationFunctionType
ALU = mybir.AluOpType

T = 4096
D = 1024
DR = 1536
DIN = 7168
DFF = 4096
NB = 16
BW = 96
ALPHA = 2.0 ** 0.25
EPS = 1e-5
SAME_ENGINE_SYNC = ("pool", "dve", "act")


class Res:
    __slots__ = ("name", "w", "r")

    def __init__(self, name):
        self.name = name
        self.w = {}
        self.r = {}


class Sched:
    ENGS = ("pe", "act", "dve", "pool", "sp")

    def __init__(self, nc, stack):
        self.nc = nc
        self.stack = stack
        self.q = {e: [] for e in self.ENGS}
        self.lane_sem = {}
        self.lane_cnt = {}
        self.seen = {e: {} for e in self.ENGS}
        for e in self.ENGS:
            self.new_lane(e)
        self.n_dma_lanes = 0

    def new_lane(self, name):
        sem = self.stack.enter_context(self.nc.semaphore("s_" + name))
        self.lane_sem[name] = sem
        self.lane_cnt[name] = 0
        return name

    def dma_lane(self):
        self.n_dma_lanes += 1
        return self.new_lane("d%d" % self.n_dma_lanes)

    def _deps(self, eng, reads, writes, force=False):
        deps = {}

        def need(lane, v):
            if deps.get(lane, 0) < v:
                deps[lane] = v
        for b in reads:
            for lane, v in b.w.items():
                need(lane, v)
        for b in writes:
            for lane, v in b.w.items():
                need(lane, v)
            for lane, v in b.r.items():
                need(lane, v)
        waits = []
        for lane, v in deps.items():
            if lane == eng and eng not in SAME_ENGINE_SYNC and not force:
                continue
            if self.seen[eng].get(lane, 0) >= v:
                continue
            self.seen[eng][lane] = v
            waits.append((self.lane_sem[lane], v))
        return waits

    @staticmethod
    def _commit(ticket, reads, writes):
        lane, v = ticket
        for b in reads:
            if b.r.get(lane, 0) < v:
                b.r[lane] = v
        for b in writes:
            if b.w.get(lane, 0) < v:
                b.w[lane] = v

    def op(self, eng, fn, reads=(), writes=(), self_sync=False):
        waits = self._deps(eng, reads, writes, self_sync)
        self.lane_cnt[eng] += 1
        ticket = (eng, self.lane_cnt[eng])
        sem = self.lane_sem[eng]

        def emit(E, fn=fn, waits=waits, sem=sem):
            for s, v in waits:
                E.wait_ge(s, v)
            fn(E).then_inc(sem, 1)
        self.q[eng].append(emit)
        self._commit(ticket, reads, writes)

    def dma(self, eng, lane, fn, reads=(), writes=(), writes_free=()):
        waits = self._deps(eng, reads, writes)
        self.lane_cnt[lane] += 16
        ticket = (lane, self.lane_cnt[lane])
        sem = self.lane_sem[lane]

        def emit(E, fn=fn, waits=waits, sem=sem):
            for s, v in waits:
                E.wait_ge(s, v)
            fn(E).then_inc(sem, 16)
        self.q[eng].append(emit)
        self._commit(ticket, reads, tuple(writes) + tuple(writes_free))

    def barrier(self, engs=None):
        for eng in (engs or self.ENGS):
            waits = []
            for lane, v in self.lane_cnt.items():
                if v == 0 or lane == eng:
                    continue
                if self.seen[eng].get(lane, 0) >= v:
                    continue
                self.seen[eng][lane] = v
                waits.append((self.lane_sem[lane], v))

            def emit(E, waits=waits):
                for s, v in waits:
                    E.wait_ge(s, v)
            self.q[eng].append(emit)

    def run(self):
        with self.nc.Block() as block:
            @block.tensor
            def _(E):
                for f in self.q["pe"]:
                    f(E)

            @block.scalar
            def _(E):
                for f in self.q["act"]:
                    f(E)

            @block.vector
            def _(E):
                for f in self.q["dve"]:
                    f(E)

            @block.gpsimd
            def _(E):
                for f in self.q["pool"]:
                    f(E)

            @block.sync
            def _(E):
                for f in self.q["sp"]:
                    f(E)


def rev(ap):
    steps = [list(x) for x in ap.ap]
    st, cnt = steps[-1]
    off = ap.offset + st * (cnt - 1)
    steps[-1] = [-st, cnt]
    return AP(ap.tensor, off, steps)


_V = {}
_off = 0
for _nm, _n in [("b_xr", 12), ("b_yr", 12), ("b_xca", 8), ("b_xcb", 8), ("b_g0", 8), ("b_g1", 8),
                ("emb_g", 8), ("emb_b", 8), ("conv_b", 8), ("cln_g", 8), ("cln_b", 8),
                ("ln1_g", 8), ("ln1_b", 8), ("b_up", 32), ("convw", 8 * 31),
                ("w4", 64), ("b4", 16), ("gb", 64), ("ap", 32)]:
    _V[_nm] = _off
    _off += _n
NV = _off
ROWS = ["emb_g", "emb_b", "b_out", "ln1_g", "ln1_b", "b_down", "ln2_g", "ln2_b"]


def build_nc(debug=False):
    nc = bass.Bass("TRN2", target_bir_lowering=False)
    dt_in = lambda nm, shp: nc.dram_tensor(nm, shp, F32, kind="ExternalInput").ap()
    x = dt_in("x", [T, D])
    w_in = dt_in("w_in", [D, DIN])
    w_ba = dt_in("w_ba", [DR, D])
    w_bb = dt_in("w_bb", [D, D])
    w_out = dt_in("w_out", [D, D])
    w_up = dt_in("w_up", [D, DFF])
    w_dn = dt_in("w_dn", [DFF, D])
    wg_in = dt_in("wg", [BW, 64 * BW])
    vecs_in = dt_in("vecs", [128, NV])
    rows_in = dt_in("rows", [8, 128, D])
    ident_in = dt_in("ident", [128, 128])
    out = nc.dram_tensor("out", [T, D], F32, kind="ExternalOutput").ap()
    kw = {"kind": "ExternalOutput"} if debug else {}
    hT_d = nc.dram_tensor("hT_d", [D, T], BF16, **kw).ap()
    xr_d = nc.dram_tensor("xr_d", [DR, T], F32, **kw).ap()
    gy_d = nc.dram_tensor("gy_d", [DR, T], BF16, **kw).ap()
    c0_d = nc.dram_tensor("c0_d", [D, T], BF16, **kw).ap()
    c1_d = nc.dram_tensor("c1_d", [D, T], F32, **kw).ap()
    v_d = nc.dram_tensor("v_d", [DR, T], BF16, **kw).ap()
    xn1_d = nc.dram_tensor("xn1_d", [T, D], F32, **kw).ap()
    h1T_d = nc.dram_tensor("h1T_d", [D, T], BF16, **kw).ap()
    cst_d = nc.dram_tensor("cst_d", [2, 128, T], F32, **kw).ap()

    with contextlib.ExitStack() as st:
        S = Sched(nc, st)
        ARENA_F32 = 53000
        arena = st.enter_context(nc.sbuf_tensor("arena", [128, ARENA_F32], F32))
        base = nc.lookup_mloc(arena).addr
        ps_all = st.enter_context(nc.psum_tensor("ps_all", [128, 4096], F32))
        Rps = [Res("ps%d" % i) for i in range(8)]

        def bank(i, n=512, p=128):
            return ps_all[0:p, i * 512:i * 512 + n]

        cur = [0]
        cnt = [0]

        class Buf:
            def __init__(self, name, shape, dt, lane=False):
                nbytes = int(np.prod(shape[1:])) * (4 if dt == F32 else 2)
                nbytes = (nbytes + 63) // 64 * 64
                assert cur[0] + nbytes <= ARENA_F32 * 4, (name, cur[0], nbytes)
                cnt[0] += 1
                self.t = nc.alloc_sbuf_tensor_at("%s_%d" % (name, cnt[0]), list(shape), dt,
                                                 offset=base + cur[0])
                cur[0] += nbytes
                self.res = Res(name)
                self.lane = S.dma_lane() if lane else None

            def __getitem__(self, k):
                return self.t[k]

        def mm(o, l, r, start, stop, R, W):
            S.op("pe", lambda E: E.matmul(o, l, r, start=start, stop=stop), R, W)

        def trp(o, i, idn, R, W):
            S.op("pe", lambda E: E.transpose(o, i, idn), R, W)

        def act(o, i, func, R, W, bias=0.0, scale=1.0, self_sync=False):
            S.op("act", lambda E: E.activation(out=o, in_=i, func=func, bias=bias, scale=scale), R, W, self_sync)

        def ts(eng, o, i, s1, s2, op0, op1, R, W):
            if s2 is None:
                S.op(eng, lambda E: E.tensor_scalar(out=o, in0=i, scalar1=s1, scalar2=None, op0=op0), R, W)
            else:
                S.op(eng, lambda E: E.tensor_scalar(out=o, in0=i, scalar1=s1, scalar2=s2, op0=op0, op1=op1), R, W)

        def stt(o, i0, sc, i1, op0, op1, R, W):
            S.op("dve", lambda E: E.scalar_tensor_tensor(out=o, in0=i0, scalar=sc, in1=i1, op0=op0, op1=op1), R, W)

        def tt(eng, o, i0, i1, op, R, W):
            S.op(eng, lambda E: E.tensor_tensor(out=o, in0=i0, in1=i1, op=op), R, W)

        def cp(eng, o, i, R, W):
            S.op(eng, lambda E: E.tensor_copy(out=o, in_=i), R, W)

        def dma(eng, lane, o, i, R=(), W=(), WF=()):
            S.dma(eng, lane, lambda E: E.dma_start(out=o, in_=i), R, W, WF)

        R_hT, R_xr, R_gy, R_c0, R_c1, R_v, R_xn1, R_h1T, R_out = [Res(n) for n in
            ("hT_d", "xr_d", "gy_d", "c0_d", "c1_d", "v_d", "xn1_d", "h1T_d", "out")]
        R_cst = Res("cst_d")

        vecs = Buf("vecs", [128, NV], F32, lane=True)
        ident_f = Buf("ident_f", [128, 128], F32, lane=True)
        ident_b = Buf("ident_b", [128, 128], BF16)
        ones_f = Buf("ones_f", [128, 128], F32)
        hb = Buf("hb", [128, 64], F32)
        hc = Buf("hc", [128, 32], F32)
        tmpc = Buf("tmpc", [128, 32], F32)
        rs_all = Buf("rs_all", [128, 32], F32)
        nm_all = Buf("nm_all", [128, 32], F32)
        persist_end = cur[0]

        dma("sp", vecs.lane, vecs[:, :], vecs_in, W=[vecs.res])
        dma("sp", ident_f.lane, ident_f[:, :], ident_in, W=[ident_f.res])
        cp("dve", ident_b[:, :], ident_f[:, :], [ident_f.res], [ident_b.res])
        S.op("pool", lambda E: E.memset(ones_f[:, :], 1.0), [], [ones_f.res])
        V = lambda nm, j=0, n=1, p=128: vecs[0:p, _V[nm] + j:_V[nm] + j + n]
        ts("dve", hb[0:96, :], V("gb", 0, 64, 96), 0.5, None, ALU.mult, None, [vecs.res], [hb.res])
        act(tmpc[0:96, :], V("ap", 0, 32, 96), AF.Exp, [vecs.res], [tmpc.res], scale=-1.0)
        act(tmpc[0:96, :], tmpc[0:96, :], AF.Ln, [tmpc.res], [tmpc.res], bias=1.0, self_sync=True)
        ts("dve", hc[0:96, :], tmpc[0:96, :], -4.0, None, ALU.mult, None, [tmpc.res], [hc.res])

        def layer_norm_stats(xsrc, nsub, Rsrc, stb, mvb, rstd, nmr):
            for s in range(nsub):
                for hh in range(2):
                    S.op("dve", (lambda o=stb[:, s, hh, :], i=xsrc(s)[:, hh * 512:(hh + 1) * 512]:
                                 lambda E: E.bn_stats(out=o, in_=i))(), Rsrc, [stb.res])
                S.op("dve", (lambda s=s: lambda E: E.bn_aggr(
                    out=mvb[:, s, :], in_=stb[:, s, :, :].rearrange("p a b -> p (a b)")))(), [stb.res], [mvb.res])
            act(rstd[:, 0:nsub], mvb[:, 0:nsub, 1], AF.Sqrt, [mvb.res], [rstd.res], bias=EPS)
            S.op("dve", lambda E: E.reciprocal(out=rstd[:, 0:nsub], in_=rstd[:, 0:nsub]), [rstd.res], [rstd.res])
            stt(nmr[:, 0:nsub], mvb[:, 0:nsub, 0], -1.0, rstd[:, 0:nsub], ALU.mult, ALU.mult,
                [mvb.res, rstd.res], [nmr.res])

        def load_weight(buf, src, nk, ncols, c0, blk, reslist):
            srcv = src.rearrange("(kc p) c -> p kc c", p=128)
            for cb in range(ncols // blk):
                r = Res("wblk")
                reslist.append(r)
                dma("pool", S.dma_lane(), buf[:, 0:nk, cb * blk:(cb + 1) * blk],
                    srcv[:, :, c0 + cb * blk:c0 + (cb + 1) * blk], W=[r])

        psrot = [0]

        def next_bank(lo, hi):
            b = lo + psrot[0] % (hi - lo)
            psrot[0] += 1
            return b

        TT = 512
        w1 = Buf("w1", [128, 8, 5120], BF16)
        w1res = []
        load_weight(w1, w_in, 8, 5120, 0, 1024, w1res)
        xt = [Buf("xt%d" % i, [128, 4, D], F32, lane=True) for i in range(2)]
        xnb = Buf("xnb", [128, 4, D], BF16)
        hTt = [Buf("hTt%d" % i, [128, 8, TT], BF16, lane=True) for i in range(2)]
        xr_st = [Buf("xr_st%d" % i, [128, 6, TT], F32, lane=True) for i in range(2)]
        gy_st = [Buf("gy_st%d" % i, [128, 6, TT], BF16, lane=True) for i in range(2)]
        c0_st = [Buf("c0_st%d" % i, [128, 4, TT], BF16, lane=True) for i in range(2)]
        sg = [Buf("sg%d" % i, [128, TT], F32) for i in range(2)]
        stb = Buf("stb", [128, 4, 2, 6], F32)
        mvb = Buf("mvb", [128, 4, 2], F32)
        rstd = Buf("rstd", [128, 4], F32)
        nmr = Buf("nmr", [128, 4], F32)

        xv = x.rearrange("(n s p) f -> n p s f", s=4, p=128)
        NT1 = T // TT
        def p1_prepA(it):
            xb = xt[it % 2]
            layer_norm_stats(lambda s: xb[:, s, :], 4, [xb.res], stb, mvb, rstd, nmr)
            cp("dve", rs_all[:, it * 4:it * 4 + 4], rstd[:, 0:4], [rstd.res], [rs_all.res])
            cp("dve", nm_all[:, it * 4:it * 4 + 4], nmr[:, 0:4], [nmr.res], [nm_all.res])
            for s in range(4):
                act(xnb[:, s, :], xb[:, s, :], AF.Identity, [xb.res, rstd.res, nmr.res], [xnb.res],
                    bias=nmr[:, s:s + 1], scale=rstd[:, s:s + 1])

        def p1_prepB(it):
            hp = hTt[it % 2]
            for kc in range(8):
                b = next_bank(0, 2)
                pst = bank(b).bitcast(BF16)
                for s in range(4):
                    trp(pst[:, s * 128:(s + 1) * 128], xnb[:, s, kc * 128:(kc + 1) * 128], ident_b[:, :],
                        [xnb.res, ident_b.res], [Rps[b]])
                act(hp[:, kc, :], pst[:, 0:TT], AF.Identity, [Rps[b], vecs.res], [hp.res],
                    bias=V("emb_b", kc), scale=V("emb_g", kc))
            dma("sp", hp.lane, hT_d.rearrange("(kc p) t -> p kc t", p=128)[:, :, it * TT:(it + 1) * TT],
                hp[:, :, :], R=[hp.res], WF=[R_hT])

        dma("sp", xt[0].lane, xt[0][:, :, :], xv[0], W=[xt[0].res])
        dma("sp", xt[1].lane, xt[1][:, :, :], xv[1], W=[xt[1].res])
        p1_prepA(0)
        p1_prepB(0)
        for it in range(NT1):
            hb_ = hTt[it % 2]
            if it + 1 < NT1:
                p1_prepA(it + 1)

            def proj(col0, b):
                for kc in range(8):
                    mm(bank(b), w1[:, kc, col0:col0 + 128], hb_[:, kc, :], kc == 0, kc == 7,
                       [w1res[col0 // 1024], hb_.res], [Rps[b]])
            for c in range(12):
                b = next_bank(2, 8)
                proj(c * 128, b)
                stg = xr_st[c // 6]
                ts("dve", stg[:, c % 6, :], bank(b), V("b_xr", c), None, ALU.add, None,
                   [Rps[b], vecs.res], [stg.res])
                if c % 6 == 5:
                    dma("sp", stg.lane,
                        xr_d.rearrange("(kc p) t -> p kc t", p=128)[:, c - 5:c + 1, it * TT:(it + 1) * TT],
                        stg[:, :, :], R=[stg.res], WF=[R_xr])
            if it + 1 < NT1:
                p1_prepB(it + 1)
            if it + 2 < NT1:
                nb_ = xt[it % 2]
                dma("sp", nb_.lane, nb_[:, :, :], xv[it + 2], W=[nb_.res])
            for c in range(12):
                b = next_bank(2, 8)
                proj(DR + c * 128, b)
                stg = gy_st[c // 6]
                act(stg[:, c % 6, :], bank(b), AF.Gelu_apprx_tanh, [Rps[b], vecs.res], [stg.res],
                    bias=V("b_yr", c))
                if c % 6 == 5:
                    dma("sp", stg.lane,
                        gy_d.rearrange("(kc p) t -> p kc t", p=128)[:, c - 5:c + 1, it * TT:(it + 1) * TT],
                        stg[:, :, :], R=[stg.res], WF=[R_gy])
            for j in range(8):
                ba = next_bank(2, 8)
                proj(2 * DR + j * 128, ba)
                bb = next_bank(2, 8)
                proj(2 * DR + D + j * 128, bb)
                sgb = sg[j % 2]
                act(sgb[:, :], bank(bb), AF.Sigmoid, [Rps[bb], vecs.res], [sgb.res], bias=V("b_xcb", j))
                stg = c0_st[j // 4]
                stt(stg[:, j % 4, :], bank(ba), V("b_xca", j), sgb[:, :], ALU.add, ALU.mult,
                    [Rps[ba], sgb.res, vecs.res], [stg.res])
                if j % 4 == 3:
                    dma("sp", stg.lane,
                        c0_d.rearrange("(kc p) t -> p kc t", p=128)[:, j - 3:j + 1, it * TT:(it + 1) * TT],
                        stg[:, :, :], R=[stg.res], WF=[R_c0])

        S.barrier()
        cur[0] = persist_end
        wgb = Buf("wgb", [128, 64 * BW], BF16, lane=True)
        dma("pool", wgb.lane, wgb[0:BW, :], wg_in, W=[wgb.res])
        xrs = [Buf("xrs0", [128, T + 4], F32, lane=True)]
        gyb = Buf("gyb", [128, T], BF16, lane=True)
        ubs = [Buf("ub%d" % i, [128, T], F32) for i in range(2)]
        ubfs = [Buf("ubf%d" % i, [128, T], BF16) for i in range(2)]
        TRb = [Buf("TR%d" % d, [128, T], F32) for d in range(2)]
        TIb = [Buf("TI%d" % d, [128, T], F32) for d in range(2)]
        B0s = [Buf("B0_%d" % i, [128, T], F32) for i in range(2)]
        B1 = Buf("B1", [128, T], F32)
        D4 = [Buf("D4_%d" % i, [128, 4, BW], F32) for i in range(2)]
        P = BW
        xs = xrs[0]
        S.op("pool", lambda E: E.memset(xs[:, 0:2], 0.0), [], [xs.res])
        S.op("pool", lambda E: E.memset(xs[:, T + 2:T + 4], 0.0), [], [xs.res])

        def p2_loadx(n):
            dma("sp", xs.lane, xs[0:P, 2:T + 2], xr_d[n * BW:(n + 1) * BW, :], R=[R_xr], W=[xs.res])

        conv_pair = {}

        def p2_conv_prep(n):
            d4 = D4[n % 2]
            for k in range(4):
                act(d4[0:P, k, :], ident_f[0:P, 0:P], AF.Copy, [ident_f.res, vecs.res], [d4.res],
                    scale=V("w4", n * 4 + k, 1, P))

        def p2_conv_piece(n, p):
            ub, ubf, d4 = ubs[n % 2], ubfs[n % 2], D4[n % 2]
            q, hh = p // 2, p % 2
            if hh == 0:
                conv_pair[n] = 2 * (next_bank(0, 4))
            b2 = conv_pair[n]
            t0_ = q * 1024 + hh * 512
            for k in range(4):
                mm(bank(b2 + hh, 512, P), d4[0:P, k, :], xs[0:P, t0_ + k:t0_ + k + 512], k == 0, k == 3,
                   [d4.res, xs.res], [Rps[b2 + hh]])
            if hh == 1:
                src = ps_all[0:P, b2 * 512:b2 * 512 + 1024]
                ts("dve", ub[0:P, q * 1024:(q + 1) * 1024], src, V("b4", n, 1, P), None, ALU.add, None,
                   [Rps[b2], Rps[b2 + 1], vecs.res], [ub.res])
                ts("dve", ubf[0:P, q * 1024:(q + 1) * 1024], src, V("b4", n, 1, P), None, ALU.add, None,
                   [Rps[b2], Rps[b2 + 1], vecs.res], [ubf.res])
            if p == 7 and n + 1 < NB:
                p2_loadx(n + 1)

        def p2_gates(n, d):
            ubf = ubfs[n % 2]
            for q in range(4):
                if q > 0 and n + 1 < NB:
                    p2_conv_piece(n + 1, d * 4 + q - 1)
                for g in range(2):
                    b2 = 2 * (next_bank(0, 4))
                    idx = (d * 2 + g) * NB + n
                    for hh in range(2):
                        mm(bank(b2 + hh, 512, P), wgb[0:P, idx * BW:(idx + 1) * BW],
                           ubf[0:P, q * 1024 + hh * 512:q * 1024 + (hh + 1) * 512], True, True,
                           [wgb.res, ubf.res], [Rps[b2 + hh]])
                    dst = (TRb if g == 0 else TIb)[d]
                    act(dst[0:P, q * 1024:(q + 1) * 1024], ps_all[0:P, b2 * 512:b2 * 512 + 1024], AF.Tanh,
                        [Rps[b2], Rps[b2 + 1], hb.res], [dst.res], bias=hb[0:P, idx:idx + 1], scale=0.5)

        def p2_dir(n, d):
            Bd = B0s[n % 2] if d == 0 else B1
            ub = ubs[n % 2]
            hcv = hc[0:P, d * NB + n:d * NB + n + 1]
            act(TRb[d][0:P, :], TRb[d][0:P, :], AF.Exp, [TRb[d].res, hc.res], [TRb[d].res], bias=hcv, scale=hcv)
            act(Bd[0:P, :], TRb[d][0:P, :], AF.Square, [TRb[d].res], [Bd.res])
            act(Bd[0:P, :], Bd[0:P, :], AF.Sqrt, [Bd.res], [Bd.res], bias=0.25, scale=-0.25)
            stt(TIb[d][0:P, :], TIb[d][0:P, :], 1.0, ub[0:P, :], ALU.add, ALU.mult,
                [TIb[d].res, ub.res], [TIb[d].res])
            tt("dve", TIb[d][0:P, :], TIb[d][0:P, :], Bd[0:P, :], ALU.mult,
               [TIb[d].res, Bd.res], [TIb[d].res])
            f = (lambda a: a) if d == 0 else rev
            S.op("dve", (lambda o=f(Bd[0:P, 0:T]), a_=f(TRb[d][0:P, 0:T]), b_=f(TIb[d][0:P, 0:T]):
                         lambda E: E.tensor_tensor_scan(out=o, data0=a_, data1=b_, initial=0.0,
                                                        op0=ALU.mult, op1=ALU.add))(),
                 [TRb[d].res, TIb[d].res, Bd.res], [Bd.res])

        p2_loadx(0)
        p2_conv_prep(0)
        for p_ in range(8):
            p2_conv_piece(0, p_)
        for n in range(NB):
            dma("sp", gyb.lane, gyb[0:P, :], gy_d[n * BW:(n + 1) * BW, :], R=[R_gy], W=[gyb.res])
            if n + 1 < NB:
                p2_conv_prep(n + 1)
            p2_gates(n, 0)
            if n + 1 < NB:
                p2_conv_piece(n + 1, 3)
            p2_dir(n, 0)
            p2_gates(n, 1)
            if n + 1 < NB:
                p2_conv_piece(n + 1, 7)
            p2_dir(n, 1)
            tt("pool", TIb[1][0:P, :], B0s[n % 2][0:P, :], B1[0:P, :], ALU.add,
               [B0s[n % 2].res, B1.res], [TIb[1].res])
            tt("pool", gyb[0:P, :], TIb[1][0:P, :], gyb[0:P, :], ALU.mult, [TIb[1].res, gyb.res], [gyb.res])
            dma("sp", gyb.lane, v_d[n * BW:(n + 1) * BW, :], gyb[0:P, :], R=[gyb.res], WF=[R_v])

        S.barrier()
        cur[0] = persist_end
        wgl = Buf("wgl", [128, 8, 2048], BF16)
        wbb = Buf("wbb", [128, 8, D], BF16)
        wba = Buf("wba", [128, 12, D], BF16)
        wou = Buf("wou", [128, 8, D], BF16)
        wgl_r, wbb_r, wba_r, wou_r = [], [], [], []
        load_weight(wgl, w_in, 8, 2048, 5120, 1024, wgl_r)
        load_weight(wbb, w_bb, 8, D, 0, 1024, wbb_r)
        load_weight(wba, w_ba, 12, D, 0, 1024, wba_r)
        load_weight(wou, w_out, 8, D, 0, 512, wou_r)
        w3_end = cur[0]
        cxb = [Buf("cx%d" % i, [128, T + 32], BF16, lane=True) for i in range(2)]
        Dg = [Buf("Dg%d" % i, [128, 31, 128], BF16) for i in range(2)]
        c1s = [Buf("c1s%d" % i, [128, T], F32, lane=True) for i in range(2)]
        acc1 = Buf("acc1", [128, T], F32, lane=True)
        acc2 = Buf("acc2", [128, T], F32, lane=True)
        sqc = Buf("sqc", [128, T], F32)
        for i in range(2):
            S.op("pool", (lambda i=i: lambda E: E.memset(cxb[i][:, 0:15], 0.0))(), [], [cxb[i].res])
            S.op("pool", (lambda i=i: lambda E: E.memset(cxb[i][:, T + 15:T + 32], 0.0))(), [], [cxb[i].res])

        NPE = 28

        def build_diag(j):
            dg_ = Dg[j % 2]
            ia = ident_f[:, :]
            wa = V("convw", j * 31, NPE)
            pstep_i = ia.ap[0][0]
            pstep_w = wa.ap[0][0]
            in0 = AP(ia.tensor, ia.offset, [[pstep_i, 128], [0, NPE], [1, 128]])
            in1 = AP(wa.tensor, wa.offset, [[pstep_w, 128], [1, NPE], [0, 128]])
            tt("dve", dg_[:, 0:NPE, :], in0, in1, ALU.mult, [ident_f.res, vecs.res], [dg_.res])

        def p2c_load(j):
            cb_ = cxb[j % 2]
            dma("sp", cb_.lane, cb_[:, 15:T + 15], c0_d[j * 128:(j + 1) * 128, :], R=[R_c0], W=[cb_.res])
        p2c_load(0)
        build_diag(0)
        for j in range(8):
            cb_ = cxb[j % 2]
            dg = Dg[j % 2]
            co = c1s[j % 2]
            if j + 1 < 8:
                p2c_load(j + 1)
                build_diag(j + 1)
            ts("dve", co[:, :], cb_[:, NPE:NPE + T], V("convw", j * 31 + NPE), None, ALU.mult, None,
               [cb_.res, vecs.res], [co.res])
            for k in range(NPE + 1, 31):
                stt(co[:, :], cb_[:, k:k + T], V("convw", j * 31 + k), co[:, :], ALU.mult, ALU.add,
                    [cb_.res, co.res, vecs.res], [co.res])
            for tt_ in range(8):
                b = next_bank(0, 8)
                for k in range(NPE):
                    mm(bank(b), dg[:, k, :], cb_[:, tt_ * 512 + k:tt_ * 512 + k + 512], k == 0, k == NPE - 1,
                       [dg.res, cb_.res], [Rps[b]])
                tsl = slice(tt_ * 512, (tt_ + 1) * 512)
                stt(co[:, tsl], bank(b), V("conv_b", j), co[:, tsl], ALU.add, ALU.add,
                    [Rps[b], co.res, vecs.res], [co.res])
            dma("sp", co.lane, c1_d[j * 128:(j + 1) * 128, :], co[:, :], R=[co.res], WF=[R_c1])
            if j == 0:
                cp("pool", acc1[:, :], co[:, :], [co.res], [acc1.res])
                act(acc2[:, :], co[:, :], AF.Square, [co.res], [acc2.res])
            else:
                tt("pool", acc1[:, :], acc1[:, :], co[:, :], ALU.add, [acc1.res, co.res], [acc1.res])
                act(sqc[:, :], co[:, :], AF.Square, [co.res], [sqc.res])
                tt("pool", acc2[:, :], acc2[:, :], sqc[:, :], ALU.add, [acc2.res, sqc.res], [acc2.res])
        for tt_ in range(8):
            tsl = slice(tt_ * 512, (tt_ + 1) * 512)
            b1_ = next_bank(0, 8)
            b2_ = next_bank(0, 8)
            mm(bank(b1_), ones_f[:, :], acc1[:, tsl], True, True, [ones_f.res, acc1.res], [Rps[b1_]])
            mm(bank(b2_), ones_f[:, :], acc2[:, tsl], True, True, [ones_f.res, acc2.res], [Rps[b2_]])
            ts("dve", sqc[:, tsl], bank(b1_), 1.0 / D, None, ALU.mult, None, [Rps[b1_], acc1.res], [sqc.res])
            tt("dve", c1s[1][:, tsl], sqc[:, tsl], sqc[:, tsl], ALU.mult, [sqc.res], [c1s[1].res])
            stt(c1s[0][:, tsl], bank(b2_), 1.0 / D, c1s[1][:, tsl], ALU.mult, ALU.subtract,
                [Rps[b2_], c1s[1].res, acc2.res], [c1s[0].res])
        act(c1s[0][:, :], c1s[0][:, :], AF.Sqrt, [c1s[0].res], [c1s[0].res], bias=EPS)
        S.op("dve", lambda E: E.reciprocal(out=c1s[0][:, :], in_=c1s[0][:, :]), [c1s[0].res], [c1s[0].res])
        dma("sp", acc1.lane, cst_d[0], sqc[:, :], R=[sqc.res], WF=[R_cst])
        dma("sp", acc2.lane, cst_d[1], c1s[0][:, :], R=[c1s[0].res], WF=[R_cst])

        S.barrier()
        cur[0] = w3_end
        TT3 = 256
        NS = TT3 // 128
        rows = {}
        for nm in ("G1", "B1"):
            rows[nm] = Buf("row_" + nm, [128, D], F32, lane=True)
        rtmp = Buf("xn0", [128, D], F32, lane=True)
        ri = {n: i for i, n in enumerate(ROWS)}

        def make_rows(gn, bn, gsrc, bsrc, addsrc):
            dma("sp", rows[gn].lane, rows[gn][:, :], rows_in[ri[gsrc]], W=[rows[gn].res])
            ts("dve", rows[gn][:, :], rows[gn][:, :], ALPHA, None, ALU.mult, None, [rows[gn].res], [rows[gn].res])
            dma("sp", rows[bn].lane, rows[bn][:, :], rows_in[ri[bsrc]], W=[rows[bn].res])
            dma("sp", rtmp.lane, rtmp[:, :], rows_in[ri[addsrc]], W=[rtmp.res])
            stt(rows[bn][:, :], rows[bn][:, :], ALPHA, rtmp[:, :], ALU.mult, ALU.add,
                [rows[bn].res, rtmp.res], [rows[bn].res])

        halfv = Buf("halfv", [128, 32], F32)
        ts("dve", halfv[:, 0:16], V("cln_g", 0, 16), 0.5, None, ALU.mult, None, [vecs.res], [halfv.res])
        ts("dve", halfv[:, 16:32], V("b_g0", 0, 16), 0.5, None, ALU.mult, None, [vecs.res], [halfv.res])
        mhalf = Buf("mhalf", [128, TT3], F32)
        S.op("pool", lambda E: E.memset(mhalf[:, :], -0.5), [], [mhalf.res])
        hT3s = [Buf("hT3_%d" % i, [128, 8, TT3], BF16, lane=True) for i in range(2)]
        v3s = [Buf("v3_%d" % i, [128, 12, TT3], BF16, lane=True) for i in range(2)]
        c13 = Buf("c13", [128, 8, TT3], F32, lane=True)
        xC = Buf("xC", [128, NS, D], F32, lane=True)
        mean3 = Buf("mean3", [128, TT3], F32, lane=True)
        rstd3 = Buf("rstd3", [128, TT3], F32, lane=True)
        xnt = [Buf("xnt%d" % i, [128, TT3], F32) for i in range(2)]
        yt = [Buf("yt%d" % i, [128, TT3], F32) for i in range(2)]
        tht = [Buf("tht%d" % i, [128, TT3], F32) for i in range(2)]
        cBs = [Buf("cB%d" % i, [128, 8, TT3], BF16) for i in range(2)]
        t0b = [Buf("t0_%d" % i, [128, TT3], F32) for i in range(2)]
        t1b = [Buf("t1_%d" % i, [128, TT3], F32) for i in range(2)]
        u1b = [Buf("u1_%d" % i, [128, TT3], F32) for i in range(2)]
        u2b = [Buf("u2_%d" % i, [128, TT3], F32) for i in range(2)]
        zTs = [Buf("zT%d" % i, [128, 8, TT3], BF16) for i in range(2)]
        xn0 = rtmp
        tbss = [Buf("tbs%d" % i, [128, D], F32) for i in range(2)]
        xn1 = [Buf("xn1_%d" % i, [128, D], F32, lane=True) for i in range(2)]
        xn1bs = [Buf("xn1b%d" % i, [128, NS, D], BF16) for i in range(2)]
        h1T = [Buf("h1T%d" % i, [128, 8, TT3], BF16, lane=True) for i in range(2)]
        stb4 = Buf("stb4", [128, 4, 2, 6], F32)
        mvb4 = Buf("mvb4", [128, 4, 2], F32)
        vt4 = Buf("vt4", [128, 4], F32)
        rs1 = Buf("rs1", [128, 4], F32)
        nm1 = Buf("nm1", [128, 4], F32)

        xv3 = x.rearrange("(n s p) f -> n p s f", s=NS, p=128)
        hTv = hT_d.rearrange("(kc p) t -> p kc t", p=128)
        vv = v_d.rearrange("(kc p) t -> p kc t", p=128)
        c1v = c1_d.rearrange("(kc p) t -> p kc t", p=128)
        h1Tv = h1T_d.rearrange("(kc p) t -> p kc t", p=128)
        NT3 = T // TT3
        BS, BQ = 4, 5

        def rstd_pow(dst, var_ap, tmp, n, Rin, Rtmp, Rdst):
            ts("dve", tmp, var_ap, EPS, None, ALU.add, None, Rin, [Rtmp])
            tt("pool", dst, tmp, mhalf[:, 0:n], ALU.pow, [Rtmp, mhalf.res], [Rdst])

        def bcast_row(row, it):
            return cst_d[row][:, it * TT3:(it + 1) * TT3]

        def load_A(it):
            tsl = slice(it * TT3, (it + 1) * TT3)
            dma("sp", c13.lane, c13[:, :, :], c1v[:, :, tsl], R=[R_c1], W=[c13.res])
            dma("sp", mean3.lane, mean3[:, :], bcast_row(0, it), R=[R_cst], W=[mean3.res])
            dma("sp", rstd3.lane, rstd3[:, :], bcast_row(1, it), R=[R_cst], W=[rstd3.res])

        def load_B(it):
            tsl = slice(it * TT3, (it + 1) * TT3)
            h_, v_ = hT3s[it % 2], v3s[it % 2]
            dma("sp", h_.lane, h_[:, :, :], hTv[:, :, tsl], R=[R_hT], W=[h_.res])
            dma("sp", v_.lane, v_[:, :, :], vv[:, :, tsl], R=[R_v], W=[v_.res])

        def stage_A(it):
            cB = cBs[it % 2]
            for i in range(8 + 2):
                if i < 8:
                    kc = i
                    xb_ = xnt[kc % 2]
                    tt("dve", xb_[:, :], c13[:, kc, :], mean3[:, :], ALU.subtract, [c13.res, mean3.res], [xb_.res])
                    tt("dve", xb_[:, :], xb_[:, :], rstd3[:, :], ALU.mult, [xb_.res, rstd3.res], [xb_.res])
                if 1 <= i < 9:
                    kc = i - 1
                    xb_, yb_, th_ = xnt[kc % 2], yt[kc % 2], tht[kc % 2]
                    act(th_[:, :], xb_[:, :], AF.Tanh, [xb_.res, halfv.res], [th_.res],
                        bias=halfv[:, 8 + kc:9 + kc], scale=halfv[:, kc:kc + 1])
                    ts("dve", yb_[:, :], xb_[:, :], V("cln_g", kc), V("cln_b", kc), ALU.mult, ALU.add,
                       [xb_.res, vecs.res], [yb_.res])
                if 2 <= i:
                    kc = i - 2
                    yb_, th_ = yt[kc % 2], tht[kc % 2]
                    stt(cB[:, kc, :], th_[:, :], 1.0, yb_[:, :], ALU.add, ALU.mult, [th_.res, yb_.res], [cB.res])
                yield
            if it + 1 < NT3:
                load_A(it + 1)

        def stage_B(it):
            cB, zT = cBs[it % 2], zTs[it % 2]
            hT3, v3 = hT3s[it % 2], v3s[it % 2]
            if it + 1 < NT3:
                load_B(it + 1)

            def merge(oc):
                Y = 2 * (oc % 2) + 1
                RY = Rps[Y]
                s0, s1, u1, u2 = t0b[oc % 2], t1b[oc % 2], u1b[oc % 2], u2b[oc % 2]
                stt(u1[:, :], s0[:, :], 1.0, bank(Y)[:, 0:TT3], ALU.add, ALU.mult, [s0.res, RY], [u1.res])
                stt(u2[:, :], s1[:, :], 1.0, bank(Y)[:, TT3:2 * TT3], ALU.add, ALU.mult, [s1.res, RY], [u2.res])
                stt(zT[:, oc, :], u2[:, :], 0.5, u1[:, :], ALU.mult, ALU.add, [u1.res, u2.res], [zT.res])

            for oc in range(8):
                X = 2 * (oc % 2)
                Y = X + 1
                RX, RY = Rps[X], Rps[Y]
                for kc in range(8):
                    mm(bank(X)[:, 0:TT3], wgl[:, kc, oc * 128:(oc + 1) * 128], hT3[:, kc, :], kc == 0, kc == 7,
                       [wgl_r[0], hT3.res], [RX])
                for kc in range(8):
                    mm(bank(X)[:, TT3:2 * TT3], wgl[:, kc, D + oc * 128:D + (oc + 1) * 128], hT3[:, kc, :],
                       kc == 0, kc == 7, [wgl_r[1], hT3.res], [RX])
                for kc in range(12):
                    mm(bank(Y)[:, 0:TT3], wba[:, kc, oc * 128:(oc + 1) * 128], v3[:, kc, :], kc == 0, kc == 11,
                       [wba_r[0], v3.res], [RY])
                for kc in range(8):
                    mm(bank(Y)[:, TT3:2 * TT3], wbb[:, kc, oc * 128:(oc + 1) * 128], cB[:, kc, :], kc == 0, kc == 7,
                       [wbb_r[0], cB.res], [RY])
                s0, s1 = t0b[oc % 2], t1b[oc % 2]
                act(s0[:, :], bank(X)[:, 0:TT3], AF.Tanh, [RX, halfv.res], [s0.res],
                    bias=halfv[:, 16 + oc:17 + oc], scale=0.5)
                act(s1[:, :], bank(X)[:, TT3:2 * TT3], AF.Tanh, [RX, halfv.res], [s1.res],
                    bias=halfv[:, 24 + oc:25 + oc], scale=0.5)
                if oc >= 1:
                    merge(oc - 1)
                yield
            merge(7)
            yield

        def stage_C(it):
            zT = zTs[it % 2]
            xn1b = xn1bs[it % 2]
            for s in range(NS):
                tbs = tbss[s]
                col = it * NS + s
                act(xn0[:, :], xC[:, s, :], AF.Identity, [xC.res, rs_all.res, nm_all.res], [xn0.res],
                    bias=nm_all[:, col:col + 1], scale=rs_all[:, col:col + 1])
                tt("pool", tbs[:, :], xn0[:, :], rows["G1"][:, :], ALU.mult, [xn0.res, rows["G1"].res], [tbs.res])
                tt("dve", tbs[:, :], tbs[:, :], rows["B1"][:, :], ALU.add, [tbs.res, rows["B1"].res], [tbs.res])
                yield
            if it + 1 < NT3:
                dma("sp", xC.lane, xC[:, :, :], xv3[it + 1], W=[xC.res])
            for s in range(NS):
                tbs = tbss[s]
                for hh in range(2):
                    b = 4 + 2 * s + hh
                    for kc in range(8):
                        mm(bank(b), zT[:, kc, s * 128:(s + 1) * 128], wou[:, kc, hh * 512:(hh + 1) * 512],
                           kc == 0, kc == 7, [zT.res, wou_r[hh]], [Rps[b]])
                    stt(tbs[:, hh * 512:(hh + 1) * 512], bank(b), 0.5, tbs[:, hh * 512:(hh + 1) * 512],
                        ALU.mult, ALU.add, [tbs.res, Rps[b]], [tbs.res])
                for hh in range(2):
                    S.op("dve", (lambda o=stb4[:, s, hh, :], i=tbs[:, hh * 512:(hh + 1) * 512]:
                                 lambda E: E.bn_stats(out=o, in_=i))(), [tbs.res], [stb4.res])
                S.op("dve", (lambda o=mvb4[:, s, :], i=stb4[:, s, :, :].rearrange("p a b -> p (a b)"):
                             lambda E: E.bn_aggr(out=o, in_=i))(), [stb4.res], [mvb4.res])
                yield
            rstd_pow(rs1[:, 0:NS], mvb4[:, 0:NS, 1], vt4[:, 0:NS], NS, [mvb4.res], vt4.res, rs1.res)
            stt(nm1[:, 0:NS], mvb4[:, 0:NS, 0], -1.0, rs1[:, 0:NS], ALU.mult, ALU.mult,
                [mvb4.res, rs1.res], [nm1.res])
            yield
            for s in range(NS):
                tbs = tbss[s]
                x1 = xn1[s % 2]
                act(xn1b[:, s, :], tbs[:, :], AF.Identity, [tbs.res, rs1.res, nm1.res], [xn1b.res],
                    bias=nm1[:, s:s + 1], scale=rs1[:, s:s + 1])
                act(x1[:, :], tbs[:, :], AF.Identity, [tbs.res, rs1.res, nm1.res], [x1.res],
                    bias=nm1[:, s:s + 1], scale=rs1[:, s:s + 1])
                dma("sp", x1.lane, xn1_d[it * TT3 + s * 128:it * TT3 + (s + 1) * 128, :], x1[:, :],
                    R=[x1.res], WF=[R_xn1])
                yield

        def stage_D(it):
            xn1b = xn1bs[it % 2]
            ho = h1T[it % 2]
            for kc in range(8):
                b = 4 + kc % 4
                pst = bank(b).bitcast(BF16)
                for s in range(NS):
                    trp(pst[:, s * 128:(s + 1) * 128], xn1b[:, s, kc * 128:(kc + 1) * 128], ident_b[:, :],
                        [xn1b.res, ident_b.res], [Rps[b]])
                act(ho[:, kc, :], pst[:, 0:TT3], AF.Identity, [Rps[b], vecs.res], [ho.res],
                    bias=V("ln1_b", kc), scale=V("ln1_g", kc))
                if kc % 2 == 1:
                    yield
            dma("sp", ho.lane, h1Tv[:, :, it * TT3:(it + 1) * TT3], ho[:, :, :], R=[ho.res], WF=[R_h1T])

        def drive(primary, others):
            live = [g for g in others if g is not None]
            for _ in primary:
                for g in list(live):
                    for _k in range(2):
                        try:
                            next(g)
                        except StopIteration:
                            if g in live:
                                live.remove(g)
                            break
            for g in live:
                for _ in g:
                    pass

        load_A(0)
        load_B(0)
        dma("sp", xC.lane, xC[:, :, :], xv3[0], W=[xC.res])
        make_rows("G1", "B1", "emb_g", "emb_b", "b_out")
        for _ in stage_A(0):
            pass
        for it in range(NT3):
            gA = stage_A(it + 1) if it + 1 < NT3 else None
            gC = stage_C(it - 1) if it >= 1 else None
            gD = stage_D(it - 2) if it >= 2 else None
            drive(stage_B(it), [gC, gD, gA])
        p3b_mark = cur[0]
        cur[0] = persist_end
        wup = Buf("wup", [128, 8, DFF], BF16)
        assert cur[0] <= w3_end - 16384
        cur[0] = p3b_mark
        wup_r = []
        dead = [wgl_r[0], wgl_r[1], wbb_r[0], wba_r[0]]
        srcv_up = w_up.rearrange("(kc p) c -> p kc c", p=128)
        for cb in range(4):
            r = Res("wupblk")
            wup_r.append(r)
            dma("pool", S.dma_lane(), wup[:, 0:8, cb * 1024:(cb + 1) * 1024],
                srcv_up[:, :, cb * 1024:(cb + 1) * 1024], W=[r] + dead)
        drive(stage_C(NT3 - 1), [stage_D(NT3 - 2)])
        for _ in stage_D(NT3 - 1):
            pass

        S.barrier()
        cur[0] = persist_end + 8 * DFF * 2
        rows = {}
        for nm in ("G2", "B2", "ln2_g", "ln2_b"):
            rows[nm] = Buf("row_" + nm, [128, D], F32, lane=True)
        rtmp = Buf("rtmp2", [128, D], F32, lane=True)
        wdn = Buf("wdn", [128, 32, D], BF16)
        wdn_r = []
        load_weight(wdn, w_dn, 32, D, 0, 512, wdn_r)
        h4 = [Buf("h4_%d" % i, [128, 8, TT3], BF16, lane=True) for i in range(2)]
        x4 = [Buf("x4_%d" % i, [128, NS, D], F32, lane=True) for i in range(1)]
        mT = Buf("mT", [128, 32, TT3], BF16)
        rl = [Buf("rl%d" % i, [128, TT3], F32) for i in range(2)]
        t4 = [Buf("t4_%d" % i, [128, D], F32) for i in range(1)]
        o4 = [Buf("o4_%d" % i, [128, D], F32, lane=True) for i in range(2)]
        stb5 = Buf("stb5", [128, 4, 2, 6], F32)
        mvb5 = Buf("mvb5", [128, 4, 2], F32)
        rs2 = Buf("rs2", [128, 4], F32)
        nm2 = Buf("nm2", [128, 4], F32)
        xn1v = xn1_d.rearrange("(n s p) f -> n p s f", s=NS, p=128)

        def p4_load_early(it):
            sl = it % 2
            dma("sp", h4[sl].lane, h4[sl][:, :, :], h1Tv[:, :, it * TT3:(it + 1) * TT3], R=[R_h1T], W=[h4[sl].res])

        def p4_load_late(it):
            dma("sp", x4[0].lane, x4[0][:, :, :], xn1v[it], R=[R_xn1], W=[x4[0].res])
        p4_load_early(0)
        p4_load_late(0)
        make_rows("G2", "B2", "ln1_g", "ln1_b", "b_down")
        dma("sp", rows["ln2_g"].lane, rows["ln2_g"][:, :], rows_in[ri["ln2_g"]], W=[rows["ln2_g"].res])
        dma("sp", rows["ln2_b"].lane, rows["ln2_b"][:, :], rows_in[ri["ln2_b"]], W=[rows["ln2_b"].res])
        oi = 0
        for it in range(NT3):
            sl = it % 2
            if it + 1 < NT3:
                p4_load_early(it + 1)
            ht, xt4 = h4[sl], x4[0]
            for fc in range(32):
                b = next_bank(0, 8)
                for kc in range(8):
                    mm(bank(b, TT3), wup[:, kc, fc * 128:(fc + 1) * 128], ht[:, kc, :], kc == 0, kc == 7,
                       [wup_r[(fc * 128) // 1024], ht.res], [Rps[b]])
                r_ = rl[fc % 2]
                ts("dve", r_[:, :], bank(b, TT3), V("b_up", fc), 0.0, ALU.add, ALU.max, [Rps[b], vecs.res], [r_.res])
                act(mT[:, fc, :], r_[:, :], AF.Square, [r_.res], [mT.res])
            for s in range(NS):
                tbb = t4[0]
                tt("pool", tbb[:, :], xt4[:, s, :], rows["G2"][:, :], ALU.mult, [xt4.res, rows["G2"].res], [tbb.res])
                tt("pool", tbb[:, :], tbb[:, :], rows["B2"][:, :], ALU.add, [tbb.res, rows["B2"].res], [tbb.res])
                for hh in range(2):
                    b = next_bank(0, 8)
                    for fc in range(32):
                        mm(bank(b), mT[:, fc, s * 128:(s + 1) * 128], wdn[:, fc, hh * 512:(hh + 1) * 512],
                           fc == 0, fc == 31, [mT.res, wdn_r[hh]], [Rps[b]])
                    tt("dve", tbb[:, hh * 512:(hh + 1) * 512], tbb[:, hh * 512:(hh + 1) * 512], bank(b), ALU.add,
                       [tbb.res, Rps[b]], [tbb.res])
                for hh in range(2):
                    S.op("dve", (lambda s=s, hh=hh, tbb=tbb: lambda E: E.bn_stats(
                        out=stb5[:, s, hh, :], in_=tbb[:, hh * 512:(hh + 1) * 512]))(), [tbb.res], [stb5.res])
                S.op("dve", (lambda s=s: lambda E: E.bn_aggr(
                    out=mvb5[:, s, :], in_=stb5[:, s, :, :].rearrange("p a b -> p (a b)")))(), [stb5.res], [mvb5.res])
                act(rs2[:, s:s + 1], mvb5[:, s, 1:2], AF.Sqrt, [mvb5.res], [rs2.res], bias=EPS)
                S.op("dve", (lambda s=s: lambda E: E.reciprocal(out=rs2[:, s:s + 1], in_=rs2[:, s:s + 1]))(),
                     [rs2.res], [rs2.res])
                stt(nm2[:, s:s + 1], mvb5[:, s, 0:1], -1.0, rs2[:, s:s + 1], ALU.mult, ALU.mult,
                    [mvb5.res, rs2.res], [nm2.res])
                ob = o4[oi % 2]
                oi += 1
                act(ob[:, :], tbb[:, :], AF.Identity, [tbb.res, rs2.res, nm2.res], [ob.res],
                    bias=nm2[:, s:s + 1], scale=rs2[:, s:s + 1])
                tt("pool", ob[:, :], ob[:, :], rows["ln2_g"][:, :], ALU.mult, [ob.res, rows["ln2_g"].res], [ob.res])
                tt("dve", ob[:, :], ob[:, :], rows["ln2_b"][:, :], ALU.add, [ob.res, rows["ln2_b"].res], [ob.res])
                dma("sp", ob.lane, out[it * TT3 + s * 128:it * TT3 + (s + 1) * 128, :], ob[:, :],
                    R=[ob.res], WF=[R_out])
            if it + 1 < NT3:
                p4_load_late(it + 1)
        S.barrier()
        S.run()
    return nc


_NC_CACHE = {}


def _cols(v, p):
    v = np.asarray(v, np.float32).reshape(-1, p).T
    o = np.zeros((128, v.shape[1]), np.float32)
    o[:p] = v
    return o


def kernel(x, emb_ln_g, emb_ln_b, w_in, b_in, rnn_conv_w, rnn_conv_b, rg_gate_w, rg_gate_b,
           rg_a_param, w_branch_a, conv_w, conv_b, conv_ln_g, conv_ln_b, w_branch_b,
           w_out, b_out, ln1_g, ln1_b, w_up, b_up, w_down, b_down, ln2_g, ln2_b):
    f = lambda a: np.ascontiguousarray(np.asarray(a, np.float32))
    x = f(x)
    b_in0 = f(b_in)[0]
    parts = {
        "b_xr": _cols(b_in0[0:DR], 128), "b_yr": _cols(b_in0[DR:2 * DR], 128),
        "b_xca": _cols(b_in0[2 * DR:2 * DR + D], 128), "b_xcb": _cols(b_in0[2 * DR + D:2 * DR + 2 * D], 128),
        "b_g0": _cols(b_in0[5120:5120 + D], 128), "b_g1": _cols(b_in0[5120 + D:], 128),
        "emb_g": _cols(emb_ln_g, 128), "emb_b": _cols(emb_ln_b, 128),
        "conv_b": _cols(f(conv_b)[0], 128), "cln_g": _cols(f(conv_ln_g)[0], 128), "cln_b": _cols(f(conv_ln_b)[0], 128),
        "ln1_g": _cols(f(ln1_g)[0], 128), "ln1_b": _cols(f(ln1_b)[0], 128), "b_up": _cols(f(b_up)[0], 128),
        "convw": np.ascontiguousarray(f(conv_w)[0].reshape(31, 8, 128).transpose(2, 1, 0)).reshape(128, 8 * 31),
    }
    w4 = np.zeros((128, 64), np.float32)
    w4[:BW] = f(rnn_conv_w)[0].reshape(4, NB, BW).transpose(2, 1, 0).reshape(BW, 64)
    parts["w4"] = w4
    parts["b4"] = _cols(f(rnn_conv_b)[0], BW)
    gb = np.zeros((128, 64), np.float32)
    gb[:BW] = f(rg_gate_b)[0].reshape(64, BW).T
    parts["gb"] = gb
    ap_ = np.zeros((128, 32), np.float32)
    ap_[:BW] = f(rg_a_param)[0].reshape(2 * NB, BW).T
    parts["ap"] = ap_
    vecs = np.zeros((128, NV), np.float32)
    for nm, off in _V.items():
        a = parts[nm]
        vecs[:, off:off + a.shape[1]] = a
    rowsrc = {"emb_g": emb_ln_g, "emb_b": emb_ln_b, "b_out": f(b_out)[0], "ln1_g": f(ln1_g)[0],
              "ln1_b": f(ln1_b)[0], "b_down": f(b_down)[0], "ln2_g": f(ln2_g)[0], "ln2_b": f(ln2_b)[0]}
    rows = np.ascontiguousarray(np.stack(
        [np.broadcast_to(f(rowsrc[n]).reshape(1, D), (128, D)) for n in ROWS]))
    wg = np.ascontiguousarray(f(rg_gate_w)[0].reshape(64, BW, BW).transpose(1, 0, 2)).reshape(BW, 64 * BW)
    shared = {
        "w_in": f(w_in)[0], "w_ba": f(w_branch_a)[0], "w_bb": f(w_branch_b)[0], "w_out": f(w_out)[0],
        "w_up": f(w_up)[0], "w_dn": f(w_down)[0], "wg": wg, "vecs": vecs, "rows": rows,
        "ident": np.eye(128, dtype=np.float32),
    }
    if "nc" not in _NC_CACHE:
        _NC_CACHE["nc"] = build_nc()
    nc = _NC_CACHE["nc"]
    in_maps = [dict(shared, x=x[b]) for b in range(8)]
    res = run_bass_kernel_spmd(nc, in_maps, core_ids=list(range(8)))
    return np.stack([np.asarray(r["out"], np.float32) for r in res.results], axis=0)
```

```python
import contextlib
import numpy as np
import concourse.bass as bass
import concourse.mybir as mybir
from concourse.ap import AP
from concourse.bass_utils import run_bass_kernel_spmd

F32 = mybir.dt.float32
BF16 = mybir.dt.bfloat16
AF = mybir.ActivationFunctionType
ALU = mybir.AluOpType

T = 4096
D = 1024
DR = 1536
DIN = 7168
DFF = 4096
NB = 16
BW = 96
ALPHA = 2.0 ** 0.25
EPS = 1e-5
SAME_ENGINE_SYNC = ("pool", "dve", "act")


class Res:
    __slots__ = ("name", "w", "r")

    def __init__(self, name):
        self.name = name
        self.w = {}
        self.r = {}


class Sched:
    ENGS = ("pe", "act", "dve", "pool", "sp")

    def __init__(self, nc, stack):
        self.nc = nc
        self.stack = stack
        self.q = {e: [] for e in self.ENGS}
        self.lane_sem = {}
        self.lane_cnt = {}
        self.seen = {e: {} for e in self.ENGS}
        for e in self.ENGS:
            self.new_lane(e)
        self.n_dma_lanes = 0

    def new_lane(self, name):
        sem = self.stack.enter_context(self.nc.semaphore("s_" + name))
        self.lane_sem[name] = sem
        self.lane_cnt[name] = 0
        return name

    def dma_lane(self):
        self.n_dma_lanes += 1
        return self.new_lane("d%d" % self.n_dma_lanes)

    def _deps(self, eng, reads, writes, force=False):
        deps = {}

        def need(lane, v):
            if deps.get(lane, 0) < v:
                deps[lane] = v
        for b in reads:
            for lane, v in b.w.items():
                need(lane, v)
        for b in writes:
            for lane, v in b.w.items():
                need(lane, v)
            for lane, v in b.r.items():
                need(lane, v)
        waits = []
        for lane, v in deps.items():
            if lane == eng and eng not in SAME_ENGINE_SYNC and not force:
                continue
            if self.seen[eng].get(lane, 0) >= v:
                continue
            self.seen[eng][lane] = v
            waits.append((self.lane_sem[lane], v))
        return waits

    @staticmethod
    def _commit(ticket, reads, writes):
        lane, v = ticket
        for b in reads:
            if b.r.get(lane, 0) < v:
                b.r[lane] = v
        for b in writes:
            if b.w.get(lane, 0) < v:
                b.w[lane] = v

    def op(self, eng, fn, reads=(), writes=(), self_sync=False):
        waits = self._deps(eng, reads, writes, self_sync)
        self.lane_cnt[eng] += 1
        ticket = (eng, self.lane_cnt[eng])
        sem = self.lane_sem[eng]

        def emit(E, fn=fn, waits=waits, sem=sem):
            for s, v in waits:
                E.wait_ge(s, v)
            fn(E).then_inc(sem, 1)
        self.q[eng].append(emit)
        self._commit(ticket, reads, writes)

    def dma(self, eng, lane, fn, reads=(), writes=(), writes_free=()):
        waits = self._deps(eng, reads, writes)
        self.lane_cnt[lane] += 16
        ticket = (lane, self.lane_cnt[lane])
        sem = self.lane_sem[lane]

        def emit(E, fn=fn, waits=waits, sem=sem):
            for s, v in waits:
                E.wait_ge(s, v)
            fn(E).then_inc(sem, 16)
        self.q[eng].append(emit)
        self._commit(ticket, reads, tuple(writes) + tuple(writes_free))

    def barrier(self, engs=None):
        for eng in (engs or self.ENGS):
            waits = []
            for lane, v in self.lane_cnt.items():
                if v == 0 or lane == eng:
                    continue
                if self.seen[eng].get(lane, 0) >= v:
                    continue
                self.seen[eng][lane] = v
                waits.append((self.lane_sem[lane], v))

            def emit(E, waits=waits):
                for s, v in waits:
                    E.wait_ge(s, v)
            self.q[eng].append(emit)

    def run(self):
        with self.nc.Block() as block:
            @block.tensor
            def _(E):
                for f in self.q["pe"]:
                    f(E)

            @block.scalar
            def _(E):
                for f in self.q["act"]:
                    f(E)

            @block.vector
            def _(E):
                for f in self.q["dve"]:
                    f(E)

            @block.gpsimd
            def _(E):
                for f in self.q["pool"]:
                    f(E)

            @block.sync
            def _(E):
                for f in self.q["sp"]:
                    f(E)


def rev(ap):
    steps = [list(x) for x in ap.ap]
    st, cnt = steps[-1]
    off = ap.offset + st * (cnt - 1)
    steps[-1] = [-st, cnt]
    return AP(ap.tensor, off, steps)


_V = {}
_off = 0
for _nm, _n in [("b_xr", 12), ("b_yr", 12), ("b_xca", 8), ("b_xcb", 8), ("b_g0", 8), ("b_g1", 8),
                ("emb_g", 8), ("emb_b", 8), ("conv_b", 8), ("cln_g", 8), ("cln_b", 8),
                ("ln1_g", 8), ("ln1_b", 8), ("b_up", 32), ("convw", 8 * 31),
                ("w4", 64), ("b4", 16), ("gb", 64), ("ap", 32)]:
    _V[_nm] = _off
    _off += _n
NV = _off
ROWS = ["emb_g", "emb_b", "b_out", "ln1_g", "ln1_b", "b_down", "ln2_g", "ln2_b"]


def build_nc(debug=False):
    nc = bass.Bass("TRN2", target_bir_lowering=False)
    dt_in = lambda nm, shp: nc.dram_tensor(nm, shp, F32, kind="ExternalInput").ap()
    x = dt_in("x", [T, D])
    w_in = dt_in("w_in", [D, DIN])
    w_ba = dt_in("w_ba", [DR, D])
    w_bb = dt_in("w_bb", [D, D])
    w_out = dt_in("w_out", [D, D])
    w_up = dt_in("w_up", [D, DFF])
    w_dn = dt_in("w_dn", [DFF, D])
    wg_in = dt_in("wg", [BW, 64 * BW])
    vecs_in = dt_in("vecs", [128, NV])
    rows_in = dt_in("rows", [8, 128, D])
    ident_in = dt_in("ident", [128, 128])
    out = nc.dram_tensor("out", [T, D], F32, kind="ExternalOutput").ap()
    kw = {"kind": "ExternalOutput"} if debug else {}
    hT_d = nc.dram_tensor("hT_d", [D, T], BF16, **kw).ap()
    xr_d = nc.dram_tensor("xr_d", [DR, T], F32, **kw).ap()
    gy_d = nc.dram_tensor("gy_d", [DR, T], BF16, **kw).ap()
    c0_d = nc.dram_tensor("c0_d", [D, T], BF16, **kw).ap()
    c1_d = nc.dram_tensor("c1_d", [D, T], F32, **kw).ap()
    v_d = nc.dram_tensor("v_d", [DR, T], BF16, **kw).ap()
    xn1_d = nc.dram_tensor("xn1_d", [T, D], F32, **kw).ap()
    h1T_d = nc.dram_tensor("h1T_d", [D, T], BF16, **kw).ap()
    cst_d = nc.dram_tensor("cst_d", [2, 128, T], F32, **kw).ap()

    with contextlib.ExitStack() as st:
        S = Sched(nc, st)
        ARENA_F32 = 53000
        arena = st.enter_context(nc.sbuf_tensor("arena", [128, ARENA_F32], F32))
        base = nc.lookup_mloc(arena).addr
        ps_all = st.enter_context(nc.psum_tensor("ps_all", [128, 4096], F32))
        Rps = [Res("ps%d" % i) for i in range(8)]

        def bank(i, n=512, p=128):
            return ps_all[0:p, i * 512:i * 512 + n]

        cur = [0]
        cnt = [0]

        class Buf:
            def __init__(self, name, shape, dt, lane=False):
                nbytes = int(np.prod(shape[1:])) * (4 if dt == F32 else 2)
                nbytes = (nbytes + 63) // 64 * 64
                assert cur[0] + nbytes <= ARENA_F32 * 4, (name, cur[0], nbytes)
                cnt[0] += 1
                self.t = nc.alloc_sbuf_tensor_at("%s_%d" % (name, cnt[0]), list(shape), dt,
                                                 offset=base + cur[0])
                cur[0] += nbytes
                self.res = Res(name)
                self.lane = S.dma_lane() if lane else None

            def __getitem__(self, k):
                return self.t[k]

        def mm(o, l, r, start, stop, R, W):
            S.op("pe", lambda E: E.matmul(o, l, r, start=start, stop=stop), R, W)

        def trp(o, i, idn, R, W):
            S.op("pe", lambda E: E.transpose(o, i, idn), R, W)

        def act(o, i, func, R, W, bias=0.0, scale=1.0, self_sync=False):
            S.op("act", lambda E: E.activation(out=o, in_=i, func=func, bias=bias, scale=scale), R, W, self_sync)

        def ts(eng, o, i, s1, s2, op0, op1, R, W):
            if s2 is None:
                S.op(eng, lambda E: E.tensor_scalar(out=o, in0=i, scalar1=s1, scalar2=None, op0=op0), R, W)
            else:
                S.op(eng, lambda E: E.tensor_scalar(out=o, in0=i, scalar1=s1, scalar2=s2, op0=op0, op1=op1), R, W)

        def stt(o, i0, sc, i1, op0, op1, R, W):
            S.op("dve", lambda E: E.scalar_tensor_tensor(out=o, in0=i0, scalar=sc, in1=i1, op0=op0, op1=op1), R, W)

        def tt(eng, o, i0, i1, op, R, W):
            S.op(eng, lambda E: E.tensor_tensor(out=o, in0=i0, in1=i1, op=op), R, W)

        def cp(eng, o, i, R, W):
            S.op(eng, lambda E: E.tensor_copy(out=o, in_=i), R, W)

        def dma(eng, lane, o, i, R=(), W=(), WF=()):
            S.dma(eng, lane, lambda E: E.dma_start(out=o, in_=i), R, W, WF)

        R_hT, R_xr, R_gy, R_c0, R_c1, R_v, R_xn1, R_h1T, R_out = [Res(n) for n in
            ("hT_d", "xr_d", "gy_d", "c0_d", "c1_d", "v_d", "xn1_d", "h1T_d", "out")]
        R_cst = Res("cst_d")

        vecs = Buf("vecs", [128, NV], F32, lane=True)
        ident_f = Buf("ident_f", [128, 128], F32, lane=True)
        ident_b = Buf("ident_b", [128, 128], BF16)
        ones_f = Buf("ones_f", [128, 128], F32)
        hb = Buf("hb", [128, 64], F32)
        hc = Buf("hc", [128, 32], F32)
        tmpc = Buf("tmpc", [128, 32], F32)
        rs_all = Buf("rs_all", [128, 32], F32)
        nm_all = Buf("nm_all", [128, 32], F32)
        persist_end = cur[0]

        dma("sp", vecs.lane, vecs[:, :], vecs_in, W=[vecs.res])
        dma("sp", ident_f.lane, ident_f[:, :], ident_in, W=[ident_f.res])
        cp("dve", ident_b[:, :], ident_f[:, :], [ident_f.res], [ident_b.res])
        S.op("pool", lambda E: E.memset(ones_f[:, :], 1.0), [], [ones_f.res])
        V = lambda nm, j=0, n=1, p=128: vecs[0:p, _V[nm] + j:_V[nm] + j + n]
        ts("dve", hb[0:96, :], V("gb", 0, 64, 96), 0.5, None, ALU.mult, None, [vecs.res], [hb.res])
        act(tmpc[0:96, :], V("ap", 0, 32, 96), AF.Exp, [vecs.res], [tmpc.res], scale=-1.0)
        act(tmpc[0:96, :], tmpc[0:96, :], AF.Ln, [tmpc.res], [tmpc.res], bias=1.0, self_sync=True)
        ts("dve", hc[0:96, :], tmpc[0:96, :], -4.0, None, ALU.mult, None, [tmpc.res], [hc.res])

        def layer_norm_stats(xsrc, nsub, Rsrc, stb, mvb, rstd, nmr):
            for s in range(nsub):
                for hh in range(2):
                    S.op("dve", (lambda o=stb[:, s, hh, :], i=xsrc(s)[:, hh * 512:(hh + 1) * 512]:
                                 lambda E: E.bn_stats(out=o, in_=i))(), Rsrc, [stb.res])
                S.op("dve", (lambda s=s: lambda E: E.bn_aggr(
                    out=mvb[:, s, :], in_=stb[:, s, :, :].rearrange("p a b -> p (a b)")))(), [stb.res], [mvb.res])
            act(rstd[:, 0:nsub], mvb[:, 0:nsub, 1], AF.Sqrt, [mvb.res], [rstd.res], bias=EPS)
            S.op("dve", lambda E: E.reciprocal(out=rstd[:, 0:nsub], in_=rstd[:, 0:nsub]), [rstd.res], [rstd.res])
            stt(nmr[:, 0:nsub], mvb[:, 0:nsub, 0], -1.0, rstd[:, 0:nsub], ALU.mult, ALU.mult,
                [mvb.res, rstd.res], [nmr.res])

        def load_weight(buf, src, nk, ncols, c0, blk, reslist):
            srcv = src.rearrange("(kc p) c -> p kc c", p=128)
            for cb in range(ncols // blk):
                r = Res("wblk")
                reslist.append(r)
                dma("pool", S.dma_lane(), buf[:, 0:nk, cb * blk:(cb + 1) * blk],
                    srcv[:, :, c0 + cb * blk:c0 + (cb + 1) * blk], W=[r])

        psrot = [0]

        def next_bank(lo, hi):
            b = lo + psrot[0] % (hi - lo)
            psrot[0] += 1
            return b

        TT = 512
        w1 = Buf("w1", [128, 8, 5120], BF16)
        w1res = []
        load_weight(w1, w_in, 8, 5120, 0, 1024, w1res)
        xt = [Buf("xt%d" % i, [128, 4, D], F32, lane=True) for i in range(2)]
        xnb = Buf("xnb", [128, 4, D], BF16)
        hTt = [Buf("hTt%d" % i, [128, 8, TT], BF16, lane=True) for i in range(2)]
        xr_st = [Buf("xr_st%d" % i, [128, 6, TT], F32, lane=True) for i in range(2)]
        gy_st = [Buf("gy_st%d" % i, [128, 6, TT], BF16, lane=True) for i in range(2)]
        c0_st = [Buf("c0_st%d" % i, [128, 4, TT], BF16, lane=True) for i in range(2)]
        sg = [Buf("sg%d" % i, [128, TT], F32) for i in range(2)]
        stb = Buf("stb", [128, 4, 2, 6], F32)
        mvb = Buf("mvb", [128, 4, 2], F32)
        rstd = Buf("rstd", [128, 4], F32)
        nmr = Buf("nmr", [128, 4], F32)

        xv = x.rearrange("(n s p) f -> n p s f", s=4, p=128)
        NT1 = T // TT
        def p1_prepA(it):
            xb = xt[it % 2]
            layer_norm_stats(lambda s: xb[:, s, :], 4, [xb.res], stb, mvb, rstd, nmr)
            cp("dve", rs_all[:, it * 4:it * 4 + 4], rstd[:, 0:4], [rstd.res], [rs_all.res])
            cp("dve", nm_all[:, it * 4:it * 4 + 4], nmr[:, 0:4], [nmr.res], [nm_all.res])
            for s in range(4):
                act(xnb[:, s, :], xb[:, s, :], AF.Identity, [xb.res, rstd.res, nmr.res], [xnb.res],
                    bias=nmr[:, s:s + 1], scale=rstd[:, s:s + 1])

        def p1_prepB(it):
            hp = hTt[it % 2]
            for kc in range(8):
                b = next_bank(0, 2)
                pst = bank(b).bitcast(BF16)
                for s in range(4):
                    trp(pst[:, s * 128:(s + 1) * 128], xnb[:, s, kc * 128:(kc + 1) * 128], ident_b[:, :],
                        [xnb.res, ident_b.res], [Rps[b]])
                act(hp[:, kc, :], pst[:, 0:TT], AF.Identity, [Rps[b], vecs.res], [hp.res],
                    bias=V("emb_b", kc), scale=V("emb_g", kc))
            dma("sp", hp.lane, hT_d.rearrange("(kc p) t -> p kc t", p=128)[:, :, it * TT:(it + 1) * TT],
                hp[:, :, :], R=[hp.res], WF=[R_hT])

        dma("sp", xt[0].lane, xt[0][:, :, :], xv[0], W=[xt[0].res])
        dma("sp", xt[1].lane, xt[1][:, :, :], xv[1], W=[xt[1].res])
        p1_prepA(0)
        p1_prepB(0)
        for it in range(NT1):
            hb_ = hTt[it % 2]
            if it + 1 < NT1:
                p1_prepA(it + 1)

            def proj(col0, b):
                for kc in range(8):
                    mm(bank(b), w1[:, kc, col0:col0 + 128], hb_[:, kc, :], kc == 0, kc == 7,
                       [w1res[col0 // 1024], hb_.res], [Rps[b]])
            for c in range(12):
                b = next_bank(2, 8)
                proj(c * 128, b)
                stg = xr_st[c // 6]
                ts("dve", stg[:, c % 6, :], bank(b), V("b_xr", c), None, ALU.add, None,
                   [Rps[b], vecs.res], [stg.res])
                if c % 6 == 5:
                    dma("sp", stg.lane,
                        xr_d.rearrange("(kc p) t -> p kc t", p=128)[:, c - 5:c + 1, it * TT:(it + 1) * TT],
                        stg[:, :, :], R=[stg.res], WF=[R_xr])
            if it + 1 < NT1:
                p1_prepB(it + 1)
            if it + 2 < NT1:
                nb_ = xt[it % 2]
                dma("sp", nb_.lane, nb_[:, :, :], xv[it + 2], W=[nb_.res])
            for c in range(12):
                b = next_bank(2, 8)
                proj(DR + c * 128, b)
                stg = gy_st[c // 6]
                act(stg[:, c % 6, :], bank(b), AF.Gelu_apprx_tanh, [Rps[b], vecs.res], [stg.res],
                    bias=V("b_yr", c))
                if c % 6 == 5:
                    dma("sp", stg.lane,
                        gy_d.rearrange("(kc p) t -> p kc t", p=128)[:, c - 5:c + 1, it * TT:(it + 1) * TT],
                        stg[:, :, :], R=[stg.res], WF=[R_gy])
            for j in range(8):
                ba = next_bank(2, 8)
                proj(2 * DR + j * 128, ba)
                bb = next_bank(2, 8)
                proj(2 * DR + D + j * 128, bb)
                sgb = sg[j % 2]
                act(sgb[:, :], bank(bb), AF.Sigmoid, [Rps[bb], vecs.res], [sgb.res], bias=V("b_xcb", j))
                stg = c0_st[j // 4]
                stt(stg[:, j % 4, :], bank(ba), V("b_xca", j), sgb[:, :], ALU.add, ALU.mult,
                    [Rps[ba], sgb.res, vecs.res], [stg.res])
                if j % 4 == 3:
                    dma("sp", stg.lane,
                        c0_d.rearrange("(kc p) t -> p kc t", p=128)[:, j - 3:j + 1, it * TT:(it + 1) * TT],
                        stg[:, :, :], R=[stg.res], WF=[R_c0])

        S.barrier()
        cur[0] = persist_end
        wgb = Buf("wgb", [128, 64 * BW], BF16, lane=True)
        dma("pool", wgb.lane, wgb[0:BW, :], wg_in, W=[wgb.res])
        xrs = [Buf("xrs0", [128, T + 4], F32, lane=True)]
        gyb = Buf("gyb", [128, T], BF16, lane=True)
        ubs = [Buf("ub%d" % i, [128, T], F32) for i in range(2)]
        ubfs = [Buf("ubf%d" % i, [128, T], BF16) for i in range(2)]
        TRb = [Buf("TR%d" % d, [128, T], F32) for d in range(2)]
        TIb = [Buf("TI%d" % d, [128, T], F32) for d in range(2)]
        B0s = [Buf("B0_%d" % i, [128, T], F32) for i in range(2)]
        B1 = Buf("B1", [128, T], F32)
        D4 = [Buf("D4_%d" % i, [128, 4, BW], F32) for i in range(2)]
        P = BW
        xs = xrs[0]
        S.op("pool", lambda E: E.memset(xs[:, 0:2], 0.0), [], [xs.res])
        S.op("pool", lambda E: E.memset(xs[:, T + 2:T + 4], 0.0), [], [xs.res])

        def p2_loadx(n):
            dma("sp", xs.lane, xs[0:P, 2:T + 2], xr_d[n * BW:(n + 1) * BW, :], R=[R_xr], W=[xs.res])

        def p2_conv(n, half):
            ub, ubf, d4 = ubs[n % 2], ubfs[n % 2], D4[n % 2]
            if half == 0:
                for k in range(4):
                    act(d4[0:P, k, :], ident_f[0:P, 0:P], AF.Copy, [ident_f.res, vecs.res], [d4.res],
                        scale=V("w4", n * 4 + k, 1, P))
            for q in range(half, half + 1):
                b2 = 2 * (next_bank(0, 4))
                for hh in range(2):
                    t0_ = q * 1024 + hh * 512
                    for k in range(4):
                        mm(bank(b2 + hh, 512, P), d4[0:P, k, :], xs[0:P, t0_ + k:t0_ + k + 512], k == 0, k == 3,
                           [d4.res, xs.res], [Rps[b2 + hh]])
                src = ps_all[0:P, b2 * 512:b2 * 512 + 1024]
                ts("dve", ub[0:P, q * 1024:(q + 1) * 1024], src, V("b4", n, 1, P), None, ALU.add, None,
                   [Rps[b2], Rps[b2 + 1], vecs.res], [ub.res])
                ts("dve", ubf[0:P, q * 1024:(q + 1) * 1024], src, V("b4", n, 1, P), None, ALU.add, None,
                   [Rps[b2], Rps[b2 + 1], vecs.res], [ubf.res])
            if half == 3 and n + 1 < NB:
                p2_loadx(n + 1)

        def p2_gates(n, d):
            ubf = ubfs[n % 2]
            for q in range(4):
                if q == 2 and n + 1 < NB:
                    p2_conv(n + 1, 2 * d)
                for g in range(2):
                    b2 = 2 * (next_bank(0, 4))
                    idx = (d * 2 + g) * NB + n
                    for hh in range(2):
                        mm(bank(b2 + hh, 512, P), wgb[0:P, idx * BW:(idx + 1) * BW],
                           ubf[0:P, q * 1024 + hh * 512:q * 1024 + (hh + 1) * 512], True, True,
                           [wgb.res, ubf.res], [Rps[b2 + hh]])
                    dst = (TRb if g == 0 else TIb)[d]
                    act(dst[0:P, q * 1024:(q + 1) * 1024], ps_all[0:P, b2 * 512:b2 * 512 + 1024], AF.Tanh,
                        [Rps[b2], Rps[b2 + 1], hb.res], [dst.res], bias=hb[0:P, idx:idx + 1], scale=0.5)

        def p2_dir(n, d):
            Bd = B0s[n % 2] if d == 0 else B1
            ub = ubs[n % 2]
            hcv = hc[0:P, d * NB + n:d * NB + n + 1]
            act(TRb[d][0:P, :], TRb[d][0:P, :], AF.Exp, [TRb[d].res, hc.res], [TRb[d].res], bias=hcv, scale=hcv)
            act(Bd[0:P, :], TRb[d][0:P, :], AF.Square, [TRb[d].res], [Bd.res])
            act(Bd[0:P, :], Bd[0:P, :], AF.Sqrt, [Bd.res], [Bd.res], bias=0.25, scale=-0.25)
            stt(TIb[d][0:P, :], TIb[d][0:P, :], 1.0, ub[0:P, :], ALU.add, ALU.mult,
                [TIb[d].res, ub.res], [TIb[d].res])
            tt("dve", TIb[d][0:P, :], TIb[d][0:P, :], Bd[0:P, :], ALU.mult,
               [TIb[d].res, Bd.res], [TIb[d].res])
            f = (lambda a: a) if d == 0 else rev
            S.op("dve", (lambda o=f(Bd[0:P, 0:T]), a_=f(TRb[d][0:P, 0:T]), b_=f(TIb[d][0:P, 0:T]):
                         lambda E: E.tensor_tensor_scan(out=o, data0=a_, data1=b_, initial=0.0,
                                                        op0=ALU.mult, op1=ALU.add))(),
                 [TRb[d].res, TIb[d].res, Bd.res], [Bd.res])

        p2_loadx(0)
        for q_ in range(4):
            p2_conv(0, q_)
        for n in range(NB):
            dma("sp", gyb.lane, gyb[0:P, :], gy_d[n * BW:(n + 1) * BW, :], R=[R_gy], W=[gyb.res])
            p2_gates(n, 0)
            if n + 1 < NB:
                p2_conv(n + 1, 1)
            p2_dir(n, 0)
            p2_gates(n, 1)
            if n + 1 < NB:
                p2_conv(n + 1, 3)
            p2_dir(n, 1)
            tt("pool", TIb[1][0:P, :], B0s[n % 2][0:P, :], B1[0:P, :], ALU.add,
               [B0s[n % 2].res, B1.res], [TIb[1].res])
            tt("pool", gyb[0:P, :], TIb[1][0:P, :], gyb[0:P, :], ALU.mult, [TIb[1].res, gyb.res], [gyb.res])
            dma("sp", gyb.lane, v_d[n * BW:(n + 1) * BW, :], gyb[0:P, :], R=[gyb.res], WF=[R_v])

        S.barrier()
        cur[0] = persist_end
        wgl = Buf("wgl", [128, 8, 2048], BF16)
        wbb = Buf("wbb", [128, 8, D], BF16)
        wba = Buf("wba", [128, 12, D], BF16)
        wou = Buf("wou", [128, 8, D], BF16)
        wgl_r, wbb_r, wba_r, wou_r = [], [], [], []
        load_weight(wgl, w_in, 8, 2048, 5120, 1024, wgl_r)
        load_weight(wbb, w_bb, 8, D, 0, 1024, wbb_r)
        load_weight(wba, w_ba, 12, D, 0, 1024, wba_r)
        load_weight(wou, w_out, 8, D, 0, 512, wou_r)
        w3_end = cur[0]
        cxb = [Buf("cx%d" % i, [128, T + 32], BF16, lane=True) for i in range(2)]
        Dg = [Buf("Dg%d" % i, [128, 31, 128], BF16) for i in range(2)]
        c1s = [Buf("c1s%d" % i, [128, T], F32, lane=True) for i in range(2)]
        acc1 = Buf("acc1", [128, T], F32, lane=True)
        acc2 = Buf("acc2", [128, T], F32, lane=True)
        sqc = Buf("sqc", [128, T], F32)
        for i in range(2):
            S.op("pool", (lambda i=i: lambda E: E.memset(cxb[i][:, 0:15], 0.0))(), [], [cxb[i].res])
            S.op("pool", (lambda i=i: lambda E: E.memset(cxb[i][:, T + 15:T + 32], 0.0))(), [], [cxb[i].res])

        NPE = 28

        def build_diag(j):
            dg_ = Dg[j % 2]
            ia = ident_f[:, :]
            wa = V("convw", j * 31, NPE)
            pstep_i = ia.ap[0][0]
            pstep_w = wa.ap[0][0]
            in0 = AP(ia.tensor, ia.offset, [[pstep_i, 128], [0, NPE], [1, 128]])
            in1 = AP(wa.tensor, wa.offset, [[pstep_w, 128], [1, NPE], [0, 128]])
            tt("dve", dg_[:, 0:NPE, :], in0, in1, ALU.mult, [ident_f.res, vecs.res], [dg_.res])

        def p2c_load(j):
            cb_ = cxb[j % 2]
            dma("sp", cb_.lane, cb_[:, 15:T + 15], c0_d[j * 128:(j + 1) * 128, :], R=[R_c0], W=[cb_.res])
        p2c_load(0)
        build_diag(0)
        for j in range(8):
            cb_ = cxb[j % 2]
            dg = Dg[j % 2]
            co = c1s[j % 2]
            if j + 1 < 8:
                p2c_load(j + 1)
                build_diag(j + 1)
            ts("dve", co[:, :], cb_[:, NPE:NPE + T], V("convw", j * 31 + NPE), None, ALU.mult, None,
               [cb_.res, vecs.res], [co.res])
            for k in range(NPE + 1, 31):
                stt(co[:, :], cb_[:, k:k + T], V("convw", j * 31 + k), co[:, :], ALU.mult, ALU.add,
                    [cb_.res, co.res, vecs.res], [co.res])
            for tt_ in range(8):
                b = next_bank(0, 8)
                for k in range(NPE):
                    mm(bank(b), dg[:, k, :], cb_[:, tt_ * 512 + k:tt_ * 512 + k + 512], k == 0, k == NPE - 1,
                       [dg.res, cb_.res], [Rps[b]])
                tsl = slice(tt_ * 512, (tt_ + 1) * 512)
                stt(co[:, tsl], bank(b), V("conv_b", j), co[:, tsl], ALU.add, ALU.add,
                    [Rps[b], co.res, vecs.res], [co.res])
            dma("sp", co.lane, c1_d[j * 128:(j + 1) * 128, :], co[:, :], R=[co.res], WF=[R_c1])
            if j == 0:
                cp("pool", acc1[:, :], co[:, :], [co.res], [acc1.res])
                act(acc2[:, :], co[:, :], AF.Square, [co.res], [acc2.res])
            else:
                tt("pool", acc1[:, :], acc1[:, :], co[:, :], ALU.add, [acc1.res, co.res], [acc1.res])
                act(sqc[:, :], co[:, :], AF.Square, [co.res], [sqc.res])
                tt("pool", acc2[:, :], acc2[:, :], sqc[:, :], ALU.add, [acc2.res, sqc.res], [acc2.res])
        for tt_ in range(8):
            tsl = slice(tt_ * 512, (tt_ + 1) * 512)
            b1_ = next_bank(0, 8)
            b2_ = next_bank(0, 8)
            mm(bank(b1_), ones_f[:, :], acc1[:, tsl], True, True, [ones_f.res, acc1.res], [Rps[b1_]])
            mm(bank(b2_), ones_f[:, :], acc2[:, tsl], True, True, [ones_f.res, acc2.res], [Rps[b2_]])
            ts("dve", sqc[:, tsl], bank(b1_), 1.0 / D, None, ALU.mult, None, [Rps[b1_], acc1.res], [sqc.res])
            tt("dve", c1s[1][:, tsl], sqc[:, tsl], sqc[:, tsl], ALU.mult, [sqc.res], [c1s[1].res])
            stt(c1s[0][:, tsl], bank(b2_), 1.0 / D, c1s[1][:, tsl], ALU.mult, ALU.subtract,
                [Rps[b2_], c1s[1].res, acc2.res], [c1s[0].res])
        act(c1s[0][:, :], c1s[0][:, :], AF.Sqrt, [c1s[0].res], [c1s[0].res], bias=EPS)
        S.op("dve", lambda E: E.reciprocal(out=c1s[0][:, :], in_=c1s[0][:, :]), [c1s[0].res], [c1s[0].res])
        dma("sp", acc1.lane, cst_d[0], sqc[:, :], R=[sqc.res], WF=[R_cst])
        dma("sp", acc2.lane, cst_d[1], c1s[0][:, :], R=[c1s[0].res], WF=[R_cst])

        S.barrier()
        cur[0] = w3_end
        TT3 = 256
        NS = TT3 // 128
        rows = {}
        for nm in ("G1", "B1"):
            rows[nm] = Buf("row_" + nm, [128, D], F32, lane=True)
        rtmp = Buf("xn0", [128, D], F32, lane=True)
        ri = {n: i for i, n in enumerate(ROWS)}

        def make_rows(gn, bn, gsrc, bsrc, addsrc):
            dma("sp", rows[gn].lane, rows[gn][:, :], rows_in[ri[gsrc]], W=[rows[gn].res])
            ts("dve", rows[gn][:, :], rows[gn][:, :], ALPHA, None, ALU.mult, None, [rows[gn].res], [rows[gn].res])
            dma("sp", rows[bn].lane, rows[bn][:, :], rows_in[ri[bsrc]], W=[rows[bn].res])
            dma("sp", rtmp.lane, rtmp[:, :], rows_in[ri[addsrc]], W=[rtmp.res])
            stt(rows[bn][:, :], rows[bn][:, :], ALPHA, rtmp[:, :], ALU.mult, ALU.add,
                [rows[bn].res, rtmp.res], [rows[bn].res])

        halfv = Buf("halfv", [128, 32], F32)
        ts("dve", halfv[:, 0:16], V("cln_g", 0, 16), 0.5, None, ALU.mult, None, [vecs.res], [halfv.res])
        ts("dve", halfv[:, 16:32], V("b_g0", 0, 16), 0.5, None, ALU.mult, None, [vecs.res], [halfv.res])
        mhalf = Buf("mhalf", [128, TT3], F32)
        S.op("pool", lambda E: E.memset(mhalf[:, :], -0.5), [], [mhalf.res])
        hT3s = [Buf("hT3_%d" % i, [128, 8, TT3], BF16, lane=True) for i in range(2)]
        v3s = [Buf("v3_%d" % i, [128, 12, TT3], BF16, lane=True) for i in range(2)]
        c13 = Buf("c13", [128, 8, TT3], F32, lane=True)
        xC = Buf("xC", [128, NS, D], F32, lane=True)
        mean3 = Buf("mean3", [128, TT3], F32, lane=True)
        rstd3 = Buf("rstd3", [128, TT3], F32, lane=True)
        xnt = [Buf("xnt%d" % i, [128, TT3], F32) for i in range(2)]
        yt = [Buf("yt%d" % i, [128, TT3], F32) for i in range(2)]
        tht = [Buf("tht%d" % i, [128, TT3], F32) for i in range(2)]
        cBs = [Buf("cB%d" % i, [128, 8, TT3], BF16) for i in range(2)]
        t0b = [Buf("t0_%d" % i, [128, TT3], F32) for i in range(2)]
        t1b = [Buf("t1_%d" % i, [128, TT3], F32) for i in range(2)]
        u1b = [Buf("u1_%d" % i, [128, TT3], F32) for i in range(2)]
        u2b = [Buf("u2_%d" % i, [128, TT3], F32) for i in range(2)]
        zTs = [Buf("zT%d" % i, [128, 8, TT3], BF16) for i in range(2)]
        xn0 = rtmp
        tbss = [Buf("tbs%d" % i, [128, D], F32) for i in range(2)]
        xn1 = [Buf("xn1_%d" % i, [128, D], F32, lane=True) for i in range(2)]
        xn1bs = [Buf("xn1b%d" % i, [128, NS, D], BF16) for i in range(2)]
        h1T = [Buf("h1T%d" % i, [128, 8, TT3], BF16, lane=True) for i in range(2)]
        stb4 = Buf("stb4", [128, 4, 2, 6], F32)
        mvb4 = Buf("mvb4", [128, 4, 2], F32)
        vt4 = Buf("vt4", [128, 4], F32)
        rs1 = Buf("rs1", [128, 4], F32)
        nm1 = Buf("nm1", [128, 4], F32)

        xv3 = x.rearrange("(n s p) f -> n p s f", s=NS, p=128)
        hTv = hT_d.rearrange("(kc p) t -> p kc t", p=128)
        vv = v_d.rearrange("(kc p) t -> p kc t", p=128)
        c1v = c1_d.rearrange("(kc p) t -> p kc t", p=128)
        h1Tv = h1T_d.rearrange("(kc p) t -> p kc t", p=128)
        NT3 = T // TT3
        BS, BQ = 4, 5

        def rstd_pow(dst, var_ap, tmp, n, Rin, Rtmp, Rdst):
            ts("dve", tmp, var_ap, EPS, None, ALU.add, None, Rin, [Rtmp])
            tt("pool", dst, tmp, mhalf[:, 0:n], ALU.pow, [Rtmp, mhalf.res], [Rdst])

        def bcast_row(row, it):
            return cst_d[row][:, it * TT3:(it + 1) * TT3]

        def load_A(it):
            tsl = slice(it * TT3, (it + 1) * TT3)
            dma("sp", c13.lane, c13[:, :, :], c1v[:, :, tsl], R=[R_c1], W=[c13.res])
            dma("sp", mean3.lane, mean3[:, :], bcast_row(0, it), R=[R_cst], W=[mean3.res])
            dma("sp", rstd3.lane, rstd3[:, :], bcast_row(1, it), R=[R_cst], W=[rstd3.res])

        def load_B(it):
            tsl = slice(it * TT3, (it + 1) * TT3)
            h_, v_ = hT3s[it % 2], v3s[it % 2]
            dma("sp", h_.lane, h_[:, :, :], hTv[:, :, tsl], R=[R_hT], W=[h_.res])
            dma("sp", v_.lane, v_[:, :, :], vv[:, :, tsl], R=[R_v], W=[v_.res])

        def stage_A(it):
            cB = cBs[it % 2]
            for i in range(8 + 2):
                if i < 8:
                    kc = i
                    xb_ = xnt[kc % 2]
                    tt("dve", xb_[:, :], c13[:, kc, :], mean3[:, :], ALU.subtract, [c13.res, mean3.res], [xb_.res])
                    tt("dve", xb_[:, :], xb_[:, :], rstd3[:, :], ALU.mult, [xb_.res, rstd3.res], [xb_.res])
                if 1 <= i < 9:
                    kc = i - 1
                    xb_, yb_, th_ = xnt[kc % 2], yt[kc % 2], tht[kc % 2]
                    act(th_[:, :], xb_[:, :], AF.Tanh, [xb_.res, halfv.res], [th_.res],
                        bias=halfv[:, 8 + kc:9 + kc], scale=halfv[:, kc:kc + 1])
                    ts("dve", yb_[:, :], xb_[:, :], V("cln_g", kc), V("cln_b", kc), ALU.mult, ALU.add,
                       [xb_.res, vecs.res], [yb_.res])
                if 2 <= i:
                    kc = i - 2
                    yb_, th_ = yt[kc % 2], tht[kc % 2]
                    stt(cB[:, kc, :], th_[:, :], 1.0, yb_[:, :], ALU.add, ALU.mult, [th_.res, yb_.res], [cB.res])
                yield
            if it + 1 < NT3:
                load_A(it + 1)

        def stage_B(it):
            cB, zT = cBs[it % 2], zTs[it % 2]
            hT3, v3 = hT3s[it % 2], v3s[it % 2]
            if it + 1 < NT3:
                load_B(it + 1)

            def merge(oc):
                Y = 2 * (oc % 2) + 1
                RY = Rps[Y]
                s0, s1, u1, u2 = t0b[oc % 2], t1b[oc % 2], u1b[oc % 2], u2b[oc % 2]
                stt(u1[:, :], s0[:, :], 1.0, bank(Y)[:, 0:TT3], ALU.add, ALU.mult, [s0.res, RY], [u1.res])
                stt(u2[:, :], s1[:, :], 1.0, bank(Y)[:, TT3:2 * TT3], ALU.add, ALU.mult, [s1.res, RY], [u2.res])
                stt(zT[:, oc, :], u2[:, :], 0.5, u1[:, :], ALU.mult, ALU.add, [u1.res, u2.res], [zT.res])

            for oc in range(8):
                X = 2 * (oc % 2)
                Y = X + 1
                RX, RY = Rps[X], Rps[Y]
                for kc in range(8):
                    mm(bank(X)[:, 0:TT3], wgl[:, kc, oc * 128:(oc + 1) * 128], hT3[:, kc, :], kc == 0, kc == 7,
                       [wgl_r[0], hT3.res], [RX])
                for kc in range(8):
                    mm(bank(X)[:, TT3:2 * TT3], wgl[:, kc, D + oc * 128:D + (oc + 1) * 128], hT3[:, kc, :],
                       kc == 0, kc == 7, [wgl_r[1], hT3.res], [RX])
                for kc in range(12):
                    mm(bank(Y)[:, 0:TT3], wba[:, kc, oc * 128:(oc + 1) * 128], v3[:, kc, :], kc == 0, kc == 11,
                       [wba_r[0], v3.res], [RY])
                for kc in range(8):
                    mm(bank(Y)[:, TT3:2 * TT3], wbb[:, kc, oc * 128:(oc + 1) * 128], cB[:, kc, :], kc == 0, kc == 7,
                       [wbb_r[0], cB.res], [RY])
                s0, s1 = t0b[oc % 2], t1b[oc % 2]
                act(s0[:, :], bank(X)[:, 0:TT3], AF.Tanh, [RX, halfv.res], [s0.res],
                    bias=halfv[:, 16 + oc:17 + oc], scale=0.5)
                act(s1[:, :], bank(X)[:, TT3:2 * TT3], AF.Tanh, [RX, halfv.res], [s1.res],
                    bias=halfv[:, 24 + oc:25 + oc], scale=0.5)
                if oc >= 1:
                    merge(oc - 1)
                yield
            merge(7)
            yield

        def stage_C(it):
            zT = zTs[it % 2]
            xn1b = xn1bs[it % 2]
            for s in range(NS):
                tbs = tbss[s]
                col = it * NS + s
                act(xn0[:, :], xC[:, s, :], AF.Identity, [xC.res, rs_all.res, nm_all.res], [xn0.res],
                    bias=nm_all[:, col:col + 1], scale=rs_all[:, col:col + 1])
                tt("pool", tbs[:, :], xn0[:, :], rows["G1"][:, :], ALU.mult, [xn0.res, rows["G1"].res], [tbs.res])
                tt("dve", tbs[:, :], tbs[:, :], rows["B1"][:, :], ALU.add, [tbs.res, rows["B1"].res], [tbs.res])
                yield
            if it + 1 < NT3:
                dma("sp", xC.lane, xC[:, :, :], xv3[it + 1], W=[xC.res])
            for s in range(NS):
                tbs = tbss[s]
                for hh in range(2):
                    b = 4 + 2 * s + hh
                    for kc in range(8):
                        mm(bank(b), zT[:, kc, s * 128:(s + 1) * 128], wou[:, kc, hh * 512:(hh + 1) * 512],
                           kc == 0, kc == 7, [zT.res, wou_r[hh]], [Rps[b]])
                    stt(tbs[:, hh * 512:(hh + 1) * 512], bank(b), 0.5, tbs[:, hh * 512:(hh + 1) * 512],
                        ALU.mult, ALU.add, [tbs.res, Rps[b]], [tbs.res])
                for hh in range(2):
                    S.op("dve", (lambda o=stb4[:, s, hh, :], i=tbs[:, hh * 512:(hh + 1) * 512]:
                                 lambda E: E.bn_stats(out=o, in_=i))(), [tbs.res], [stb4.res])
                S.op("dve", (lambda o=mvb4[:, s, :], i=stb4[:, s, :, :].rearrange("p a b -> p (a b)"):
                             lambda E: E.bn_aggr(out=o, in_=i))(), [stb4.res], [mvb4.res])
                yield
            rstd_pow(rs1[:, 0:NS], mvb4[:, 0:NS, 1], vt4[:, 0:NS], NS, [mvb4.res], vt4.res, rs1.res)
            stt(nm1[:, 0:NS], mvb4[:, 0:NS, 0], -1.0, rs1[:, 0:NS], ALU.mult, ALU.mult,
                [mvb4.res, rs1.res], [nm1.res])
            yield
            for s in range(NS):
                tbs = tbss[s]
                x1 = xn1[s % 2]
                act(xn1b[:, s, :], tbs[:, :], AF.Identity, [tbs.res, rs1.res, nm1.res], [xn1b.res],
                    bias=nm1[:, s:s + 1], scale=rs1[:, s:s + 1])
                act(x1[:, :], tbs[:, :], AF.Identity, [tbs.res, rs1.res, nm1.res], [x1.res],
                    bias=nm1[:, s:s + 1], scale=rs1[:, s:s + 1])
                dma("sp", x1.lane, xn1_d[it * TT3 + s * 128:it * TT3 + (s + 1) * 128, :], x1[:, :],
                    R=[x1.res], WF=[R_xn1])
                yield

        def stage_D(it):
            xn1b = xn1bs[it % 2]
            ho = h1T[it % 2]
            for kc in range(8):
                b = 4 + kc % 4
                pst = bank(b).bitcast(BF16)
                for s in range(NS):
                    trp(pst[:, s * 128:(s + 1) * 128], xn1b[:, s, kc * 128:(kc + 1) * 128], ident_b[:, :],
                        [xn1b.res, ident_b.res], [Rps[b]])
                act(ho[:, kc, :], pst[:, 0:TT3], AF.Identity, [Rps[b], vecs.res], [ho.res],
                    bias=V("ln1_b", kc), scale=V("ln1_g", kc))
                if kc % 2 == 1:
                    yield
            dma("sp", ho.lane, h1Tv[:, :, it * TT3:(it + 1) * TT3], ho[:, :, :], R=[ho.res], WF=[R_h1T])

        def drive(primary, others):
            live = [g for g in others if g is not None]
            for _ in primary:
                for g in list(live):
                    for _k in range(2):
                        try:
                            next(g)
                        except StopIteration:
                            if g in live:
                                live.remove(g)
                            break
            for g in live:
                for _ in g:
                    pass

        load_A(0)
        load_B(0)
        dma("sp", xC.lane, xC[:, :, :], xv3[0], W=[xC.res])
        make_rows("G1", "B1", "emb_g", "emb_b", "b_out")
        for _ in stage_A(0):
            pass
        for it in range(NT3):
            gA = stage_A(it + 1) if it + 1 < NT3 else None
            gC = stage_C(it - 1) if it >= 1 else None
            gD = stage_D(it - 2) if it >= 2 else None
            drive(stage_B(it), [gC, gD, gA])
        p3b_mark = cur[0]
        cur[0] = persist_end
        wup = Buf("wup", [128, 8, DFF], BF16)
        assert cur[0] <= w3_end - 16384
        cur[0] = p3b_mark
        wup_r = []
        dead = [wgl_r[0], wgl_r[1], wbb_r[0], wba_r[0]]
        srcv_up = w_up.rearrange("(kc p) c -> p kc c", p=128)
        for cb in range(4):
            r = Res("wupblk")
            wup_r.append(r)
            dma("pool", S.dma_lane(), wup[:, 0:8, cb * 1024:(cb + 1) * 1024],
                srcv_up[:, :, cb * 1024:(cb + 1) * 1024], W=[r] + dead)
        drive(stage_C(NT3 - 1), [stage_D(NT3 - 2)])
        for _ in stage_D(NT3 - 1):
            pass

        S.barrier()
        cur[0] = persist_end + 8 * DFF * 2
        rows = {}
        for nm in ("G2", "B2", "ln2_g", "ln2_b"):
            rows[nm] = Buf("row_" + nm, [128, D], F32, lane=True)
        rtmp = Buf("rtmp2", [128, D], F32, lane=True)
        wdn = Buf("wdn", [128, 32, D], BF16)
        wdn_r = []
        load_weight(wdn, w_dn, 32, D, 0, 512, wdn_r)
        h4 = [Buf("h4_%d" % i, [128, 8, TT3], BF16, lane=True) for i in range(2)]
        x4 = [Buf("x4_%d" % i, [128, NS, D], F32, lane=True) for i in range(1)]
        mT = Buf("mT", [128, 32, TT3], BF16)
        rl = [Buf("rl%d" % i, [128, TT3], F32) for i in range(2)]
        t4 = [Buf("t4_%d" % i, [128, D], F32) for i in range(1)]
        o4 = [Buf("o4_%d" % i, [128, D], F32, lane=True) for i in range(2)]
        stb5 = Buf("stb5", [128, 4, 2, 6], F32)
        mvb5 = Buf("mvb5", [128, 4, 2], F32)
        rs2 = Buf("rs2", [128, 4], F32)
        nm2 = Buf("nm2", [128, 4], F32)
        xn1v = xn1_d.rearrange("(n s p) f -> n p s f", s=NS, p=128)

        def p4_load_early(it):
            sl = it % 2
            dma("sp", h4[sl].lane, h4[sl][:, :, :], h1Tv[:, :, it * TT3:(it + 1) * TT3], R=[R_h1T], W=[h4[sl].res])

        def p4_load_late(it):
            dma("sp", x4[0].lane, x4[0][:, :, :], xn1v[it], R=[R_xn1], W=[x4[0].res])
        p4_load_early(0)
        p4_load_late(0)
        make_rows("G2", "B2", "ln1_g", "ln1_b", "b_down")
        dma("sp", rows["ln2_g"].lane, rows["ln2_g"][:, :], rows_in[ri["ln2_g"]], W=[rows["ln2_g"].res])
        dma("sp", rows["ln2_b"].lane, rows["ln2_b"][:, :], rows_in[ri["ln2_b"]], W=[rows["ln2_b"].res])
        oi = 0
        for it in range(NT3):
            sl = it % 2
            if it + 1 < NT3:
                p4_load_early(it + 1)
            ht, xt4 = h4[sl], x4[0]
            for fc in range(32):
                b = next_bank(0, 8)
                for kc in range(8):
                    mm(bank(b, TT3), wup[:, kc, fc * 128:(fc + 1) * 128], ht[:, kc, :], kc == 0, kc == 7,
                       [wup_r[(fc * 128) // 1024], ht.res], [Rps[b]])
                r_ = rl[fc % 2]
                ts("dve", r_[:, :], bank(b, TT3), V("b_up", fc), 0.0, ALU.add, ALU.max, [Rps[b], vecs.res], [r_.res])
                act(mT[:, fc, :], r_[:, :], AF.Square, [r_.res], [mT.res])
            for s in range(NS):
                tbb = t4[0]
                tt("pool", tbb[:, :], xt4[:, s, :], rows["G2"][:, :], ALU.mult, [xt4.res, rows["G2"].res], [tbb.res])
                tt("pool", tbb[:, :], tbb[:, :], rows["B2"][:, :], ALU.add, [tbb.res, rows["B2"].res], [tbb.res])
                for hh in range(2):
                    b = next_bank(0, 8)
                    for fc in range(32):
                        mm(bank(b), mT[:, fc, s * 128:(s + 1) * 128], wdn[:, fc, hh * 512:(hh + 1) * 512],
                           fc == 0, fc == 31, [mT.res, wdn_r[hh]], [Rps[b]])
                    tt("dve", tbb[:, hh * 512:(hh + 1) * 512], tbb[:, hh * 512:(hh + 1) * 512], bank(b), ALU.add,
                       [tbb.res, Rps[b]], [tbb.res])
                for hh in range(2):
                    S.op("dve", (lambda s=s, hh=hh, tbb=tbb: lambda E: E.bn_stats(
                        out=stb5[:, s, hh, :], in_=tbb[:, hh * 512:(hh + 1) * 512]))(), [tbb.res], [stb5.res])
                S.op("dve", (lambda s=s: lambda E: E.bn_aggr(
                    out=mvb5[:, s, :], in_=stb5[:, s, :, :].rearrange("p a b -> p (a b)")))(), [stb5.res], [mvb5.res])
                act(rs2[:, s:s + 1], mvb5[:, s, 1:2], AF.Sqrt, [mvb5.res], [rs2.res], bias=EPS)
                S.op("dve", (lambda s=s: lambda E: E.reciprocal(out=rs2[:, s:s + 1], in_=rs2[:, s:s + 1]))(),
                     [rs2.res], [rs2.res])
                stt(nm2[:, s:s + 1], mvb5[:, s, 0:1], -1.0, rs2[:, s:s + 1], ALU.mult, ALU.mult,
                    [mvb5.res, rs2.res], [nm2.res])
                ob = o4[oi % 2]
                oi += 1
                act(ob[:, :], tbb[:, :], AF.Identity, [tbb.res, rs2.res, nm2.res], [ob.res],
                    bias=nm2[:, s:s + 1], scale=rs2[:, s:s + 1])
                tt("pool", ob[:, :], ob[:, :], rows["ln2_g"][:, :], ALU.mult, [ob.res, rows["ln2_g"].res], [ob.res])
                tt("dve", ob[:, :], ob[:, :], rows["ln2_b"][:, :], ALU.add, [ob.res, rows["ln2_b"].res], [ob.res])
                dma("sp", ob.lane, out[it * TT3 + s * 128:it * TT3 + (s + 1) * 128, :], ob[:, :],
                    R=[ob.res], WF=[R_out])
            if it + 1 < NT3:
                p4_load_late(it + 1)
        S.barrier()
        S.run()
    return nc


_NC_CACHE = {}


def _cols(v, p):
    v = np.asarray(v, np.float32).reshape(-1, p).T
    o = np.zeros((128, v.shape[1]), np.float32)
    o[:p] = v
    return o


def kernel(x, emb_ln_g, emb_ln_b, w_in, b_in, rnn_conv_w, rnn_conv_b, rg_gate_w, rg_gate_b,
           rg_a_param, w_branch_a, conv_w, conv_b, conv_ln_g, conv_ln_b, w_branch_b,
           w_out, b_out, ln1_g, ln1_b, w_up, b_up, w_down, b_down, ln2_g, ln2_b):
    f = lambda a: np.ascontiguousarray(np.asarray(a, np.float32))
    x = f(x)
    b_in0 = f(b_in)[0]
    parts = {
        "b_xr": _cols(b_in0[0:DR], 128), "b_yr": _cols(b_in0[DR:2 * DR], 128),
        "b_xca": _cols(b_in0[2 * DR:2 * DR + D], 128), "b_xcb": _cols(b_in0[2 * DR + D:2 * DR + 2 * D], 128),
        "b_g0": _cols(b_in0[5120:5120 + D], 128), "b_g1": _cols(b_in0[5120 + D:], 128),
        "emb_g": _cols(emb_ln_g, 128), "emb_b": _cols(emb_ln_b, 128),
        "conv_b": _cols(f(conv_b)[0], 128), "cln_g": _cols(f(conv_ln_g)[0], 128), "cln_b": _cols(f(conv_ln_b)[0], 128),
        "ln1_g": _cols(f(ln1_g)[0], 128), "ln1_b": _cols(f(ln1_b)[0], 128), "b_up": _cols(f(b_up)[0], 128),
        "convw": np.ascontiguousarray(f(conv_w)[0].reshape(31, 8, 128).transpose(2, 1, 0)).reshape(128, 8 * 31),
    }
    w4 = np.zeros((128, 64), np.float32)
    w4[:BW] = f(rnn_conv_w)[0].reshape(4, NB, BW).transpose(2, 1, 0).reshape(BW, 64)
    parts["w4"] = w4
    parts["b4"] = _cols(f(rnn_conv_b)[0], BW)
    gb = np.zeros((128, 64), np.float32)
    gb[:BW] = f(rg_gate_b)[0].reshape(64, BW).T
    parts["gb"] = gb
    ap_ = np.zeros((128, 32), np.float32)
    ap_[:BW] = f(rg_a_param)[0].reshape(2 * NB, BW).T
    parts["ap"] = ap_
    vecs = np.zeros((128, NV), np.float32)
    for nm, off in _V.items():
        a = parts[nm]
        vecs[:, off:off + a.shape[1]] = a
    rowsrc = {"emb_g": emb_ln_g, "emb_b": emb_ln_b, "b_out": f(b_out)[0], "ln1_g": f(ln1_g)[0],
              "ln1_b": f(ln1_b)[0], "b_down": f(b_down)[0], "ln2_g": f(ln2_g)[0], "ln2_b": f(ln2_b)[0]}
    rows = np.ascontiguousarray(np.stack(
        [np.broadcast_to(f(rowsrc[n]).reshape(1, D), (128, D)) for n in ROWS]))
    wg = np.ascontiguousarray(f(rg_gate_w)[0].reshape(64, BW, BW).transpose(1, 0, 2)).reshape(BW, 64 * BW)
    shared = {
        "w_in": f(w_in)[0], "w_ba": f(w_branch_a)[0], "w_bb": f(w_branch_b)[0], "w_out": f(w_out)[0],
        "w_up": f(w_up)[0], "w_dn": f(w_down)[0], "wg": wg, "vecs": vecs, "rows": rows,
        "ident": np.eye(128, dtype=np.float32),
    }
    if "nc" not in _NC_CACHE:
        _NC_CACHE["nc"] = build_nc()
    nc = _NC_CACHE["nc"]
    in_maps = [dict(shared, x=x[b]) for b in range(8)]
    res = run_bass_kernel_spmd(nc, in_maps, core_ids=list(range(8)))
    return np.stack([np.asarray(r["out"], np.float32) for r in res.results], axis=0)
```

```python
import contextlib
import numpy as np
import concourse.bass as bass
import concourse.mybir as mybir
from concourse.ap import AP
from concourse.bass_utils import run_bass_kernel_spmd

F32 = mybir.dt.float32
BF16 = mybir.dt.bfloat16
AF = mybir.ActivationFunctionType
ALU = mybir.AluOpType

T = 4096
D = 1024
DR = 1536
DIN = 7168
DFF = 4096
NB = 16
BW = 96
ALPHA = 2.0 ** 0.25
EPS = 1e-5
SAME_ENGINE_SYNC = ("pool", "dve", "act")


class Res:
    __slots__ = ("name", "w", "r")

    def __init__(self, name):
        self.name = name
        self.w = {}
        self.r = {}


class Sched:
    ENGS = ("pe", "act", "dve", "pool", "sp")

    def __init__(self, nc, stack):
        self.nc = nc
        self.stack = stack
        self.q = {e: [] for e in self.ENGS}
        self.lane_sem = {}
        self.lane_cnt = {}
        self.seen = {e: {} for e in self.ENGS}
        for e in self.ENGS:
            self.new_lane(e)
        self.n_dma_lanes = 0

    def new_lane(self, name):
        sem = self.stack.enter_context(self.nc.semaphore("s_" + name))
        self.lane_sem[name] = sem
        self.lane_cnt[name] = 0
        return name

    def dma_lane(self):
        self.n_dma_lanes += 1
        return self.new_lane("d%d" % self.n_dma_lanes)

    def _deps(self, eng, reads, writes, force=False):
        deps = {}

        def need(lane, v):
            if deps.get(lane, 0) < v:
                deps[lane] = v
        for b in reads:
            for lane, v in b.w.items():
                need(lane, v)
        for b in writes:
            for lane, v in b.w.items():
                need(lane, v)
            for lane, v in b.r.items():
                need(lane, v)
        waits = []
        for lane, v in deps.items():
            if lane == eng and eng not in SAME_ENGINE_SYNC and not force:
                continue
            if self.seen[eng].get(lane, 0) >= v:
                continue
            self.seen[eng][lane] = v
            waits.append((self.lane_sem[lane], v))
        return waits

    @staticmethod
    def _commit(ticket, reads, writes):
        lane, v = ticket
        for b in reads:
            if b.r.get(lane, 0) < v:
                b.r[lane] = v
        for b in writes:
            if b.w.get(lane, 0) < v:
                b.w[lane] = v

    def op(self, eng, fn, reads=(), writes=(), self_sync=False):
        waits = self._deps(eng, reads, writes, self_sync)
        self.lane_cnt[eng] += 1
        ticket = (eng, self.lane_cnt[eng])
        sem = self.lane_sem[eng]

        def emit(E, fn=fn, waits=waits, sem=sem):
            for s, v in waits:
                E.wait_ge(s, v)
            fn(E).then_inc(sem, 1)
        self.q[eng].append(emit)
        self._commit(ticket, reads, writes)

    def dma(self, eng, lane, fn, reads=(), writes=(), writes_free=()):
        waits = self._deps(eng, reads, writes)
        self.lane_cnt[lane] += 16
        ticket = (lane, self.lane_cnt[lane])
        sem = self.lane_sem[lane]

        def emit(E, fn=fn, waits=waits, sem=sem):
            for s, v in waits:
                E.wait_ge(s, v)
            fn(E).then_inc(sem, 16)
        self.q[eng].append(emit)
        self._commit(ticket, reads, tuple(writes) + tuple(writes_free))

    def barrier(self, engs=None):
        for eng in (engs or self.ENGS):
            waits = []
            for lane, v in self.lane_cnt.items():
                if v == 0 or lane == eng:
                    continue
                if self.seen[eng].get(lane, 0) >= v:
                    continue
                self.seen[eng][lane] = v
                waits.append((self.lane_sem[lane], v))

            def emit(E, waits=waits):
                for s, v in waits:
                    E.wait_ge(s, v)
            self.q[eng].append(emit)

    def run(self):
        with self.nc.Block() as block:
            @block.tensor
            def _(E):
                for f in self.q["pe"]:
                    f(E)

            @block.scalar
            def _(E):
                for f in self.q["act"]:
                    f(E)

            @block.vector
            def _(E):
                for f in self.q["dve"]:
                    f(E)

            @block.gpsimd
            def _(E):
                for f in self.q["pool"]:
                    f(E)

            @block.sync
            def _(E):
                for f in self.q["sp"]:
                    f(E)


def rev(ap):
    steps = [list(x) for x in ap.ap]
    st, cnt = steps[-1]
    off = ap.offset + st * (cnt - 1)
    steps[-1] = [-st, cnt]
    return AP(ap.tensor, off, steps)


_V = {}
_off = 0
for _nm, _n in [("b_xr", 12), ("b_yr", 12), ("b_xca", 8), ("b_xcb", 8), ("b_g0", 8), ("b_g1", 8),
                ("emb_g", 8), ("emb_b", 8), ("conv_b", 8), ("cln_g", 8), ("cln_b", 8),
                ("ln1_g", 8), ("ln1_b", 8), ("b_up", 32), ("convw", 8 * 31),
                ("w4", 64), ("b4", 16), ("gb", 64), ("ap", 32)]:
    _V[_nm] = _off
    _off += _n
NV = _off
ROWS = ["emb_g", "emb_b", "b_out", "ln1_g", "ln1_b", "b_down", "ln2_g", "ln2_b"]


def build_nc(debug=False):
    nc = bass.Bass("TRN2", target_bir_lowering=False)
    dt_in = lambda nm, shp: nc.dram_tensor(nm, shp, F32, kind="ExternalInput").ap()
    x = dt_in("x", [T, D])
    w_in = dt_in("w_in", [D, DIN])
    w_ba = dt_in("w_ba", [DR, D])
    w_bb = dt_in("w_bb", [D, D])
    w_out = dt_in("w_out", [D, D])
    w_up = dt_in("w_up", [D, DFF])
    w_dn = dt_in("w_dn", [DFF, D])
    wg_in = dt_in("wg", [BW, 64 * BW])
    vecs_in = dt_in("vecs", [128, NV])
    rows_in = dt_in("rows", [8, 128, D])
    ident_in = dt_in("ident", [128, 128])
    out = nc.dram_tensor("out", [T, D], F32, kind="ExternalOutput").ap()
    kw = {"kind": "ExternalOutput"} if debug else {}
    hT_d = nc.dram_tensor("hT_d", [D, T], BF16, **kw).ap()
    xr_d = nc.dram_tensor("xr_d", [DR, T], F32, **kw).ap()
    gy_d = nc.dram_tensor("gy_d", [DR, T], BF16, **kw).ap()
    c0_d = nc.dram_tensor("c0_d", [D, T], BF16, **kw).ap()
    c1_d = nc.dram_tensor("c1_d", [D, T], F32, **kw).ap()
    v_d = nc.dram_tensor("v_d", [DR, T], BF16, **kw).ap()
    xn1_d = nc.dram_tensor("xn1_d", [T, D], F32, **kw).ap()
    h1T_d = nc.dram_tensor("h1T_d", [D, T], BF16, **kw).ap()
    cst_d = nc.dram_tensor("cst_d", [2, 128, T], F32, **kw).ap()

    with contextlib.ExitStack() as st:
        S = Sched(nc, st)
        ARENA_F32 = 53000
        arena = st.enter_context(nc.sbuf_tensor("arena", [128, ARENA_F32], F32))
        base = nc.lookup_mloc(arena).addr
        ps_all = st.enter_context(nc.psum_tensor("ps_all", [128, 4096], F32))
        Rps = [Res("ps%d" % i) for i in range(8)]

        def bank(i, n=512, p=128):
            return ps_all[0:p, i * 512:i * 512 + n]

        cur = [0]
        cnt = [0]

        class Buf:
            def __init__(self, name, shape, dt, lane=False):
                nbytes = int(np.prod(shape[1:])) * (4 if dt == F32 else 2)
                nbytes = (nbytes + 63) // 64 * 64
                assert cur[0] + nbytes <= ARENA_F32 * 4, (name, cur[0], nbytes)
                cnt[0] += 1
                self.t = nc.alloc_sbuf_tensor_at("%s_%d" % (name, cnt[0]), list(shape), dt,
                                                 offset=base + cur[0])
                cur[0] += nbytes
                self.res = Res(name)
                self.lane = S.dma_lane() if lane else None

            def __getitem__(self, k):
                return self.t[k]

        def mm(o, l, r, start, stop, R, W):
            S.op("pe", lambda E: E.matmul(o, l, r, start=start, stop=stop), R, W)

        def trp(o, i, idn, R, W):
            S.op("pe", lambda E: E.transpose(o, i, idn), R, W)

        def act(o, i, func, R, W, bias=0.0, scale=1.0, self_sync=False):
            S.op("act", lambda E: E.activation(out=o, in_=i, func=func, bias=bias, scale=scale), R, W, self_sync)

        def ts(eng, o, i, s1, s2, op0, op1, R, W):
            if s2 is None:
                S.op(eng, lambda E: E.tensor_scalar(out=o, in0=i, scalar1=s1, scalar2=None, op0=op0), R, W)
            else:
                S.op(eng, lambda E: E.tensor_scalar(out=o, in0=i, scalar1=s1, scalar2=s2, op0=op0, op1=op1), R, W)

        def stt(o, i0, sc, i1, op0, op1, R, W):
            S.op("dve", lambda E: E.scalar_tensor_tensor(out=o, in0=i0, scalar=sc, in1=i1, op0=op0, op1=op1), R, W)

        def tt(eng, o, i0, i1, op, R, W):
            S.op(eng, lambda E: E.tensor_tensor(out=o, in0=i0, in1=i1, op=op), R, W)

        def cp(eng, o, i, R, W):
            S.op(eng, lambda E: E.tensor_copy(out=o, in_=i), R, W)

        def dma(eng, lane, o, i, R=(), W=(), WF=()):
            S.dma(eng, lane, lambda E: E.dma_start(out=o, in_=i), R, W, WF)

        R_hT, R_xr, R_gy, R_c0, R_c1, R_v, R_xn1, R_h1T, R_out = [Res(n) for n in
            ("hT_d", "xr_d", "gy_d", "c0_d", "c1_d", "v_d", "xn1_d", "h1T_d", "out")]
        R_cst = Res("cst_d")

        vecs = Buf("vecs", [128, NV], F32, lane=True)
        ident_f = Buf("ident_f", [128, 128], F32, lane=True)
        ident_b = Buf("ident_b", [128, 128], BF16)
        ones_f = Buf("ones_f", [128, 128], F32)
        hb = Buf("hb", [128, 64], F32)
        hc = Buf("hc", [128, 32], F32)
        tmpc = Buf("tmpc", [128, 32], F32)
        rs_all = Buf("rs_all", [128, 32], F32)
        nm_all = Buf("nm_all", [128, 32], F32)
        persist_end = cur[0]

        dma("sp", vecs.lane, vecs[:, :], vecs_in, W=[vecs.res])
        dma("sp", ident_f.lane, ident_f[:, :], ident_in, W=[ident_f.res])
        cp("dve", ident_b[:, :], ident_f[:, :], [ident_f.res], [ident_b.res])
        S.op("pool", lambda E: E.memset(ones_f[:, :], 1.0), [], [ones_f.res])
        V = lambda nm, j=0, n=1, p=128: vecs[0:p, _V[nm] + j:_V[nm] + j + n]
        ts("dve", hb[0:96, :], V("gb", 0, 64, 96), 0.5, None, ALU.mult, None, [vecs.res], [hb.res])
        act(tmpc[0:96, :], V("ap", 0, 32, 96), AF.Exp, [vecs.res], [tmpc.res], scale=-1.0)
        act(tmpc[0:96, :], tmpc[0:96, :], AF.Ln, [tmpc.res], [tmpc.res], bias=1.0, self_sync=True)
        ts("dve", hc[0:96, :], tmpc[0:96, :], -4.0, None, ALU.mult, None, [tmpc.res], [hc.res])

        def layer_norm_stats(xsrc, nsub, Rsrc, stb, mvb, rstd, nmr):
            for s in range(nsub):
                for hh in range(2):
                    S.op("dve", (lambda o=stb[:, s, hh, :], i=xsrc(s)[:, hh * 512:(hh + 1) * 512]:
                                 lambda E: E.bn_stats(out=o, in_=i))(), Rsrc, [stb.res])
                S.op("dve", (lambda s=s: lambda E: E.bn_aggr(
                    out=mvb[:, s, :], in_=stb[:, s, :, :].rearrange("p a b -> p (a b)")))(), [stb.res], [mvb.res])
            act(rstd[:, 0:nsub], mvb[:, 0:nsub, 1], AF.Sqrt, [mvb.res], [rstd.res], bias=EPS)
            S.op("dve", lambda E: E.reciprocal(out=rstd[:, 0:nsub], in_=rstd[:, 0:nsub]), [rstd.res], [rstd.res])
            stt(nmr[:, 0:nsub], mvb[:, 0:nsub, 0], -1.0, rstd[:, 0:nsub], ALU.mult, ALU.mult,
                [mvb.res, rstd.res], [nmr.res])

        def load_weight(buf, src, nk, ncols, c0, blk, reslist):
            srcv = src.rearrange("(kc p) c -> p kc c", p=128)
            for cb in range(ncols // blk):
                r = Res("wblk")
                reslist.append(r)
                dma("pool", S.dma_lane(), buf[:, 0:nk, cb * blk:(cb + 1) * blk],
                    srcv[:, :, c0 + cb * blk:c0 + (cb + 1) * blk], W=[r])

        psrot = [0]

        def next_bank(lo, hi):
            b = lo + psrot[0] % (hi - lo)
            psrot[0] += 1
            return b

        TT = 512
        w1 = Buf("w1", [128, 8, 5120], BF16)
        w1res = []
        load_weight(w1, w_in, 8, 5120, 0, 1024, w1res)
        xt = [Buf("xt%d" % i, [128, 4, D], F32, lane=True) for i in range(2)]
        xnb = Buf("xnb", [128, 4, D], BF16)
        hTt = [Buf("hTt%d" % i, [128, 8, TT], BF16, lane=True) for i in range(2)]
        xr_st = [Buf("xr_st%d" % i, [128, 6, TT], F32, lane=True) for i in range(2)]
        gy_st = [Buf("gy_st%d" % i, [128, 6, TT], BF16, lane=True) for i in range(2)]
        c0_st = [Buf("c0_st%d" % i, [128, 4, TT], BF16, lane=True) for i in range(2)]
        sg = [Buf("sg%d" % i, [128, TT], F32) for i in range(2)]
        stb = Buf("stb", [128, 4, 2, 6], F32)
        mvb = Buf("mvb", [128, 4, 2], F32)
        rstd = Buf("rstd", [128, 4], F32)
        nmr = Buf("nmr", [128, 4], F32)

        xv = x.rearrange("(n s p) f -> n p s f", s=4, p=128)
        NT1 = T // TT
        def p1_prepA(it):
            xb = xt[it % 2]
            layer_norm_stats(lambda s: xb[:, s, :], 4, [xb.res], stb, mvb, rstd, nmr)
            cp("dve", rs_all[:, it * 4:it * 4 + 4], rstd[:, 0:4], [rstd.res], [rs_all.res])
            cp("dve", nm_all[:, it * 4:it * 4 + 4], nmr[:, 0:4], [nmr.res], [nm_all.res])
            for s in range(4):
                act(xnb[:, s, :], xb[:, s, :], AF.Identity, [xb.res, rstd.res, nmr.res], [xnb.res],
                    bias=nmr[:, s:s + 1], scale=rstd[:, s:s + 1])

        def p1_prepB(it):
            hp = hTt[it % 2]
            for kc in range(8):
                b = next_bank(0, 2)
                pst = bank(b).bitcast(BF16)
                for s in range(4):
                    trp(pst[:, s * 128:(s + 1) * 128], xnb[:, s, kc * 128:(kc + 1) * 128], ident_b[:, :],
                        [xnb.res, ident_b.res], [Rps[b]])
                act(hp[:, kc, :], pst[:, 0:TT], AF.Identity, [Rps[b], vecs.res], [hp.res],
                    bias=V("emb_b", kc), scale=V("emb_g", kc))
            dma("sp", hp.lane, hT_d.rearrange("(kc p) t -> p kc t", p=128)[:, :, it * TT:(it + 1) * TT],
                hp[:, :, :], R=[hp.res], WF=[R_hT])

        dma("sp", xt[0].lane, xt[0][:, :, :], xv[0], W=[xt[0].res])
        dma("sp", xt[1].lane, xt[1][:, :, :], xv[1], W=[xt[1].res])
        p1_prepA(0)
        p1_prepB(0)
        for it in range(NT1):
            hb_ = hTt[it % 2]
            if it + 1 < NT1:
                p1_prepA(it + 1)

            def proj(col0, b):
                for kc in range(8):
                    mm(bank(b), w1[:, kc, col0:col0 + 128], hb_[:, kc, :], kc == 0, kc == 7,
                       [w1res[col0 // 1024], hb_.res], [Rps[b]])
            for c in range(12):
                b = next_bank(2, 8)
                proj(c * 128, b)
                stg = xr_st[c // 6]
                ts("dve", stg[:, c % 6, :], bank(b), V("b_xr", c), None, ALU.add, None,
                   [Rps[b], vecs.res], [stg.res])
                if c % 6 == 5:
                    dma("sp", stg.lane,
                        xr_d.rearrange("(kc p) t -> p kc t", p=128)[:, c - 5:c + 1, it * TT:(it + 1) * TT],
                        stg[:, :, :], R=[stg.res], WF=[R_xr])
            if it + 1 < NT1:
                p1_prepB(it + 1)
            if it + 2 < NT1:
                nb_ = xt[it % 2]
                dma("sp", nb_.lane, nb_[:, :, :], xv[it + 2], W=[nb_.res])
            for c in range(12):
                b = next_bank(2, 8)
                proj(DR + c * 128, b)
                stg = gy_st[c // 6]
                act(stg[:, c % 6, :], bank(b), AF.Gelu_apprx_tanh, [Rps[b], vecs.res], [stg.res],
                    bias=V("b_yr", c))
                if c % 6 == 5:
                    dma("sp", stg.lane,
                        gy_d.rearrange("(kc p) t -> p kc t", p=128)[:, c - 5:c + 1, it * TT:(it + 1) * TT],
                        stg[:, :, :], R=[stg.res], WF=[R_gy])
            for j in range(8):
                ba = next_bank(2, 8)
                proj(2 * DR + j * 128, ba)
                bb = next_bank(2, 8)
                proj(2 * DR + D + j * 128, bb)
                sgb = sg[j % 2]
                act(sgb[:, :], bank(bb), AF.Sigmoid, [Rps[bb], vecs.res], [sgb.res], bias=V("b_xcb", j))
                stg = c0_st[j // 4]
                stt(stg[:, j % 4, :], bank(ba), V("b_xca", j), sgb[:, :], ALU.add, ALU.mult,
                    [Rps[ba], sgb.res, vecs.res], [stg.res])
                if j % 4 == 3:
                    dma("sp", stg.lane,
                        c0_d.rearrange("(kc p) t -> p kc t", p=128)[:, j - 3:j + 1, it * TT:(it + 1) * TT],
                        stg[:, :, :], R=[stg.res], WF=[R_c0])

        S.barrier()
        cur[0] = persist_end
        wgb = Buf("wgb", [128, 64 * BW], BF16, lane=True)
        dma("pool", wgb.lane, wgb[0:BW, :], wg_in, W=[wgb.res])
        xrs = [Buf("xrs0", [128, T + 4], F32, lane=True)]
        gyb = Buf("gyb", [128, T], BF16, lane=True)
        ubs = [Buf("ub%d" % i, [128, T], F32) for i in range(2)]
        ubfs = [Buf("ubf%d" % i, [128, T], BF16) for i in range(2)]
        TRb = [Buf("TR%d" % d, [128, T], F32) for d in range(2)]
        TIb = [Buf("TI%d" % d, [128, T], F32) for d in range(2)]
        B0s = [Buf("B0_%d" % i, [128, T], F32) for i in range(2)]
        B1 = Buf("B1", [128, T], F32)
        D4 = [Buf("D4_%d" % i, [128, 4, BW], F32) for i in range(2)]
        P = BW
        xs = xrs[0]
        S.op("pool", lambda E: E.memset(xs[:, 0:2], 0.0), [], [xs.res])
        S.op("pool", lambda E: E.memset(xs[:, T + 2:T + 4], 0.0), [], [xs.res])

        def p2_loadx(n):
            dma("sp", xs.lane, xs[0:P, 2:T + 2], xr_d[n * BW:(n + 1) * BW, :], R=[R_xr], W=[xs.res])

        def p2_conv(n, half):
            ub, ubf, d4 = ubs[n % 2], ubfs[n % 2], D4[n % 2]
            if half == 0:
                for k in range(4):
                    act(d4[0:P, k, :], ident_f[0:P, 0:P], AF.Copy, [ident_f.res, vecs.res], [d4.res],
                        scale=V("w4", n * 4 + k, 1, P))
            for q in range(2 * half, 2 * half + 2):
                b2 = 2 * (next_bank(0, 4))
                for hh in range(2):
                    t0_ = q * 1024 + hh * 512
                    for k in range(4):
                        mm(bank(b2 + hh, 512, P), d4[0:P, k, :], xs[0:P, t0_ + k:t0_ + k + 512], k == 0, k == 3,
                           [d4.res, xs.res], [Rps[b2 + hh]])
                src = ps_all[0:P, b2 * 512:b2 * 512 + 1024]
                ts("dve", ub[0:P, q * 1024:(q + 1) * 1024], src, V("b4", n, 1, P), None, ALU.add, None,
                   [Rps[b2], Rps[b2 + 1], vecs.res], [ub.res])
                ts("dve", ubf[0:P, q * 1024:(q + 1) * 1024], src, V("b4", n, 1, P), None, ALU.add, None,
                   [Rps[b2], Rps[b2 + 1], vecs.res], [ubf.res])
            if half == 1 and n + 1 < NB:
                p2_loadx(n + 1)

        def p2_gates(n, d):
            ubf = ubfs[n % 2]
            for q in range(4):
                for g in range(2):
                    b2 = 2 * (next_bank(0, 4))
                    idx = (d * 2 + g) * NB + n
                    for hh in range(2):
                        mm(bank(b2 + hh, 512, P), wgb[0:P, idx * BW:(idx + 1) * BW],
                           ubf[0:P, q * 1024 + hh * 512:q * 1024 + (hh + 1) * 512], True, True,
                           [wgb.res, ubf.res], [Rps[b2 + hh]])
                    dst = (TRb if g == 0 else TIb)[d]
                    act(dst[0:P, q * 1024:(q + 1) * 1024], ps_all[0:P, b2 * 512:b2 * 512 + 1024], AF.Tanh,
                        [Rps[b2], Rps[b2 + 1], hb.res], [dst.res], bias=hb[0:P, idx:idx + 1], scale=0.5)

        def p2_dir(n, d):
            Bd = B0s[n % 2] if d == 0 else B1
            ub = ubs[n % 2]
            hcv = hc[0:P, d * NB + n:d * NB + n + 1]
            act(TRb[d][0:P, :], TRb[d][0:P, :], AF.Exp, [TRb[d].res, hc.res], [TRb[d].res], bias=hcv, scale=hcv)
            act(Bd[0:P, :], TRb[d][0:P, :], AF.Square, [TRb[d].res], [Bd.res])
            act(Bd[0:P, :], Bd[0:P, :], AF.Sqrt, [Bd.res], [Bd.res], bias=0.25, scale=-0.25)
            stt(TIb[d][0:P, :], TIb[d][0:P, :], 1.0, ub[0:P, :], ALU.add, ALU.mult,
                [TIb[d].res, ub.res], [TIb[d].res])
            tt("dve", TIb[d][0:P, :], TIb[d][0:P, :], Bd[0:P, :], ALU.mult,
               [TIb[d].res, Bd.res], [TIb[d].res])
            f = (lambda a: a) if d == 0 else rev
            S.op("dve", (lambda o=f(Bd[0:P, 0:T]), a_=f(TRb[d][0:P, 0:T]), b_=f(TIb[d][0:P, 0:T]):
                         lambda E: E.tensor_tensor_scan(out=o, data0=a_, data1=b_, initial=0.0,
                                                        op0=ALU.mult, op1=ALU.add))(),
                 [TRb[d].res, TIb[d].res, Bd.res], [Bd.res])

        p2_loadx(0)
        p2_conv(0, 0)
        p2_conv(0, 1)
        for n in range(NB):
            dma("sp", gyb.lane, gyb[0:P, :], gy_d[n * BW:(n + 1) * BW, :], R=[R_gy], W=[gyb.res])
            p2_gates(n, 0)
            if n + 1 < NB:
                p2_conv(n + 1, 0)
            p2_dir(n, 0)
            p2_gates(n, 1)
            if n + 1 < NB:
                p2_conv(n + 1, 1)
            p2_dir(n, 1)
            tt("pool", TIb[1][0:P, :], B0s[n % 2][0:P, :], B1[0:P, :], ALU.add,
               [B0s[n % 2].res, B1.res], [TIb[1].res])
            tt("pool", gyb[0:P, :], TIb[1][0:P, :], gyb[0:P, :], ALU.mult, [TIb[1].res, gyb.res], [gyb.res])
            dma("sp", gyb.lane, v_d[n * BW:(n + 1) * BW, :], gyb[0:P, :], R=[gyb.res], WF=[R_v])

        S.barrier()
        cur[0] = persist_end
        wou = Buf("wou", [128, 8, D], BF16)
        dead_base = cur[0]
        wgl = Buf("wgl", [128, 8, 2048], BF16)
        wbb = Buf("wbb", [128, 8, D], BF16)
        wba = Buf("wba", [128, 12, D], BF16)
        wgl_r, wbb_r, wba_r, wou_r = [], [], [], []
        load_weight(wgl, w_in, 8, 2048, 5120, 1024, wgl_r)
        load_weight(wbb, w_bb, 8, D, 0, 1024, wbb_r)
        load_weight(wba, w_ba, 12, D, 0, 1024, wba_r)
        load_weight(wou, w_out, 8, D, 0, 512, wou_r)
        w3_end = cur[0]
        cxb = [Buf("cx%d" % i, [128, T + 32], BF16, lane=True) for i in range(2)]
        Dg = [Buf("Dg%d" % i, [128, 31, 128], BF16) for i in range(2)]
        c1s = [Buf("c1s%d" % i, [128, T], F32, lane=True) for i in range(2)]
        acc1 = Buf("acc1", [128, T], F32, lane=True)
        acc2 = Buf("acc2", [128, T], F32, lane=True)
        sqc = Buf("sqc", [128, T], F32)
        for i in range(2):
            S.op("pool", (lambda i=i: lambda E: E.memset(cxb[i][:, 0:15], 0.0))(), [], [cxb[i].res])
            S.op("pool", (lambda i=i: lambda E: E.memset(cxb[i][:, T + 15:T + 32], 0.0))(), [], [cxb[i].res])

        NPE = 28

        def build_diag(j):
            dg_ = Dg[j % 2]
            ia = ident_f[:, :]
            wa = V("convw", j * 31, NPE)
            pstep_i = ia.ap[0][0]
            pstep_w = wa.ap[0][0]
            in0 = AP(ia.tensor, ia.offset, [[pstep_i, 128], [0, NPE], [1, 128]])
            in1 = AP(wa.tensor, wa.offset, [[pstep_w, 128], [1, NPE], [0, 128]])
            tt("dve", dg_[:, 0:NPE, :], in0, in1, ALU.mult, [ident_f.res, vecs.res], [dg_.res])

        def p2c_load(j):
            cb_ = cxb[j % 2]
            dma("sp", cb_.lane, cb_[:, 15:T + 15], c0_d[j * 128:(j + 1) * 128, :], R=[R_c0], W=[cb_.res])
        p2c_load(0)
        build_diag(0)
        for j in range(8):
            cb_ = cxb[j % 2]
            dg = Dg[j % 2]
            co = c1s[j % 2]
            if j + 1 < 8:
                p2c_load(j + 1)
                build_diag(j + 1)
            ts("dve", co[:, :], cb_[:, NPE:NPE + T], V("convw", j * 31 + NPE), None, ALU.mult, None,
               [cb_.res, vecs.res], [co.res])
            for k in range(NPE + 1, 31):
                stt(co[:, :], cb_[:, k:k + T], V("convw", j * 31 + k), co[:, :], ALU.mult, ALU.add,
                    [cb_.res, co.res, vecs.res], [co.res])
            for tt_ in range(8):
                b = next_bank(0, 8)
                for k in range(NPE):
                    mm(bank(b), dg[:, k, :], cb_[:, tt_ * 512 + k:tt_ * 512 + k + 512], k == 0, k == NPE - 1,
                       [dg.res, cb_.res], [Rps[b]])
                tsl = slice(tt_ * 512, (tt_ + 1) * 512)
                stt(co[:, tsl], bank(b), V("conv_b", j), co[:, tsl], ALU.add, ALU.add,
                    [Rps[b], co.res, vecs.res], [co.res])
            dma("sp", co.lane, c1_d[j * 128:(j + 1) * 128, :], co[:, :], R=[co.res], WF=[R_c1])
            if j == 0:
                cp("pool", acc1[:, :], co[:, :], [co.res], [acc1.res])
                act(acc2[:, :], co[:, :], AF.Square, [co.res], [acc2.res])
            else:
                tt("pool", acc1[:, :], acc1[:, :], co[:, :], ALU.add, [acc1.res, co.res], [acc1.res])
                act(sqc[:, :], co[:, :], AF.Square, [co.res], [sqc.res])
                tt("pool", acc2[:, :], acc2[:, :], sqc[:, :], ALU.add, [acc2.res, sqc.res], [acc2.res])
        for tt_ in range(8):
            tsl = slice(tt_ * 512, (tt_ + 1) * 512)
            b1_ = next_bank(0, 8)
            b2_ = next_bank(0, 8)
            mm(bank(b1_), ones_f[:, :], acc1[:, tsl], True, True, [ones_f.res, acc1.res], [Rps[b1_]])
            mm(bank(b2_), ones_f[:, :], acc2[:, tsl], True, True, [ones_f.res, acc2.res], [Rps[b2_]])
            ts("dve", sqc[:, tsl], bank(b1_), 1.0 / D, None, ALU.mult, None, [Rps[b1_], acc1.res], [sqc.res])
            tt("dve", c1s[1][:, tsl], sqc[:, tsl], sqc[:, tsl], ALU.mult, [sqc.res], [c1s[1].res])
            stt(c1s[0][:, tsl], bank(b2_), 1.0 / D, c1s[1][:, tsl], ALU.mult, ALU.subtract,
                [Rps[b2_], c1s[1].res, acc2.res], [c1s[0].res])
        act(c1s[0][:, :], c1s[0][:, :], AF.Sqrt, [c1s[0].res], [c1s[0].res], bias=EPS)
        S.op("dve", lambda E: E.reciprocal(out=c1s[0][:, :], in_=c1s[0][:, :]), [c1s[0].res], [c1s[0].res])
        dma("sp", acc1.lane, cst_d[0], sqc[:, :], R=[sqc.res], WF=[R_cst])
        dma("sp", acc2.lane, cst_d[1], c1s[0][:, :], R=[c1s[0].res], WF=[R_cst])

        S.barrier()
        cur[0] = w3_end
        TT3 = 256
        NS = TT3 // 128
        hT3s = [Buf("hT3_%d" % i, [128, 8, TT3], BF16, lane=True) for i in range(2)]
        v3s = [Buf("v3_%d" % i, [128, 12, TT3], BF16, lane=True) for i in range(2)]
        c13 = Buf("c13", [128, 8, TT3], F32, lane=True)
        mean3 = Buf("mean3", [128, TT3], F32, lane=True)
        rstd3 = Buf("rstd3", [128, TT3], F32, lane=True)
        xnt = [Buf("xnt%d" % i, [128, TT3], F32) for i in range(2)]
        yt = [Buf("yt%d" % i, [128, TT3], F32) for i in range(2)]
        tht = [Buf("tht%d" % i, [128, TT3], F32) for i in range(2)]
        cBs = [Buf("cB%d" % i, [128, 8, TT3], BF16) for i in range(2)]
        t0b = [Buf("t0_%d" % i, [128, TT3], F32) for i in range(2)]
        t1b = [Buf("t1_%d" % i, [128, TT3], F32) for i in range(2)]
        u1b = [Buf("u1_%d" % i, [128, TT3], F32) for i in range(2)]
        u2b = [Buf("u2_%d" % i, [128, TT3], F32) for i in range(2)]
        zTs = [Buf("zT%d" % i, [128, 8, TT3], BF16) for i in range(2)]
        assert cur[0] - 8 * TT3 * 2 == dead_base + 2 * 8 * DFF * 2, (cur[0], dead_base)
        rows = {}
        for nm in ("G1", "B1"):
            rows[nm] = Buf("row_" + nm, [128, D], F32, lane=True)
        rtmp = Buf("xn0", [128, D], F32, lane=True)
        ri = {n: i for i, n in enumerate(ROWS)}

        def make_rows(gn, bn, gsrc, bsrc, addsrc):
            dma("sp", rows[gn].lane, rows[gn][:, :], rows_in[ri[gsrc]], W=[rows[gn].res])
            ts("dve", rows[gn][:, :], rows[gn][:, :], ALPHA, None, ALU.mult, None, [rows[gn].res], [rows[gn].res])
            dma("sp", rows[bn].lane, rows[bn][:, :], rows_in[ri[bsrc]], W=[rows[bn].res])
            dma("sp", rtmp.lane, rtmp[:, :], rows_in[ri[addsrc]], W=[rtmp.res])
            stt(rows[bn][:, :], rows[bn][:, :], ALPHA, rtmp[:, :], ALU.mult, ALU.add,
                [rows[bn].res, rtmp.res], [rows[bn].res])

        halfv = Buf("halfv", [128, 32], F32)
        ts("dve", halfv[:, 0:16], V("cln_g", 0, 16), 0.5, None, ALU.mult, None, [vecs.res], [halfv.res])
        ts("dve", halfv[:, 16:32], V("b_g0", 0, 16), 0.5, None, ALU.mult, None, [vecs.res], [halfv.res])
        mhalf = Buf("mhalf", [128, TT3], F32)
        S.op("pool", lambda E: E.memset(mhalf[:, :], -0.5), [], [mhalf.res])
        xC = Buf("xC", [128, NS, D], F32, lane=True)
        xn0 = rtmp
        tbss = [Buf("tbs%d" % i, [128, D], F32) for i in range(2)]
        xn1 = [Buf("xn1_%d" % i, [128, D], F32, lane=True) for i in range(2)]
        xn1bs = [Buf("xn1b%d" % i, [128, NS, D], BF16) for i in range(2)]
        h1T = [Buf("h1T%d" % i, [128, 8, TT3], BF16, lane=True) for i in range(2)]
        stb4 = Buf("stb4", [128, 4, 2, 6], F32)
        mvb4 = Buf("mvb4", [128, 4, 2], F32)
        vt4 = Buf("vt4", [128, 4], F32)
        rs1 = Buf("rs1", [128, 4], F32)
        nm1 = Buf("nm1", [128, 4], F32)

        xv3 = x.rearrange("(n s p) f -> n p s f", s=NS, p=128)
        hTv = hT_d.rearrange("(kc p) t -> p kc t", p=128)
        vv = v_d.rearrange("(kc p) t -> p kc t", p=128)
        c1v = c1_d.rearrange("(kc p) t -> p kc t", p=128)
        h1Tv = h1T_d.rearrange("(kc p) t -> p kc t", p=128)
        NT3 = T // TT3
        BS, BQ = 4, 5

        def rstd_pow(dst, var_ap, tmp, n, Rin, Rtmp, Rdst):
            ts("dve", tmp, var_ap, EPS, None, ALU.add, None, Rin, [Rtmp])
            tt("pool", dst, tmp, mhalf[:, 0:n], ALU.pow, [Rtmp, mhalf.res], [Rdst])

        def bcast_row(row, it):
            return cst_d[row][:, it * TT3:(it + 1) * TT3]

        def load_A(it):
            tsl = slice(it * TT3, (it + 1) * TT3)
            dma("sp", c13.lane, c13[:, :, :], c1v[:, :, tsl], R=[R_c1], W=[c13.res])
            dma("sp", mean3.lane, mean3[:, :], bcast_row(0, it), R=[R_cst], W=[mean3.res])
            dma("sp", rstd3.lane, rstd3[:, :], bcast_row(1, it), R=[R_cst], W=[rstd3.res])

        def load_B(it):
            tsl = slice(it * TT3, (it + 1) * TT3)
            h_, v_ = hT3s[it % 2], v3s[it % 2]
            dma("sp", h_.lane, h_[:, :, :], hTv[:, :, tsl], R=[R_hT], W=[h_.res])
            dma("sp", v_.lane, v_[:, :, :], vv[:, :, tsl], R=[R_v], W=[v_.res])

        def stage_A(it):
            cB = cBs[it % 2]
            for i in range(8 + 2):
                if i < 8:
                    kc = i
                    xb_ = xnt[kc % 2]
                    tt("dve", xb_[:, :], c13[:, kc, :], mean3[:, :], ALU.subtract, [c13.res, mean3.res], [xb_.res])
                    tt("dve", xb_[:, :], xb_[:, :], rstd3[:, :], ALU.mult, [xb_.res, rstd3.res], [xb_.res])
                if 1 <= i < 9:
                    kc = i - 1
                    xb_, yb_, th_ = xnt[kc % 2], yt[kc % 2], tht[kc % 2]
                    act(th_[:, :], xb_[:, :], AF.Tanh, [xb_.res, halfv.res], [th_.res],
                        bias=halfv[:, 8 + kc:9 + kc], scale=halfv[:, kc:kc + 1])
                    ts("dve", yb_[:, :], xb_[:, :], V("cln_g", kc), V("cln_b", kc), ALU.mult, ALU.add,
                       [xb_.res, vecs.res], [yb_.res])
                if 2 <= i:
                    kc = i - 2
                    yb_, th_ = yt[kc % 2], tht[kc % 2]
                    stt(cB[:, kc, :], th_[:, :], 1.0, yb_[:, :], ALU.add, ALU.mult, [th_.res, yb_.res], [cB.res])
                yield
            if it + 1 < NT3:
                load_A(it + 1)

        def stage_B(it):
            cB, zT = cBs[it % 2], zTs[it % 2]
            hT3, v3 = hT3s[it % 2], v3s[it % 2]
            if it + 1 < NT3:
                load_B(it + 1)

            def merge(oc):
                Y = 2 * (oc % 2) + 1
                RY = Rps[Y]
                s0, s1, u1, u2 = t0b[oc % 2], t1b[oc % 2], u1b[oc % 2], u2b[oc % 2]
                stt(u1[:, :], s0[:, :], 1.0, bank(Y)[:, 0:TT3], ALU.add, ALU.mult, [s0.res, RY], [u1.res])
                stt(u2[:, :], s1[:, :], 1.0, bank(Y)[:, TT3:2 * TT3], ALU.add, ALU.mult, [s1.res, RY], [u2.res])
                stt(zT[:, oc, :], u2[:, :], 0.5, u1[:, :], ALU.mult, ALU.add, [u1.res, u2.res], [zT.res])

            for oc in range(8):
                X = 2 * (oc % 2)
                Y = X + 1
                RX, RY = Rps[X], Rps[Y]
                for kc in range(8):
                    mm(bank(X)[:, 0:TT3], wgl[:, kc, oc * 128:(oc + 1) * 128], hT3[:, kc, :], kc == 0, kc == 7,
                       [wgl_r[0], hT3.res], [RX])
                for kc in range(8):
                    mm(bank(X)[:, TT3:2 * TT3], wgl[:, kc, D + oc * 128:D + (oc + 1) * 128], hT3[:, kc, :],
                       kc == 0, kc == 7, [wgl_r[1], hT3.res], [RX])
                for kc in range(12):
                    mm(bank(Y)[:, 0:TT3], wba[:, kc, oc * 128:(oc + 1) * 128], v3[:, kc, :], kc == 0, kc == 11,
                       [wba_r[0], v3.res], [RY])
                for kc in range(8):
                    mm(bank(Y)[:, TT3:2 * TT3], wbb[:, kc, oc * 128:(oc + 1) * 128], cB[:, kc, :], kc == 0, kc == 7,
                       [wbb_r[0], cB.res], [RY])
                s0, s1 = t0b[oc % 2], t1b[oc % 2]
                act(s0[:, :], bank(X)[:, 0:TT3], AF.Tanh, [RX, halfv.res], [s0.res],
                    bias=halfv[:, 16 + oc:17 + oc], scale=0.5)
                act(s1[:, :], bank(X)[:, TT3:2 * TT3], AF.Tanh, [RX, halfv.res], [s1.res],
                    bias=halfv[:, 24 + oc:25 + oc], scale=0.5)
                if oc >= 1:
                    merge(oc - 1)
                yield
            merge(7)
            yield

        def stage_C(it):
            zT = zTs[it % 2]
            xn1b = xn1bs[it % 2]
            for s in range(NS):
                tbs = tbss[s]
                col = it * NS + s
                act(xn0[:, :], xC[:, s, :], AF.Identity, [xC.res, rs_all.res, nm_all.res], [xn0.res],
                    bias=nm_all[:, col:col + 1], scale=rs_all[:, col:col + 1])
                tt("pool", tbs[:, :], xn0[:, :], rows["G1"][:, :], ALU.mult, [xn0.res, rows["G1"].res], [tbs.res])
                tt("dve", tbs[:, :], tbs[:, :], rows["B1"][:, :], ALU.add, [tbs.res, rows["B1"].res], [tbs.res])
                yield
            if it + 1 < NT3:
                dma("sp", xC.lane, xC[:, :, :], xv3[it + 1], W=[xC.res])
            for s in range(NS):
                tbs = tbss[s]
                for hh in range(2):
                    b = 4 + 2 * s + hh
                    for kc in range(8):
                        mm(bank(b), zT[:, kc, s * 128:(s + 1) * 128], wou[:, kc, hh * 512:(hh + 1) * 512],
                           kc == 0, kc == 7, [zT.res, wou_r[hh]], [Rps[b]])
                    stt(tbs[:, hh * 512:(hh + 1) * 512], bank(b), 0.5, tbs[:, hh * 512:(hh + 1) * 512],
                        ALU.mult, ALU.add, [tbs.res, Rps[b]], [tbs.res])
                for hh in range(2):
                    S.op("dve", (lambda o=stb4[:, s, hh, :], i=tbs[:, hh * 512:(hh + 1) * 512]:
                                 lambda E: E.bn_stats(out=o, in_=i))(), [tbs.res], [stb4.res])
                S.op("dve", (lambda o=mvb4[:, s, :], i=stb4[:, s, :, :].rearrange("p a b -> p (a b)"):
                             lambda E: E.bn_aggr(out=o, in_=i))(), [stb4.res], [mvb4.res])
                yield
            rstd_pow(rs1[:, 0:NS], mvb4[:, 0:NS, 1], vt4[:, 0:NS], NS, [mvb4.res], vt4.res, rs1.res)
            stt(nm1[:, 0:NS], mvb4[:, 0:NS, 0], -1.0, rs1[:, 0:NS], ALU.mult, ALU.mult,
                [mvb4.res, rs1.res], [nm1.res])
            yield
            for s in range(NS):
                tbs = tbss[s]
                x1 = xn1[s % 2]
                act(xn1b[:, s, :], tbs[:, :], AF.Identity, [tbs.res, rs1.res, nm1.res], [xn1b.res],
                    bias=nm1[:, s:s + 1], scale=rs1[:, s:s + 1])
                act(x1[:, :], tbs[:, :], AF.Identity, [tbs.res, rs1.res, nm1.res], [x1.res],
                    bias=nm1[:, s:s + 1], scale=rs1[:, s:s + 1])
                dma("sp", x1.lane, xn1_d[it * TT3 + s * 128:it * TT3 + (s + 1) * 128, :], x1[:, :],
                    R=[x1.res], WF=[R_xn1])
                yield

        def stage_D(it):
            xn1b = xn1bs[it % 2]
            ho = h1T[it % 2]
            for kc in range(8):
                b = 4 + kc % 4
                pst = bank(b).bitcast(BF16)
                for s in range(NS):
                    trp(pst[:, s * 128:(s + 1) * 128], xn1b[:, s, kc * 128:(kc + 1) * 128], ident_b[:, :],
                        [xn1b.res, ident_b.res], [Rps[b]])
                act(ho[:, kc, :], pst[:, 0:TT3], AF.Identity, [Rps[b], vecs.res], [ho.res],
                    bias=V("ln1_b", kc), scale=V("ln1_g", kc))
                if kc % 2 == 1:
                    yield
            dma("sp", ho.lane, h1Tv[:, :, it * TT3:(it + 1) * TT3], ho[:, :, :], R=[ho.res], WF=[R_h1T])

        def drive(primary, others):
            live = [g for g in others if g is not None]
            for _ in primary:
                for g in list(live):
                    for _k in range(2):
                        try:
                            next(g)
                        except StopIteration:
                            if g in live:
                                live.remove(g)
                            break
            for g in live:
                for _ in g:
                    pass

        load_A(0)
        load_B(0)
        dma("sp", xC.lane, xC[:, :, :], xv3[0], W=[xC.res])
        make_rows("G1", "B1", "emb_g", "emb_b", "b_out")
        for _ in stage_A(0):
            pass
        for it in range(NT3):
            gA = stage_A(it + 1) if it + 1 < NT3 else None
            gC = stage_C(it - 1) if it >= 1 else None
            gD = stage_D(it - 2) if it >= 2 else None
            drive(stage_B(it), [gC, gD, gA])
        assert (NT3 - 1) % 2 == 1
        p3b_mark = cur[0]
        cur[0] = dead_base
        wup = Buf("wup", [128, 8, DFF], BF16)
        wdn = Buf("wdn", [128, 32, D], BF16)
        ffn_w_end = cur[0]
        cur[0] = p3b_mark
        dead = [wgl_r[0], wgl_r[1], wbb_r[0], wba_r[0], c13.res, mean3.res, rstd3.res, zTs[0].res]
        for lst in (hT3s, v3s, xnt, yt, tht, cBs, t0b, t1b, u1b, u2b):
            dead += [b_.res for b_ in lst]
        wup_r, wdn_r = [], []
        srcv_up = w_up.rearrange("(kc p) c -> p kc c", p=128)
        for cb in range(4):
            r = Res("wupblk")
            wup_r.append(r)
            dma("pool", S.dma_lane(), wup[:, 0:8, cb * 1024:(cb + 1) * 1024],
                srcv_up[:, :, cb * 1024:(cb + 1) * 1024], W=[r] + dead)
        srcv_dn = w_dn.rearrange("(kc p) c -> p kc c", p=128)
        for cb in range(2):
            r = Res("wdnblk")
            wdn_r.append(r)
            dma("pool", S.dma_lane(), wdn[:, 0:32, cb * 512:(cb + 1) * 512],
                srcv_dn[:, :, cb * 512:(cb + 1) * 512], W=[r] + dead)
        drive(stage_C(NT3 - 1), [stage_D(NT3 - 2)])
        for _ in stage_D(NT3 - 1):
            pass

        S.barrier()
        cur[0] = persist_end
        rows = {}
        for nm in ("G2", "B2", "ln2_g", "ln2_b"):
            rows[nm] = Buf("row_" + nm, [128, D], F32, lane=True)
        assert cur[0] <= dead_base
        cur[0] = ffn_w_end
        rtmp = Buf("rtmp2", [128, D], F32, lane=True)
        h4 = [Buf("h4_%d" % i, [128, 8, TT3], BF16, lane=True) for i in range(2)]
        x4 = [Buf("x4_%d" % i, [128, NS, D], F32, lane=True) for i in range(1)]
        mT = Buf("mT", [128, 32, TT3], BF16)
        rl = [Buf("rl%d" % i, [128, TT3], F32) for i in range(2)]
        t4 = [Buf("t4_%d" % i, [128, D], F32) for i in range(1)]
        o4 = [Buf("o4_%d" % i, [128, D], F32, lane=True) for i in range(2)]
        stb5 = Buf("stb5", [128, 4, 2, 6], F32)
        mvb5 = Buf("mvb5", [128, 4, 2], F32)
        rs2 = Buf("rs2", [128, 4], F32)
        nm2 = Buf("nm2", [128, 4], F32)
        xn1v = xn1_d.rearrange("(n s p) f -> n p s f", s=NS, p=128)

        def p4_load_early(it):
            sl = it % 2
            dma("sp", h4[sl].lane, h4[sl][:, :, :], h1Tv[:, :, it * TT3:(it + 1) * TT3], R=[R_h1T], W=[h4[sl].res])

        def p4_load_late(it):
            dma("sp", x4[0].lane, x4[0][:, :, :], xn1v[it], R=[R_xn1], W=[x4[0].res])
        p4_load_early(0)
        p4_load_late(0)
        make_rows("G2", "B2", "ln1_g", "ln1_b", "b_down")
        dma("sp", rows["ln2_g"].lane, rows["ln2_g"][:, :], rows_in[ri["ln2_g"]], W=[rows["ln2_g"].res])
        dma("sp", rows["ln2_b"].lane, rows["ln2_b"][:, :], rows_in[ri["ln2_b"]], W=[rows["ln2_b"].res])
        oi = 0
        for it in range(NT3):
            sl = it % 2
            if it + 1 < NT3:
                p4_load_early(it + 1)
            ht, xt4 = h4[sl], x4[0]
            for fc in range(32):
                b = next_bank(0, 8)
                for kc in range(8):
                    mm(bank(b, TT3), wup[:, kc, fc * 128:(fc + 1) * 128], ht[:, kc, :], kc == 0, kc == 7,
                       [wup_r[(fc * 128) // 1024], ht.res], [Rps[b]])
                r_ = rl[fc % 2]
                ts("dve", r_[:, :], bank(b, TT3), V("b_up", fc), 0.0, ALU.add, ALU.max, [Rps[b], vecs.res], [r_.res])
                act(mT[:, fc, :], r_[:, :], AF.Square, [r_.res], [mT.res])
            for s in range(NS):
                tbb = t4[0]
                tt("pool", tbb[:, :], xt4[:, s, :], rows["G2"][:, :], ALU.mult, [xt4.res, rows["G2"].res], [tbb.res])
                tt("pool", tbb[:, :], tbb[:, :], rows["B2"][:, :], ALU.add, [tbb.res, rows["B2"].res], [tbb.res])
                for hh in range(2):
                    b = next_bank(0, 8)
                    for fc in range(32):
                        mm(bank(b), mT[:, fc, s * 128:(s + 1) * 128], wdn[:, fc, hh * 512:(hh + 1) * 512],
                           fc == 0, fc == 31, [mT.res, wdn_r[hh]], [Rps[b]])
                    tt("dve", tbb[:, hh * 512:(hh + 1) * 512], tbb[:, hh * 512:(hh + 1) * 512], bank(b), ALU.add,
                       [tbb.res, Rps[b]], [tbb.res])
                for hh in range(2):
                    S.op("dve", (lambda s=s, hh=hh, tbb=tbb: lambda E: E.bn_stats(
                        out=stb5[:, s, hh, :], in_=tbb[:, hh * 512:(hh + 1) * 512]))(), [tbb.res], [stb5.res])
                S.op("dve", (lambda s=s: lambda E: E.bn_aggr(
                    out=mvb5[:, s, :], in_=stb5[:, s, :, :].rearrange("p a b -> p (a b)")))(), [stb5.res], [mvb5.res])
                act(rs2[:, s:s + 1], mvb5[:, s, 1:2], AF.Sqrt, [mvb5.res], [rs2.res], bias=EPS)
                S.op("dve", (lambda s=s: lambda E: E.reciprocal(out=rs2[:, s:s + 1], in_=rs2[:, s:s + 1]))(),
                     [rs2.res], [rs2.res])
                stt(nm2[:, s:s + 1], mvb5[:, s, 0:1], -1.0, rs2[:, s:s + 1], ALU.mult, ALU.mult,
                    [mvb5.res, rs2.res], [nm2.res])
                ob = o4[oi % 2]
                oi += 1
                act(ob[:, :], tbb[:, :], AF.Identity, [tbb.res, rs2.res, nm2.res], [ob.res],
                    bias=nm2[:, s:s + 1], scale=rs2[:, s:s + 1])
                tt("pool", ob[:, :], ob[:, :], rows["ln2_g"][:, :], ALU.mult, [ob.res, rows["ln2_g"].res], [ob.res])
                tt("dve", ob[:, :], ob[:, :], rows["ln2_b"][:, :], ALU.add, [ob.res, rows["ln2_b"].res], [ob.res])
                dma("sp", ob.lane, out[it * TT3 + s * 128:it * TT3 + (s + 1) * 128, :], ob[:, :],
                    R=[ob.res], WF=[R_out])
            if it + 1 < NT3:
                p4_load_late(it + 1)
        S.barrier()
        S.run()
    return nc


_NC_CACHE = {}


def _cols(v, p):
    v = np.asarray(v, np.float32).reshape(-1, p).T
    o = np.zeros((128, v.shape[1]), np.float32)
    o[:p] = v
    return o


def kernel(x, emb_ln_g, emb_ln_b, w_in, b_in, rnn_conv_w, rnn_conv_b, rg_gate_w, rg_gate_b,
           rg_a_param, w_branch_a, conv_w, conv_b, conv_ln_g, conv_ln_b, w_branch_b,
           w_out, b_out, ln1_g, ln1_b, w_up, b_up, w_down, b_down, ln2_g, ln2_b):
    f = lambda a: np.ascontiguousarray(np.asarray(a, np.float32))
    x = f(x)
    b_in0 = f(b_in)[0]
    parts = {
        "b_xr": _cols(b_in0[0:DR], 128), "b_yr": _cols(b_in0[DR:2 * DR], 128),
        "b_xca": _cols(b_in0[2 * DR:2 * DR + D], 128), "b_xcb": _cols(b_in0[2 * DR + D:2 * DR + 2 * D], 128),
        "b_g0": _cols(b_in0[5120:5120 + D], 128), "b_g1": _cols(b_in0[5120 + D:], 128),
        "emb_g": _cols(emb_ln_g, 128), "emb_b": _cols(emb_ln_b, 128),
        "conv_b": _cols(f(conv_b)[0], 128), "cln_g": _cols(f(conv_ln_g)[0], 128), "cln_b": _cols(f(conv_ln_b)[0], 128),
        "ln1_g": _cols(f(ln1_g)[0], 128), "ln1_b": _cols(f(ln1_b)[0], 128), "b_up": _cols(f(b_up)[0], 128),
        "convw": np.ascontiguousarray(f(conv_w)[0].reshape(31, 8, 128).transpose(2, 1, 0)).reshape(128, 8 * 31),
    }
    w4 = np.zeros((128, 64), np.float32)
    w4[:BW] = f(rnn_conv_w)[0].reshape(4, NB, BW).transpose(2, 1, 0).reshape(BW, 64)
    parts["w4"] = w4
    parts["b4"] = _cols(f(rnn_conv_b)[0], BW)
    gb = np.zeros((128, 64), np.float32)
    gb[:BW] = f(rg_gate_b)[0].reshape(64, BW).T
    parts["gb"] = gb
    ap_ = np.zeros((128, 32), np.float32)
    ap_[:BW] = f(rg_a_param)[0].reshape(2 * NB, BW).T
    parts["ap"] = ap_
    vecs = np.zeros((128, NV), np.float32)
    for nm, off in _V.items():
        a = parts[nm]
        vecs[:, off:off + a.shape[1]] = a
    rowsrc = {"emb_g": emb_ln_g, "emb_b": emb_ln_b, "b_out": f(b_out)[0], "ln1_g": f(ln1_g)[0],
              "ln1_b": f(ln1_b)[0], "b_down": f(b_down)[0], "ln2_g": f(ln2_g)[0], "ln2_b": f(ln2_b)[0]}
    rows = np.ascontiguousarray(np.stack(
        [np.broadcast_to(f(rowsrc[n]).reshape(1, D), (128, D)) for n in ROWS]))
    wg = np.ascontiguousarray(f(rg_gate_w)[0].reshape(64, BW, BW).transpose(1, 0, 2)).reshape(BW, 64 * BW)
    shared = {
        "w_in": f(w_in)[0], "w_ba": f(w_branch_a)[0], "w_bb": f(w_branch_b)[0], "w_out": f(w_out)[0],
        "w_up": f(w_up)[0], "w_dn": f(w_down)[0], "wg": wg, "vecs": vecs, "rows": rows,
        "ident": np.eye(128, dtype=np.float32),
    }
    if "nc" not in _NC_CACHE:
        _NC_CACHE["nc"] = build_nc()
    nc = _NC_CACHE["nc"]
    in_maps = [dict(shared, x=x[b]) for b in range(8)]
    res = run_bass_kernel_spmd(nc, in_maps, core_ids=list(range(8)))
    return np.stack([np.asarray(r["out"], np.float32) for r in res.results], axis=0)
```

```python
import contextlib
import numpy as np
import concourse.bass as bass
import concourse.mybir as mybir
from concourse.ap import AP
from concourse.bass_utils import run_bass_kernel_spmd

F32 = mybir.dt.float32
BF16 = mybir.dt.bfloat16
AF = mybir.ActivationFunctionType
ALU = mybir.AluOpType

T = 4096
D = 1024
DR = 1536
DIN = 7168
DFF = 4096
NB = 16
BW = 96
ALPHA = 2.0 ** 0.25
EPS = 1e-5
SAME_ENGINE_SYNC = ("pool", "dve", "act")


class Res:
    __slots__ = ("name", "w", "r")

    def __init__(self, name):
        self.name = name
        self.w = {}
        self.r = {}


class Sched:
    ENGS = ("pe", "act", "dve", "pool", "sp")

    def __init__(self, nc, stack):
        self.nc = nc
        self.stack = stack
        self.q = {e: [] for e in self.ENGS}
        self.lane_sem = {}
        self.lane_cnt = {}
        self.seen = {e: {} for e in self.ENGS}
        for e in self.ENGS:
            self.new_lane(e)
        self.n_dma_lanes = 0

    def new_lane(self, name):
        sem = self.stack.enter_context(self.nc.semaphore("s_" + name))
        self.lane_sem[name] = sem
        self.lane_cnt[name] = 0
        return name

    def dma_lane(self):
        self.n_dma_lanes += 1
        return self.new_lane("d%d" % self.n_dma_lanes)

    def _deps(self, eng, reads, writes, force=False):
        deps = {}

        def need(lane, v):
            if deps.get(lane, 0) < v:
                deps[lane] = v
        for b in reads:
            for lane, v in b.w.items():
                need(lane, v)
        for b in writes:
            for lane, v in b.w.items():
                need(lane, v)
            for lane, v in b.r.items():
                need(lane, v)
        waits = []
        for lane, v in deps.items():
            if lane == eng and eng not in SAME_ENGINE_SYNC and not force:
                continue
            if self.seen[eng].get(lane, 0) >= v:
                continue
            self.seen[eng][lane] = v
            waits.append((self.lane_sem[lane], v))
        return waits

    @staticmethod
    def _commit(ticket, reads, writes):
        lane, v = ticket
        for b in reads:
            if b.r.get(lane, 0) < v:
                b.r[lane] = v
        for b in writes:
            if b.w.get(lane, 0) < v:
                b.w[lane] = v

    def op(self, eng, fn, reads=(), writes=(), self_sync=False):
        waits = self._deps(eng, reads, writes, self_sync)
        self.lane_cnt[eng] += 1
        ticket = (eng, self.lane_cnt[eng])
        sem = self.lane_sem[eng]

        def emit(E, fn=fn, waits=waits, sem=sem):
            for s, v in waits:
                E.wait_ge(s, v)
            fn(E).then_inc(sem, 1)
        self.q[eng].append(emit)
        self._commit(ticket, reads, writes)

    def dma(self, eng, lane, fn, reads=(), writes=(), writes_free=()):
        waits = self._deps(eng, reads, writes)
        self.lane_cnt[lane] += 16
        ticket = (lane, self.lane_cnt[lane])
        sem = self.lane_sem[lane]

        def emit(E, fn=fn, waits=waits, sem=sem):
            for s, v in waits:
                E.wait_ge(s, v)
            fn(E).then_inc(sem, 16)
        self.q[eng].append(emit)
        self._commit(ticket, reads, tuple(writes) + tuple(writes_free))

    def barrier(self, engs=None):
        for eng in (engs or self.ENGS):
            waits = []
            for lane, v in self.lane_cnt.items():
                if v == 0 or lane == eng:
                    continue
                if self.seen[eng].get(lane, 0) >= v:
                    continue
                self.seen[eng][lane] = v
                waits.append((self.lane_sem[lane], v))

            def emit(E, waits=waits):
                for s, v in waits:
                    E.wait_ge(s, v)
            self.q[eng].append(emit)

    def run(self):
        with self.nc.Block() as block:
            @block.tensor
            def _(E):
                for f in self.q["pe"]:
                    f(E)

            @block.scalar
            def _(E):
                for f in self.q["act"]:
                    f(E)

            @block.vector
            def _(E):
                for f in self.q["dve"]:
                    f(E)

            @block.gpsimd
            def _(E):
                for f in self.q["pool"]:
                    f(E)

            @block.sync
            def _(E):
                for f in self.q["sp"]:
                    f(E)


def rev(ap):
    steps = [list(x) for x in ap.ap]
    st, cnt = steps[-1]
    off = ap.offset + st * (cnt - 1)
    steps[-1] = [-st, cnt]
    return AP(ap.tensor, off, steps)


_V = {}
_off = 0
for _nm, _n in [("b_xr", 12), ("b_yr", 12), ("b_xca", 8), ("b_xcb", 8), ("b_g0", 8), ("b_g1", 8),
                ("emb_g", 8), ("emb_b", 8), ("conv_b", 8), ("cln_g", 8), ("cln_b", 8),
                ("ln1_g", 8), ("ln1_b", 8), ("b_up", 32), ("convw", 8 * 31),
                ("w4", 64), ("b4", 16), ("gb", 64), ("ap", 32)]:
    _V[_nm] = _off
    _off += _n
NV = _off
ROWS = ["emb_g", "emb_b", "b_out", "ln1_g", "ln1_b", "b_down", "ln2_g", "ln2_b"]


def build_nc(debug=False):
    nc = bass.Bass("TRN2", target_bir_lowering=False)
    dt_in = lambda nm, shp: nc.dram_tensor(nm, shp, F32, kind="ExternalInput").ap()
    x = dt_in("x", [T, D])
    w_in = dt_in("w_in", [D, DIN])
    w_ba = dt_in("w_ba", [DR, D])
    w_bb = dt_in("w_bb", [D, D])
    w_out = dt_in("w_out", [D, D])
    w_up = dt_in("w_up", [D, DFF])
    w_dn = dt_in("w_dn", [DFF, D])
    wg_in = dt_in("wg", [BW, 64 * BW])
    vecs_in = dt_in("vecs", [128, NV])
    rows_in = dt_in("rows", [8, 128, D])
    ident_in = dt_in("ident", [128, 128])
    out = nc.dram_tensor("out", [T, D], F32, kind="ExternalOutput").ap()
    kw = {"kind": "ExternalOutput"} if debug else {}
    hT_d = nc.dram_tensor("hT_d", [D, T], BF16, **kw).ap()
    xr_d = nc.dram_tensor("xr_d", [DR, T], F32, **kw).ap()
    gy_d = nc.dram_tensor("gy_d", [DR, T], BF16, **kw).ap()
    c0_d = nc.dram_tensor("c0_d", [D, T], BF16, **kw).ap()
    c1_d = nc.dram_tensor("c1_d", [D, T], F32, **kw).ap()
    v_d = nc.dram_tensor("v_d", [DR, T], BF16, **kw).ap()
    xn1_d = nc.dram_tensor("xn1_d", [T, D], F32, **kw).ap()
    h1T_d = nc.dram_tensor("h1T_d", [D, T], BF16, **kw).ap()
    cst_d = nc.dram_tensor("cst_d", [2, 128, T], F32, **kw).ap()

    with contextlib.ExitStack() as st:
        S = Sched(nc, st)
        ARENA_F32 = 53000
        arena = st.enter_context(nc.sbuf_tensor("arena", [128, ARENA_F32], F32))
        base = nc.lookup_mloc(arena).addr
        ps_all = st.enter_context(nc.psum_tensor("ps_all", [128, 4096], F32))
        Rps = [Res("ps%d" % i) for i in range(8)]

        def bank(i, n=512, p=128):
            return ps_all[0:p, i * 512:i * 512 + n]

        cur = [0]
        cnt = [0]

        class Buf:
            def __init__(self, name, shape, dt, lane=False):
                nbytes = int(np.prod(shape[1:])) * (4 if dt == F32 else 2)
                nbytes = (nbytes + 63) // 64 * 64
                assert cur[0] + nbytes <= ARENA_F32 * 4, (name, cur[0], nbytes)
                cnt[0] += 1
                self.t = nc.alloc_sbuf_tensor_at("%s_%d" % (name, cnt[0]), list(shape), dt,
                                                 offset=base + cur[0])
                cur[0] += nbytes
                self.res = Res(name)
                self.lane = S.dma_lane() if lane else None

            def __getitem__(self, k):
                return self.t[k]

        def mm(o, l, r, start, stop, R, W):
            S.op("pe", lambda E: E.matmul(o, l, r, start=start, stop=stop), R, W)

        def trp(o, i, idn, R, W):
            S.op("pe", lambda E: E.transpose(o, i, idn), R, W)

        def act(o, i, func, R, W, bias=0.0, scale=1.0, self_sync=False):
            S.op("act", lambda E: E.activation(out=o, in_=i, func=func, bias=bias, scale=scale), R, W, self_sync)

        def ts(eng, o, i, s1, s2, op0, op1, R, W):
            if s2 is None:
                S.op(eng, lambda E: E.tensor_scalar(out=o, in0=i, scalar1=s1, scalar2=None, op0=op0), R, W)
            else:
                S.op(eng, lambda E: E.tensor_scalar(out=o, in0=i, scalar1=s1, scalar2=s2, op0=op0, op1=op1), R, W)

        def stt(o, i0, sc, i1, op0, op1, R, W):
            S.op("dve", lambda E: E.scalar_tensor_tensor(out=o, in0=i0, scalar=sc, in1=i1, op0=op0, op1=op1), R, W)

        def tt(eng, o, i0, i1, op, R, W):
            S.op(eng, lambda E: E.tensor_tensor(out=o, in0=i0, in1=i1, op=op), R, W)

        def cp(eng, o, i, R, W):
            S.op(eng, lambda E: E.tensor_copy(out=o, in_=i), R, W)

        def dma(eng, lane, o, i, R=(), W=(), WF=()):
            S.dma(eng, lane, lambda E: E.dma_start(out=o, in_=i), R, W, WF)

        R_hT, R_xr, R_gy, R_c0, R_c1, R_v, R_xn1, R_h1T, R_out = [Res(n) for n in
            ("hT_d", "xr_d", "gy_d", "c0_d", "c1_d", "v_d", "xn1_d", "h1T_d", "out")]
        R_cst = Res("cst_d")

        vecs = Buf("vecs", [128, NV], F32, lane=True)
        ident_f = Buf("ident_f", [128, 128], F32, lane=True)
        ident_b = Buf("ident_b", [128, 128], BF16)
        ones_f = Buf("ones_f", [128, 128], F32)
        hb = Buf("hb", [128, 64], F32)
        hc = Buf("hc", [128, 32], F32)
        tmpc = Buf("tmpc", [128, 32], F32)
        rs_all = Buf("rs_all", [128, 32], F32)
        nm_all = Buf("nm_all", [128, 32], F32)
        persist_end = cur[0]

        dma("sp", vecs.lane, vecs[:, :], vecs_in, W=[vecs.res])
        dma("sp", ident_f.lane, ident_f[:, :], ident_in, W=[ident_f.res])
        cp("dve", ident_b[:, :], ident_f[:, :], [ident_f.res], [ident_b.res])
        S.op("pool", lambda E: E.memset(ones_f[:, :], 1.0), [], [ones_f.res])
        V = lambda nm, j=0, n=1, p=128: vecs[0:p, _V[nm] + j:_V[nm] + j + n]
        ts("dve", hb[0:96, :], V("gb", 0, 64, 96), 0.5, None, ALU.mult, None, [vecs.res], [hb.res])
        act(tmpc[0:96, :], V("ap", 0, 32, 96), AF.Exp, [vecs.res], [tmpc.res], scale=-1.0)
        act(tmpc[0:96, :], tmpc[0:96, :], AF.Ln, [tmpc.res], [tmpc.res], bias=1.0, self_sync=True)
        ts("dve", hc[0:96, :], tmpc[0:96, :], -4.0, None, ALU.mult, None, [tmpc.res], [hc.res])

        def layer_norm_stats(xsrc, nsub, Rsrc, stb, mvb, rstd, nmr):
            for s in range(nsub):
                for hh in range(2):
                    S.op("dve", (lambda o=stb[:, s, hh, :], i=xsrc(s)[:, hh * 512:(hh + 1) * 512]:
                                 lambda E: E.bn_stats(out=o, in_=i))(), Rsrc, [stb.res])
                S.op("dve", (lambda s=s: lambda E: E.bn_aggr(
                    out=mvb[:, s, :], in_=stb[:, s, :, :].rearrange("p a b -> p (a b)")))(), [stb.res], [mvb.res])
            act(rstd[:, 0:nsub], mvb[:, 0:nsub, 1], AF.Sqrt, [mvb.res], [rstd.res], bias=EPS)
            S.op("dve", lambda E: E.reciprocal(out=rstd[:, 0:nsub], in_=rstd[:, 0:nsub]), [rstd.res], [rstd.res])
            stt(nmr[:, 0:nsub], mvb[:, 0:nsub, 0], -1.0, rstd[:, 0:nsub], ALU.mult, ALU.mult,
                [mvb.res, rstd.res], [nmr.res])

        def load_weight(buf, src, nk, ncols, c0, blk, reslist):
            srcv = src.rearrange("(kc p) c -> p kc c", p=128)
            for cb in range(ncols // blk):
                r = Res("wblk")
                reslist.append(r)
                dma("pool", S.dma_lane(), buf[:, 0:nk, cb * blk:(cb + 1) * blk],
                    srcv[:, :, c0 + cb * blk:c0 + (cb + 1) * blk], W=[r])

        psrot = [0]

        def next_bank(lo, hi):
            b = lo + psrot[0] % (hi - lo)
            psrot[0] += 1
            return b

        TT = 512
        w1 = Buf("w1", [128, 8, 5120], BF16)
        w1res = []
        load_weight(w1, w_in, 8, 5120, 0, 1024, w1res)
        xt = [Buf("xt%d" % i, [128, 4, D], F32, lane=True) for i in range(2)]
        xnb = Buf("xnb", [128, 4, D], BF16)
        hTt = [Buf("hTt%d" % i, [128, 8, TT], BF16, lane=True) for i in range(2)]
        xr_st = [Buf("xr_st%d" % i, [128, 6, TT], F32, lane=True) for i in range(2)]
        gy_st = [Buf("gy_st%d" % i, [128, 6, TT], BF16, lane=True) for i in range(2)]
        c0_st = [Buf("c0_st%d" % i, [128, 4, TT], BF16, lane=True) for i in range(2)]
        sg = [Buf("sg%d" % i, [128, TT], F32) for i in range(2)]
        stb = Buf("stb", [128, 4, 2, 6], F32)
        mvb = Buf("mvb", [128, 4, 2], F32)
        rstd = Buf("rstd", [128, 4], F32)
        nmr = Buf("nmr", [128, 4], F32)

        xv = x.rearrange("(n s p) f -> n p s f", s=4, p=128)
        NT1 = T // TT
        def p1_prepA(it):
            xb = xt[it % 2]
            layer_norm_stats(lambda s: xb[:, s, :], 4, [xb.res], stb, mvb, rstd, nmr)
            cp("dve", rs_all[:, it * 4:it * 4 + 4], rstd[:, 0:4], [rstd.res], [rs_all.res])
            cp("dve", nm_all[:, it * 4:it * 4 + 4], nmr[:, 0:4], [nmr.res], [nm_all.res])
            for s in range(4):
                act(xnb[:, s, :], xb[:, s, :], AF.Identity, [xb.res, rstd.res, nmr.res], [xnb.res],
                    bias=nmr[:, s:s + 1], scale=rstd[:, s:s + 1])

        def p1_prepB(it):
            hp = hTt[it % 2]
            for kc in range(8):
                b = next_bank(0, 2)
                pst = bank(b).bitcast(BF16)
                for s in range(4):
                    trp(pst[:, s * 128:(s + 1) * 128], xnb[:, s, kc * 128:(kc + 1) * 128], ident_b[:, :],
                        [xnb.res, ident_b.res], [Rps[b]])
                act(hp[:, kc, :], pst[:, 0:TT], AF.Identity, [Rps[b], vecs.res], [hp.res],
                    bias=V("emb_b", kc), scale=V("emb_g", kc))
            dma("sp", hp.lane, hT_d.rearrange("(kc p) t -> p kc t", p=128)[:, :, it * TT:(it + 1) * TT],
                hp[:, :, :], R=[hp.res], WF=[R_hT])

        dma("sp", xt[0].lane, xt[0][:, :, :], xv[0], W=[xt[0].res])
        dma("sp", xt[1].lane, xt[1][:, :, :], xv[1], W=[xt[1].res])
        p1_prepA(0)
        p1_prepB(0)
        for it in range(NT1):
            hb_ = hTt[it % 2]
            if it + 1 < NT1:
                p1_prepA(it + 1)

            def proj(col0, b):
                for kc in range(8):
                    mm(bank(b), w1[:, kc, col0:col0 + 128], hb_[:, kc, :], kc == 0, kc == 7,
                       [w1res[col0 // 1024], hb_.res], [Rps[b]])
            for c in range(12):
                b = next_bank(2, 8)
                proj(c * 128, b)
                stg = xr_st[c // 6]
                ts("dve", stg[:, c % 6, :], bank(b), V("b_xr", c), None, ALU.add, None,
                   [Rps[b], vecs.res], [stg.res])
                if c % 6 == 5:
                    dma("sp", stg.lane,
                        xr_d.rearrange("(kc p) t -> p kc t", p=128)[:, c - 5:c + 1, it * TT:(it + 1) * TT],
                        stg[:, :, :], R=[stg.res], WF=[R_xr])
            if it + 1 < NT1:
                p1_prepB(it + 1)
            if it + 2 < NT1:
                nb_ = xt[it % 2]
                dma("sp", nb_.lane, nb_[:, :, :], xv[it + 2], W=[nb_.res])
            for c in range(12):
                b = next_bank(2, 8)
                proj(DR + c * 128, b)
                stg = gy_st[c // 6]
                act(stg[:, c % 6, :], bank(b), AF.Gelu_apprx_tanh, [Rps[b], vecs.res], [stg.res],
                    bias=V("b_yr", c))
                if c % 6 == 5:
                    dma("sp", stg.lane,
                        gy_d.rearrange("(kc p) t -> p kc t", p=128)[:, c - 5:c + 1, it * TT:(it + 1) * TT],
                        stg[:, :, :], R=[stg.res], WF=[R_gy])
            for j in range(8):
                ba = next_bank(2, 8)
                proj(2 * DR + j * 128, ba)
                bb = next_bank(2, 8)
                proj(2 * DR + D + j * 128, bb)
                sgb = sg[j % 2]
                act(sgb[:, :], bank(bb), AF.Sigmoid, [Rps[bb], vecs.res], [sgb.res], bias=V("b_xcb", j))
                stg = c0_st[j // 4]
                stt(stg[:, j % 4, :], bank(ba), V("b_xca", j), sgb[:, :], ALU.add, ALU.mult,
                    [Rps[ba], sgb.res, vecs.res], [stg.res])
                if j % 4 == 3:
                    dma("sp", stg.lane,
                        c0_d.rearrange("(kc p) t -> p kc t", p=128)[:, j - 3:j + 1, it * TT:(it + 1) * TT],
                        stg[:, :, :], R=[stg.res], WF=[R_c0])

        S.barrier()
        cur[0] = persist_end
        wgb = Buf("wgb", [128, 64 * BW], BF16, lane=True)
        dma("pool", wgb.lane, wgb[0:BW, :], wg_in, W=[wgb.res])
        xrs = [Buf("xrs0", [128, T + 4], F32, lane=True)]
        gyb = Buf("gyb", [128, T], BF16, lane=True)
        ubs = [Buf("ub%d" % i, [128, T], F32) for i in range(2)]
        ubfs = [Buf("ubf%d" % i, [128, T], BF16) for i in range(2)]
        TRb = [Buf("TR%d" % d, [128, T], F32) for d in range(2)]
        TIb = [Buf("TI%d" % d, [128, T], F32) for d in range(2)]
        B0s = [Buf("B0_%d" % i, [128, T], F32) for i in range(2)]
        B1 = Buf("B1", [128, T], F32)
        D4 = [Buf("D4_%d" % i, [128, 4, BW], F32) for i in range(2)]
        P = BW
        xs = xrs[0]
        S.op("pool", lambda E: E.memset(xs[:, 0:2], 0.0), [], [xs.res])
        S.op("pool", lambda E: E.memset(xs[:, T + 2:T + 4], 0.0), [], [xs.res])

        def p2_loadx(n):
            dma("sp", xs.lane, xs[0:P, 2:T + 2], xr_d[n * BW:(n + 1) * BW, :], R=[R_xr], W=[xs.res])

        def p2_conv(n, half):
            ub, ubf, d4 = ubs[n % 2], ubfs[n % 2], D4[n % 2]
            if half == 0:
                for k in range(4):
                    act(d4[0:P, k, :], ident_f[0:P, 0:P], AF.Copy, [ident_f.res, vecs.res], [d4.res],
                        scale=V("w4", n * 4 + k, 1, P))
            for q in range(2 * half, 2 * half + 2):
                b2 = 2 * (next_bank(0, 4))
                for hh in range(2):
                    t0_ = q * 1024 + hh * 512
                    for k in range(4):
                        mm(bank(b2 + hh, 512, P), d4[0:P, k, :], xs[0:P, t0_ + k:t0_ + k + 512], k == 0, k == 3,
                           [d4.res, xs.res], [Rps[b2 + hh]])
                src = ps_all[0:P, b2 * 512:b2 * 512 + 1024]
                ts("dve", ub[0:P, q * 1024:(q + 1) * 1024], src, V("b4", n, 1, P), None, ALU.add, None,
                   [Rps[b2], Rps[b2 + 1], vecs.res], [ub.res])
                ts("dve", ubf[0:P, q * 1024:(q + 1) * 1024], src, V("b4", n, 1, P), None, ALU.add, None,
                   [Rps[b2], Rps[b2 + 1], vecs.res], [ubf.res])
            if half == 1 and n + 1 < NB:
                p2_loadx(n + 1)

        def p2_gates(n, d):
            ubf = ubfs[n % 2]
            for q in range(4):
                for g in range(2):
                    b2 = 2 * (next_bank(0, 4))
                    idx = (d * 2 + g) * NB + n
                    for hh in range(2):
                        mm(bank(b2 + hh, 512, P), wgb[0:P, idx * BW:(idx + 1) * BW],
                           ubf[0:P, q * 1024 + hh * 512:q * 1024 + (hh + 1) * 512], True, True,
                           [wgb.res, ubf.res], [Rps[b2 + hh]])
                    dst = (TRb if g == 0 else TIb)[d]
                    act(dst[0:P, q * 1024:(q + 1) * 1024], ps_all[0:P, b2 * 512:b2 * 512 + 1024], AF.Tanh,
                        [Rps[b2], Rps[b2 + 1], hb.res], [dst.res], bias=hb[0:P, idx:idx + 1], scale=0.5)

        def p2_dir(n, d):
            Bd = B0s[n % 2] if d == 0 else B1
            ub = ubs[n % 2]
            hcv = hc[0:P, d * NB + n:d * NB + n + 1]
            act(TRb[d][0:P, :], TRb[d][0:P, :], AF.Exp, [TRb[d].res, hc.res], [TRb[d].res], bias=hcv, scale=hcv)
            act(Bd[0:P, :], TRb[d][0:P, :], AF.Square, [TRb[d].res], [Bd.res])
            act(Bd[0:P, :], Bd[0:P, :], AF.Sqrt, [Bd.res], [Bd.res], bias=0.25, scale=-0.25)
            stt(TIb[d][0:P, :], TIb[d][0:P, :], 1.0, ub[0:P, :], ALU.add, ALU.mult,
                [TIb[d].res, ub.res], [TIb[d].res])
            tt("dve", TIb[d][0:P, :], TIb[d][0:P, :], Bd[0:P, :], ALU.mult,
               [TIb[d].res, Bd.res], [TIb[d].res])
            f = (lambda a: a) if d == 0 else rev
            S.op("dve", (lambda o=f(Bd[0:P, 0:T]), a_=f(TRb[d][0:P, 0:T]), b_=f(TIb[d][0:P, 0:T]):
                         lambda E: E.tensor_tensor_scan(out=o, data0=a_, data1=b_, initial=0.0,
                                                        op0=ALU.mult, op1=ALU.add))(),
                 [TRb[d].res, TIb[d].res, Bd.res], [Bd.res])

        p2_loadx(0)
        p2_conv(0, 0)
        p2_conv(0, 1)
        for n in range(NB):
            dma("sp", gyb.lane, gyb[0:P, :], gy_d[n * BW:(n + 1) * BW, :], R=[R_gy], W=[gyb.res])
            p2_gates(n, 0)
            if n + 1 < NB:
                p2_conv(n + 1, 0)
            p2_dir(n, 0)
            p2_gates(n, 1)
            if n + 1 < NB:
                p2_conv(n + 1, 1)
            p2_dir(n, 1)
            tt("pool", TIb[1][0:P, :], B0s[n % 2][0:P, :], B1[0:P, :], ALU.add,
               [B0s[n % 2].res, B1.res], [TIb[1].res])
            tt("pool", gyb[0:P, :], TIb[1][0:P, :], gyb[0:P, :], ALU.mult, [TIb[1].res, gyb.res], [gyb.res])
            dma("sp", gyb.lane, v_d[n * BW:(n + 1) * BW, :], gyb[0:P, :], R=[gyb.res], WF=[R_v])

        S.barrier()
        cur[0] = persist_end
        wou = Buf("wou", [128, 8, D], BF16)
        dead_base = cur[0]
        wgl = Buf("wgl", [128, 8, 2048], BF16)
        wbb = Buf("wbb", [128, 8, D], BF16)
        wba = Buf("wba", [128, 12, D], BF16)
        wgl_r, wbb_r, wba_r, wou_r = [], [], [], []
        load_weight(wgl, w_in, 8, 2048, 5120, 1024, wgl_r)
        load_weight(wbb, w_bb, 8, D, 0, 1024, wbb_r)
        load_weight(wba, w_ba, 12, D, 0, 1024, wba_r)
        load_weight(wou, w_out, 8, D, 0, 512, wou_r)
        w3_end = cur[0]
        cxb = [Buf("cx%d" % i, [128, T + 32], BF16, lane=True) for i in range(2)]
        Dg = [Buf("Dg%d" % i, [128, 31, 128], BF16) for i in range(2)]
        c1s = [Buf("c1s%d" % i, [128, T], F32, lane=True) for i in range(2)]
        acc1 = Buf("acc1", [128, T], F32, lane=True)
        acc2 = Buf("acc2", [128, T], F32, lane=True)
        sqc = Buf("sqc", [128, T], F32)
        for i in range(2):
            S.op("pool", (lambda i=i: lambda E: E.memset(cxb[i][:, 0:15], 0.0))(), [], [cxb[i].res])
            S.op("pool", (lambda i=i: lambda E: E.memset(cxb[i][:, T + 15:T + 32], 0.0))(), [], [cxb[i].res])

        NPE = 28

        def build_diag(j):
            dg_ = Dg[j % 2]
            ia = ident_f[:, :]
            wa = V("convw", j * 31, NPE)
            pstep_i = ia.ap[0][0]
            pstep_w = wa.ap[0][0]
            in0 = AP(ia.tensor, ia.offset, [[pstep_i, 128], [0, NPE], [1, 128]])
            in1 = AP(wa.tensor, wa.offset, [[pstep_w, 128], [1, NPE], [0, 128]])
            tt("dve", dg_[:, 0:NPE, :], in0, in1, ALU.mult, [ident_f.res, vecs.res], [dg_.res])

        def p2c_load(j):
            cb_ = cxb[j % 2]
            dma("sp", cb_.lane, cb_[:, 15:T + 15], c0_d[j * 128:(j + 1) * 128, :], R=[R_c0], W=[cb_.res])
        p2c_load(0)
        build_diag(0)
        for j in range(8):
            cb_ = cxb[j % 2]
            dg = Dg[j % 2]
            co = c1s[j % 2]
            if j + 1 < 8:
                p2c_load(j + 1)
                build_diag(j + 1)
            ts("dve", co[:, :], cb_[:, NPE:NPE + T], V("convw", j * 31 + NPE), None, ALU.mult, None,
               [cb_.res, vecs.res], [co.res])
            for k in range(NPE + 1, 31):
                stt(co[:, :], cb_[:, k:k + T], V("convw", j * 31 + k), co[:, :], ALU.mult, ALU.add,
                    [cb_.res, co.res, vecs.res], [co.res])
            for tt_ in range(8):
                b = next_bank(0, 8)
                for k in range(NPE):
                    mm(bank(b), dg[:, k, :], cb_[:, tt_ * 512 + k:tt_ * 512 + k + 512], k == 0, k == NPE - 1,
                       [dg.res, cb_.res], [Rps[b]])
                tsl = slice(tt_ * 512, (tt_ + 1) * 512)
                stt(co[:, tsl], bank(b), V("conv_b", j), co[:, tsl], ALU.add, ALU.add,
                    [Rps[b], co.res, vecs.res], [co.res])
            dma("sp", co.lane, c1_d[j * 128:(j + 1) * 128, :], co[:, :], R=[co.res], WF=[R_c1])
            if j == 0:
                cp("pool", acc1[:, :], co[:, :], [co.res], [acc1.res])
                act(acc2[:, :], co[:, :], AF.Square, [co.res], [acc2.res])
            else:
                tt("pool", acc1[:, :], acc1[:, :], co[:, :], ALU.add, [acc1.res, co.res], [acc1.res])
                act(sqc[:, :], co[:, :], AF.Square, [co.res], [sqc.res])
                tt("pool", acc2[:, :], acc2[:, :], sqc[:, :], ALU.add, [acc2.res, sqc.res], [acc2.res])
        for tt_ in range(8):
            tsl = slice(tt_ * 512, (tt_ + 1) * 512)
            b1_ = next_bank(0, 8)
            b2_ = next_bank(0, 8)
            mm(bank(b1_), ones_f[:, :], acc1[:, tsl], True, True, [ones_f.res, acc1.res], [Rps[b1_]])
            mm(bank(b2_), ones_f[:, :], acc2[:, tsl], True, True, [ones_f.res, acc2.res], [Rps[b2_]])
            ts("dve", sqc[:, tsl], bank(b1_), 1.0 / D, None, ALU.mult, None, [Rps[b1_], acc1.res], [sqc.res])
            tt("dve", c1s[1][:, tsl], sqc[:, tsl], sqc[:, tsl], ALU.mult, [sqc.res], [c1s[1].res])
            stt(c1s[0][:, tsl], bank(b2_), 1.0 / D, c1s[1][:, tsl], ALU.mult, ALU.subtract,
                [Rps[b2_], c1s[1].res, acc2.res], [c1s[0].res])
        act(c1s[0][:, :], c1s[0][:, :], AF.Sqrt, [c1s[0].res], [c1s[0].res], bias=EPS)
        S.op("dve", lambda E: E.reciprocal(out=c1s[0][:, :], in_=c1s[0][:, :]), [c1s[0].res], [c1s[0].res])
        dma("sp", acc1.lane, cst_d[0], sqc[:, :], R=[sqc.res], WF=[R_cst])
        dma("sp", acc2.lane, cst_d[1], c1s[0][:, :], R=[c1s[0].res], WF=[R_cst])

        S.barrier()
        cur[0] = w3_end
        TT3 = 256
        NS = TT3 // 128
        hT3s = [Buf("hT3_%d" % i, [128, 8, TT3], BF16, lane=True) for i in range(2)]
        v3s = [Buf("v3_%d" % i, [128, 12, TT3], BF16, lane=True) for i in range(2)]
        c13 = Buf("c13", [128, 8, TT3], F32, lane=True)
        mean3 = Buf("mean3", [128, TT3], F32, lane=True)
        rstd3 = Buf("rstd3", [128, TT3], F32, lane=True)
        xnt = [Buf("xnt%d" % i, [128, TT3], F32) for i in range(2)]
        yt = [Buf("yt%d" % i, [128, TT3], F32) for i in range(2)]
        tht = [Buf("tht%d" % i, [128, TT3], F32) for i in range(2)]
        cBs = [Buf("cB%d" % i, [128, 8, TT3], BF16) for i in range(2)]
        t0b = [Buf("t0_%d" % i, [128, TT3], F32) for i in range(2)]
        t1b = [Buf("t1_%d" % i, [128, TT3], F32) for i in range(2)]
        u1b = [Buf("u1_%d" % i, [128, TT3], F32) for i in range(2)]
        u2b = [Buf("u2_%d" % i, [128, TT3], F32) for i in range(2)]
        zTs = [Buf("zT%d" % i, [128, 8, TT3], BF16) for i in range(2)]
        assert cur[0] - 8 * TT3 * 2 == dead_base + 2 * 8 * DFF * 2, (cur[0], dead_base)
        rows = {}
        for nm in ("G1", "B1"):
            rows[nm] = Buf("row_" + nm, [128, D], F32, lane=True)
        rtmp = Buf("xn0", [128, D], F32, lane=True)
        ri = {n: i for i, n in enumerate(ROWS)}

        def make_rows(gn, bn, gsrc, bsrc, addsrc):
            dma("sp", rows[gn].lane, rows[gn][:, :], rows_in[ri[gsrc]], W=[rows[gn].res])
            ts("dve", rows[gn][:, :], rows[gn][:, :], ALPHA, None, ALU.mult, None, [rows[gn].res], [rows[gn].res])
            dma("sp", rows[bn].lane, rows[bn][:, :], rows_in[ri[bsrc]], W=[rows[bn].res])
            dma("sp", rtmp.lane, rtmp[:, :], rows_in[ri[addsrc]], W=[rtmp.res])
            stt(rows[bn][:, :], rows[bn][:, :], ALPHA, rtmp[:, :], ALU.mult, ALU.add,
                [rows[bn].res, rtmp.res], [rows[bn].res])

        halfv = Buf("halfv", [128, 32], F32)
        ts("dve", halfv[:, 0:16], V("cln_g", 0, 16), 0.5, None, ALU.mult, None, [vecs.res], [halfv.res])
        ts("dve", halfv[:, 16:32], V("b_g0", 0, 16), 0.5, None, ALU.mult, None, [vecs.res], [halfv.res])
        mhalf = Buf("mhalf", [128, TT3], F32)
        S.op("pool", lambda E: E.memset(mhalf[:, :], -0.5), [], [mhalf.res])
        xC = Buf("xC", [128, NS, D], F32, lane=True)
        xn0 = rtmp
        tbss = [Buf("tbs%d" % i, [128, D], F32) for i in range(2)]
        xn1 = [Buf("xn1_%d" % i, [128, D], F32, lane=True) for i in range(2)]
        xn1bs = [Buf("xn1b%d" % i, [128, NS, D], BF16) for i in range(2)]
        h1T = [Buf("h1T%d" % i, [128, 8, TT3], BF16, lane=True) for i in range(2)]
        stb4 = Buf("stb4", [128, 4, 2, 6], F32)
        mvb4 = Buf("mvb4", [128, 4, 2], F32)
        vt4 = Buf("vt4", [128, 4], F32)
        rs1 = Buf("rs1", [128, 4], F32)
        nm1 = Buf("nm1", [128, 4], F32)

        xv3 = x.rearrange("(n s p) f -> n p s f", s=NS, p=128)
        hTv = hT_d.rearrange("(kc p) t -> p kc t", p=128)
        vv = v_d.rearrange("(kc p) t -> p kc t", p=128)
        c1v = c1_d.rearrange("(kc p) t -> p kc t", p=128)
        h1Tv = h1T_d.rearrange("(kc p) t -> p kc t", p=128)
        NT3 = T // TT3
        BS, BQ = 4, 5

        def rstd_pow(dst, var_ap, tmp, n, Rin, Rtmp, Rdst):
            ts("dve", tmp, var_ap, EPS, None, ALU.add, None, Rin, [Rtmp])
            tt("pool", dst, tmp, mhalf[:, 0:n], ALU.pow, [Rtmp, mhalf.res], [Rdst])

        def bcast_row(row, it):
            return cst_d[row][:, it * TT3:(it + 1) * TT3]

        def load_A(it):
            tsl = slice(it * TT3, (it + 1) * TT3)
            dma("sp", c13.lane, c13[:, :, :], c1v[:, :, tsl], R=[R_c1], W=[c13.res])
            dma("sp", mean3.lane, mean3[:, :], bcast_row(0, it), R=[R_cst], W=[mean3.res])
            dma("sp", rstd3.lane, rstd3[:, :], bcast_row(1, it), R=[R_cst], W=[rstd3.res])

        def load_B(it):
            tsl = slice(it * TT3, (it + 1) * TT3)
            h_, v_ = hT3s[it % 2], v3s[it % 2]
            dma("sp", h_.lane, h_[:, :, :], hTv[:, :, tsl], R=[R_hT], W=[h_.res])
            dma("sp", v_.lane, v_[:, :, :], vv[:, :, tsl], R=[R_v], W=[v_.res])

        def stage_A(it):
            cB = cBs[it % 2]
            for i in range(8 + 2):
                if i < 8:
                    kc = i
                    xb_ = xnt[kc % 2]
                    tt("dve", xb_[:, :], c13[:, kc, :], mean3[:, :], ALU.subtract, [c13.res, mean3.res], [xb_.res])
                    tt("dve", xb_[:, :], xb_[:, :], rstd3[:, :], ALU.mult, [xb_.res, rstd3.res], [xb_.res])
                if 1 <= i < 9:
                    kc = i - 1
                    xb_, yb_, th_ = xnt[kc % 2], yt[kc % 2], tht[kc % 2]
                    act(th_[:, :], xb_[:, :], AF.Tanh, [xb_.res, halfv.res], [th_.res],
                        bias=halfv[:, 8 + kc:9 + kc], scale=halfv[:, kc:kc + 1])
                    ts("dve", yb_[:, :], xb_[:, :], V("cln_g", kc), V("cln_b", kc), ALU.mult, ALU.add,
                       [xb_.res, vecs.res], [yb_.res])
                if 2 <= i:
                    kc = i - 2
                    yb_, th_ = yt[kc % 2], tht[kc % 2]
                    stt(cB[:, kc, :], th_[:, :], 1.0, yb_[:, :], ALU.add, ALU.mult, [th_.res, yb_.res], [cB.res])
                yield
            if it + 1 < NT3:
                load_A(it + 1)

        def stage_B(it):
            cB, zT = cBs[it % 2], zTs[it % 2]
            hT3, v3 = hT3s[it % 2], v3s[it % 2]
            if it + 1 < NT3:
                load_B(it + 1)

            def merge(oc):
                Y = 2 * (oc % 2) + 1
                RY = Rps[Y]
                s0, s1, u1, u2 = t0b[oc % 2], t1b[oc % 2], u1b[oc % 2], u2b[oc % 2]
                stt(u1[:, :], s0[:, :], 1.0, bank(Y)[:, 0:TT3], ALU.add, ALU.mult, [s0.res, RY], [u1.res])
                stt(u2[:, :], s1[:, :], 1.0, bank(Y)[:, TT3:2 * TT3], ALU.add, ALU.mult, [s1.res, RY], [u2.res])
                stt(zT[:, oc, :], u2[:, :], 0.5, u1[:, :], ALU.mult, ALU.add, [u1.res, u2.res], [zT.res])

            for oc in range(8):
                X = 2 * (oc % 2)
                Y = X + 1
                RX, RY = Rps[X], Rps[Y]
                for kc in range(8):
                    mm(bank(X)[:, 0:TT3], wgl[:, kc, oc * 128:(oc + 1) * 128], hT3[:, kc, :], kc == 0, kc == 7,
                       [wgl_r[0], hT3.res], [RX])
                for kc in range(8):
                    mm(bank(X)[:, TT3:2 * TT3], wgl[:, kc, D + oc * 128:D + (oc + 1) * 128], hT3[:, kc, :],
                       kc == 0, kc == 7, [wgl_r[1], hT3.res], [RX])
                for kc in range(12):
                    mm(bank(Y)[:, 0:TT3], wba[:, kc, oc * 128:(oc + 1) * 128], v3[:, kc, :], kc == 0, kc == 11,
                       [wba_r[0], v3.res], [RY])
                for kc in range(8):
                    mm(bank(Y)[:, TT3:2 * TT3], wbb[:, kc, oc * 128:(oc + 1) * 128], cB[:, kc, :], kc == 0, kc == 7,
                       [wbb_r[0], cB.res], [RY])
                s0, s1 = t0b[oc % 2], t1b[oc % 2]
                act(s0[:, :], bank(X)[:, 0:TT3], AF.Tanh, [RX, halfv.res], [s0.res],
                    bias=halfv[:, 16 + oc:17 + oc], scale=0.5)
                act(s1[:, :], bank(X)[:, TT3:2 * TT3], AF.Tanh, [RX, halfv.res], [s1.res],
                    bias=halfv[:, 24 + oc:25 + oc], scale=0.5)
                if oc >= 1:
                    merge(oc - 1)
                yield
            merge(7)
            yield

        def stage_C(it, use_pool=True):
            zT = zTs[it % 2]
            xn1b = xn1bs[it % 2]
            meng = "pool" if use_pool else "dve"
            for s in range(NS):
                tbs = tbss[s]
                col = it * NS + s
                act(xn0[:, :], xC[:, s, :], AF.Identity, [xC.res, rs_all.res, nm_all.res], [xn0.res],
                    bias=nm_all[:, col:col + 1], scale=rs_all[:, col:col + 1])
                tt(meng, tbs[:, :], xn0[:, :], rows["G1"][:, :], ALU.mult, [xn0.res, rows["G1"].res], [tbs.res])
                tt("dve", tbs[:, :], tbs[:, :], rows["B1"][:, :], ALU.add, [tbs.res, rows["B1"].res], [tbs.res])
                yield
            if it + 1 < NT3:
                dma("sp", xC.lane, xC[:, :, :], xv3[it + 1], W=[xC.res])
            for s in range(NS):
                tbs = tbss[s]
                for hh in range(2):
                    b = 4 + 2 * s + hh
                    for kc in range(8):
                        mm(bank(b), zT[:, kc, s * 128:(s + 1) * 128], wou[:, kc, hh * 512:(hh + 1) * 512],
                           kc == 0, kc == 7, [zT.res, wou_r[hh]], [Rps[b]])
                    stt(tbs[:, hh * 512:(hh + 1) * 512], bank(b), 0.5, tbs[:, hh * 512:(hh + 1) * 512],
                        ALU.mult, ALU.add, [tbs.res, Rps[b]], [tbs.res])
                for hh in range(2):
                    S.op("dve", (lambda o=stb4[:, s, hh, :], i=tbs[:, hh * 512:(hh + 1) * 512]:
                                 lambda E: E.bn_stats(out=o, in_=i))(), [tbs.res], [stb4.res])
                S.op("dve", (lambda o=mvb4[:, s, :], i=stb4[:, s, :, :].rearrange("p a b -> p (a b)"):
                             lambda E: E.bn_aggr(out=o, in_=i))(), [stb4.res], [mvb4.res])
                yield
            if use_pool:
                rstd_pow(rs1[:, 0:NS], mvb4[:, 0:NS, 1], vt4[:, 0:NS], NS, [mvb4.res], vt4.res, rs1.res)
            else:
                act(rs1[:, 0:NS], mvb4[:, 0:NS, 1], AF.Sqrt, [mvb4.res], [rs1.res], bias=EPS)
                S.op("dve", lambda E: E.reciprocal(out=rs1[:, 0:NS], in_=rs1[:, 0:NS]), [rs1.res], [rs1.res])
            stt(nm1[:, 0:NS], mvb4[:, 0:NS, 0], -1.0, rs1[:, 0:NS], ALU.mult, ALU.mult,
                [mvb4.res, rs1.res], [nm1.res])
            yield
            for s in range(NS):
                tbs = tbss[s]
                x1 = xn1[s % 2]
                act(xn1b[:, s, :], tbs[:, :], AF.Identity, [tbs.res, rs1.res, nm1.res], [xn1b.res],
                    bias=nm1[:, s:s + 1], scale=rs1[:, s:s + 1])
                act(x1[:, :], tbs[:, :], AF.Identity, [tbs.res, rs1.res, nm1.res], [x1.res],
                    bias=nm1[:, s:s + 1], scale=rs1[:, s:s + 1])
                dma("sp", x1.lane, xn1_d[it * TT3 + s * 128:it * TT3 + (s + 1) * 128, :], x1[:, :],
                    R=[x1.res], WF=[R_xn1])
                yield

        def stage_D(it):
            xn1b = xn1bs[it % 2]
            ho = h1T[it % 2]
            for kc in range(8):
                b = 4 + kc % 4
                pst = bank(b).bitcast(BF16)
                for s in range(NS):
                    trp(pst[:, s * 128:(s + 1) * 128], xn1b[:, s, kc * 128:(kc + 1) * 128], ident_b[:, :],
                        [xn1b.res, ident_b.res], [Rps[b]])
                act(ho[:, kc, :], pst[:, 0:TT3], AF.Identity, [Rps[b], vecs.res], [ho.res],
                    bias=V("ln1_b", kc), scale=V("ln1_g", kc))
                if kc % 2 == 1:
                    yield
            dma("sp", ho.lane, h1Tv[:, :, it * TT3:(it + 1) * TT3], ho[:, :, :], R=[ho.res], WF=[R_h1T])

        def drive(primary, others):
            live = [g for g in others if g is not None]
            for _ in primary:
                for g in list(live):
                    for _k in range(2):
                        try:
                            next(g)
                        except StopIteration:
                            if g in live:
                                live.remove(g)
                            break
            for g in live:
                for _ in g:
                    pass

        load_A(0)
        load_B(0)
        dma("sp", xC.lane, xC[:, :, :], xv3[0], W=[xC.res])
        make_rows("G1", "B1", "emb_g", "emb_b", "b_out")
        for _ in stage_A(0):
            pass
        for it in range(NT3):
            gA = stage_A(it + 1) if it + 1 < NT3 else None
            gC = stage_C(it - 1) if it >= 1 else None
            gD = stage_D(it - 2) if it >= 2 else None
            drive(stage_B(it), [gC, gD, gA])
        assert (NT3 - 1) % 2 == 1
        p3b_mark = cur[0]
        cur[0] = dead_base
        wup = Buf("wup", [128, 8, DFF], BF16)
        wdn = Buf("wdn", [128, 32, D], BF16)
        ffn_w_end = cur[0]
        cur[0] = p3b_mark
        dead = [wgl_r[0], wgl_r[1], wbb_r[0], wba_r[0], c13.res, mean3.res, rstd3.res, zTs[0].res]
        for lst in (hT3s, v3s, xnt, yt, tht, cBs, t0b, t1b, u1b, u2b):
            dead += [b_.res for b_ in lst]
        wup_r, wdn_r = [], []
        srcv_up = w_up.rearrange("(kc p) c -> p kc c", p=128)
        for cb in range(4):
            r = Res("wupblk")
            wup_r.append(r)
            dma("pool", S.dma_lane(), wup[:, 0:8, cb * 1024:(cb + 1) * 1024],
                srcv_up[:, :, cb * 1024:(cb + 1) * 1024], W=[r] + dead)
        srcv_dn = w_dn.rearrange("(kc p) c -> p kc c", p=128)
        for cb in range(2):
            r = Res("wdnblk")
            wdn_r.append(r)
            dma("pool", S.dma_lane(), wdn[:, 0:32, cb * 512:(cb + 1) * 512],
                srcv_dn[:, :, cb * 512:(cb + 1) * 512], W=[r] + dead)
        drive(stage_C(NT3 - 1, use_pool=False), [stage_D(NT3 - 2)])
        for _ in stage_D(NT3 - 1):
            pass

        S.barrier()
        cur[0] = persist_end
        rows = {}
        for nm in ("G2", "B2", "ln2_g", "ln2_b"):
            rows[nm] = Buf("row_" + nm, [128, D], F32, lane=True)
        assert cur[0] <= dead_base
        cur[0] = ffn_w_end
        rtmp = Buf("rtmp2", [128, D], F32, lane=True)
        h4 = [Buf("h4_%d" % i, [128, 8, TT3], BF16, lane=True) for i in range(2)]
        x4 = [Buf("x4_%d" % i, [128, NS, D], F32, lane=True) for i in range(1)]
        mT = Buf("mT", [128, 32, TT3], BF16)
        rl = [Buf("rl%d" % i, [128, TT3], F32) for i in range(2)]
        t4 = [Buf("t4_%d" % i, [128, D], F32) for i in range(1)]
        o4 = [Buf("o4_%d" % i, [128, D], F32, lane=True) for i in range(2)]
        stb5 = Buf("stb5", [128, 4, 2, 6], F32)
        mvb5 = Buf("mvb5", [128, 4, 2], F32)
        rs2 = Buf("rs2", [128, 4], F32)
        nm2 = Buf("nm2", [128, 4], F32)
        xn1v = xn1_d.rearrange("(n s p) f -> n p s f", s=NS, p=128)

        def p4_load_early(it):
            sl = it % 2
            dma("sp", h4[sl].lane, h4[sl][:, :, :], h1Tv[:, :, it * TT3:(it + 1) * TT3], R=[R_h1T], W=[h4[sl].res])

        def p4_load_late(it):
            dma("sp", x4[0].lane, x4[0][:, :, :], xn1v[it], R=[R_xn1], W=[x4[0].res])
        p4_load_early(0)
        p4_load_late(0)
        make_rows("G2", "B2", "ln1_g", "ln1_b", "b_down")
        dma("sp", rows["ln2_g"].lane, rows["ln2_g"][:, :], rows_in[ri["ln2_g"]], W=[rows["ln2_g"].res])
        dma("sp", rows["ln2_b"].lane, rows["ln2_b"][:, :], rows_in[ri["ln2_b"]], W=[rows["ln2_b"].res])
        oi = 0
        for it in range(NT3):
            sl = it % 2
            if it + 1 < NT3:
                p4_load_early(it + 1)
            ht, xt4 = h4[sl], x4[0]
            for fc in range(32):
                b = next_bank(0, 8)
                for kc in range(8):
                    mm(bank(b, TT3), wup[:, kc, fc * 128:(fc + 1) * 128], ht[:, kc, :], kc == 0, kc == 7,
                       [wup_r[(fc * 128) // 1024], ht.res], [Rps[b]])
                r_ = rl[fc % 2]
                ts("dve", r_[:, :], bank(b, TT3), V("b_up", fc), 0.0, ALU.add, ALU.max, [Rps[b], vecs.res], [r_.res])
                act(mT[:, fc, :], r_[:, :], AF.Square, [r_.res], [mT.res])
            for s in range(NS):
                tbb = t4[0]
                tt("pool", tbb[:, :], xt4[:, s, :], rows["G2"][:, :], ALU.mult, [xt4.res, rows["G2"].res], [tbb.res])
                tt("pool", tbb[:, :], tbb[:, :], rows["B2"][:, :], ALU.add, [tbb.res, rows["B2"].res], [tbb.res])
                for hh in range(2):
                    b = next_bank(0, 8)
                    for fc in range(32):
                        mm(bank(b), mT[:, fc, s * 128:(s + 1) * 128], wdn[:, fc, hh * 512:(hh + 1) * 512],
                           fc == 0, fc == 31, [mT.res, wdn_r[hh]], [Rps[b]])
                    tt("dve", tbb[:, hh * 512:(hh + 1) * 512], tbb[:, hh * 512:(hh + 1) * 512], bank(b), ALU.add,
                       [tbb.res, Rps[b]], [tbb.res])
                for hh in range(2):
                    S.op("dve", (lambda s=s, hh=hh, tbb=tbb: lambda E: E.bn_stats(
                        out=stb5[:, s, hh, :], in_=tbb[:, hh * 512:(hh + 1) * 512]))(), [tbb.res], [stb5.res])
                S.op("dve", (lambda s=s: lambda E: E.bn_aggr(
                    out=mvb5[:, s, :], in_=stb5[:, s, :, :].rearrange("p a b -> p (a b)")))(), [stb5.res], [mvb5.res])
                act(rs2[:, s:s + 1], mvb5[:, s, 1:2], AF.Sqrt, [mvb5.res], [rs2.res], bias=EPS)
                S.op("dve", (lambda s=s: lambda E: E.reciprocal(out=rs2[:, s:s + 1], in_=rs2[:, s:s + 1]))(),
                     [rs2.res], [rs2.res])
                stt(nm2[:, s:s + 1], mvb5[:, s, 0:1], -1.0, rs2[:, s:s + 1], ALU.mult, ALU.mult,
                    [mvb5.res, rs2.res], [nm2.res])
                ob = o4[oi % 2]
                oi += 1
                act(ob[:, :], tbb[:, :], AF.Identity, [tbb.res, rs2.res, nm2.res], [ob.res],
                    bias=nm2[:, s:s + 1], scale=rs2[:, s:s + 1])
                tt("pool", ob[:, :], ob[:, :], rows["ln2_g"][:, :], ALU.mult, [ob.res, rows["ln2_g"].res], [ob.res])
                tt("dve", ob[:, :], ob[:, :], rows["ln2_b"][:, :], ALU.add, [ob.res, rows["ln2_b"].res], [ob.res])
                dma("sp", ob.lane, out[it * TT3 + s * 128:it * TT3 + (s + 1) * 128, :], ob[:, :],
                    R=[ob.res], WF=[R_out])
            if it + 1 < NT3:
                p4_load_late(it + 1)
        S.barrier()
        S.run()
    return nc


_NC_CACHE = {}


def _cols(v, p):
    v = np.asarray(v, np.float32).reshape(-1, p).T
    o = np.zeros((128, v.shape[1]), np.float32)
    o[:p] = v
    return o


def kernel(x, emb_ln_g, emb_ln_b, w_in, b_in, rnn_conv_w, rnn_conv_b, rg_gate_w, rg_gate_b,
           rg_a_param, w_branch_a, conv_w, conv_b, conv_ln_g, conv_ln_b, w_branch_b,
           w_out, b_out, ln1_g, ln1_b, w_up, b_up, w_down, b_down, ln2_g, ln2_b):
    f = lambda a: np.ascontiguousarray(np.asarray(a, np.float32))
    x = f(x)
    b_in0 = f(b_in)[0]
    parts = {
        "b_xr": _cols(b_in0[0:DR], 128), "b_yr": _cols(b_in0[DR:2 * DR], 128),
        "b_xca": _cols(b_in0[2 * DR:2 * DR + D], 128), "b_xcb": _cols(b_in0[2 * DR + D:2 * DR + 2 * D], 128),
        "b_g0": _cols(b_in0[5120:5120 + D], 128), "b_g1": _cols(b_in0[5120 + D:], 128),
        "emb_g": _cols(emb_ln_g, 128), "emb_b": _cols(emb_ln_b, 128),
        "conv_b": _cols(f(conv_b)[0], 128), "cln_g": _cols(f(conv_ln_g)[0], 128), "cln_b": _cols(f(conv_ln_b)[0], 128),
        "ln1_g": _cols(f(ln1_g)[0], 128), "ln1_b": _cols(f(ln1_b)[0], 128), "b_up": _cols(f(b_up)[0], 128),
        "convw": np.ascontiguousarray(f(conv_w)[0].reshape(31, 8, 128).transpose(2, 1, 0)).reshape(128, 8 * 31),
    }
    w4 = np.zeros((128, 64), np.float32)
    w4[:BW] = f(rnn_conv_w)[0].reshape(4, NB, BW).transpose(2, 1, 0).reshape(BW, 64)
    parts["w4"] = w4
    parts["b4"] = _cols(f(rnn_conv_b)[0], BW)
    gb = np.zeros((128, 64), np.float32)
    gb[:BW] = f(rg_gate_b)[0].reshape(64, BW).T
    parts["gb"] = gb
    ap_ = np.zeros((128, 32), np.float32)
    ap_[:BW] = f(rg_a_param)[0].reshape(2 * NB, BW).T
    parts["ap"] = ap_
    vecs = np.zeros((128, NV), np.float32)
    for nm, off in _V.items():
        a = parts[nm]
        vecs[:, off:off + a.shape[1]] = a
    rowsrc = {"emb_g": emb_ln_g, "emb_b": emb_ln_b, "b_out": f(b_out)[0], "ln1_g": f(ln1_g)[0],
              "ln1_b": f(ln1_b)[0], "b_down": f(b_down)[0], "ln2_g": f(ln2_g)[0], "ln2_b": f(ln2_b)[0]}
    rows = np.ascontiguousarray(np.stack(
        [np.broadcast_to(f(rowsrc[n]).reshape(1, D), (128, D)) for n in ROWS]))
    wg = np.ascontiguousarray(f(rg_gate_w)[0].reshape(64, BW, BW).transpose(1, 0, 2)).reshape(BW, 64 * BW)
    shared = {
        "w_in": f(w_in)[0], "w_ba": f(w_branch_a)[0], "w_bb": f(w_branch_b)[0], "w_out": f(w_out)[0],
        "w_up": f(w_up)[0], "w_dn": f(w_down)[0], "wg": wg, "vecs": vecs, "rows": rows,
        "ident": np.eye(128, dtype=np.float32),
    }
    if "nc" not in _NC_CACHE:
        _NC_CACHE["nc"] = build_nc()
    nc = _NC_CACHE["nc"]
    in_maps = [dict(shared, x=x[b]) for b in range(8)]
    res = run_bass_kernel_spmd(nc, in_maps, core_ids=list(range(8)))
    return np.stack([np.asarray(r["out"], np.float32) for r in res.results], axis=0)
```

```python
import contextlib
import numpy as np
import concourse.bass as bass
import concourse.mybir as mybir
from concourse.ap import AP
from concourse.bass_utils import run_bass_kernel_spmd

F32 = mybir.dt.float32
BF16 = mybir.dt.bfloat16
AF = mybir.ActivationFunctionType
ALU = mybir.AluOpType

T = 4096
D = 1024
DR = 1536
DIN = 7168
DFF = 4096
NB = 16
BW = 96
ALPHA = 2.0 ** 0.25
EPS = 1e-5
SAME_ENGINE_SYNC = ("pool", "dve", "act")


class Res:
    __slots__ = ("name", "w", "r")

    def __init__(self, name):
        self.name = name
        self.w = {}
        self.r = {}


class Sched:
    ENGS = ("pe", "act", "dve", "pool", "sp")

    def __init__(self, nc, stack):
        self.nc = nc
        self.stack = stack
        self.q = {e: [] for e in self.ENGS}
        self.lane_sem = {}
        self.lane_cnt = {}
        self.seen = {e: {} for e in self.ENGS}
        for e in self.ENGS:
            self.new_lane(e)
        self.n_dma_lanes = 0

    def new_lane(self, name):
        sem = self.stack.enter_context(self.nc.semaphore("s_" + name))
        self.lane_sem[name] = sem
        self.lane_cnt[name] = 0
        return name

    def dma_lane(self):
        self.n_dma_lanes += 1
        return self.new_lane("d%d" % self.n_dma_lanes)

    def _deps(self, eng, reads, writes, force=False):
        deps = {}

        def need(lane, v):
            if deps.get(lane, 0) < v:
                deps[lane] = v
        for b in reads:
            for lane, v in b.w.items():
                need(lane, v)
        for b in writes:
            for lane, v in b.w.items():
                need(lane, v)
            for lane, v in b.r.items():
                need(lane, v)
        waits = []
        for lane, v in deps.items():
            if lane == eng and eng not in SAME_ENGINE_SYNC and not force:
                continue
            if self.seen[eng].get(lane, 0) >= v:
                continue
            self.seen[eng][lane] = v
            waits.append((self.lane_sem[lane], v))
        return waits

    @staticmethod
    def _commit(ticket, reads, writes):
        lane, v = ticket
        for b in reads:
            if b.r.get(lane, 0) < v:
                b.r[lane] = v
        for b in writes:
            if b.w.get(lane, 0) < v:
                b.w[lane] = v

    def op(self, eng, fn, reads=(), writes=(), self_sync=False):
        waits = self._deps(eng, reads, writes, self_sync)
        self.lane_cnt[eng] += 1
        ticket = (eng, self.lane_cnt[eng])
        sem = self.lane_sem[eng]

        def emit(E, fn=fn, waits=waits, sem=sem):
            for s, v in waits:
                E.wait_ge(s, v)
            fn(E).then_inc(sem, 1)
        self.q[eng].append(emit)
        self._commit(ticket, reads, writes)

    def dma(self, eng, lane, fn, reads=(), writes=(), writes_free=()):
        waits = self._deps(eng, reads, writes)
        self.lane_cnt[lane] += 16
        ticket = (lane, self.lane_cnt[lane])
        sem = self.lane_sem[lane]

        def emit(E, fn=fn, waits=waits, sem=sem):
            for s, v in waits:
                E.wait_ge(s, v)
            fn(E).then_inc(sem, 16)
        self.q[eng].append(emit)
        self._commit(ticket, reads, tuple(writes) + tuple(writes_free))

    def barrier(self, engs=None):
        for eng in (engs or self.ENGS):
            waits = []
            for lane, v in self.lane_cnt.items():
                if v == 0 or lane == eng:
                    continue
                if self.seen[eng].get(lane, 0) >= v:
                    continue
                self.seen[eng][lane] = v
                waits.append((self.lane_sem[lane], v))

            def emit(E, waits=waits):
                for s, v in waits:
                    E.wait_ge(s, v)
            self.q[eng].append(emit)

    def run(self):
        with self.nc.Block() as block:
            @block.tensor
            def _(E):
                for f in self.q["pe"]:
                    f(E)

            @block.scalar
            def _(E):
                for f in self.q["act"]:
                    f(E)

            @block.vector
            def _(E):
                for f in self.q["dve"]:
                    f(E)

            @block.gpsimd
            def _(E):
                for f in self.q["pool"]:
                    f(E)

            @block.sync
            def _(E):
                for f in self.q["sp"]:
                    f(E)


def rev(ap):
    steps = [list(x) for x in ap.ap]
    st, cnt = steps[-1]
    off = ap.offset + st * (cnt - 1)
    steps[-1] = [-st, cnt]
    return AP(ap.tensor, off, steps)


_V = {}
_off = 0
for _nm, _n in [("b_xr", 12), ("b_yr", 12), ("b_xca", 8), ("b_xcb", 8), ("b_g0", 8), ("b_g1", 8),
                ("emb_g", 8), ("emb_b", 8), ("conv_b", 8), ("cln_g", 8), ("cln_b", 8),
                ("ln1_g", 8), ("ln1_b", 8), ("b_up", 32), ("convw", 8 * 31),
                ("w4", 64), ("b4", 16), ("gb", 64), ("ap", 32)]:
    _V[_nm] = _off
    _off += _n
NV = _off
ROWS = ["emb_g", "emb_b", "b_out", "ln1_g", "ln1_b", "b_down", "ln2_g", "ln2_b"]


def build_nc(debug=False):
    nc = bass.Bass("TRN2", target_bir_lowering=False)
    dt_in = lambda nm, shp: nc.dram_tensor(nm, shp, F32, kind="ExternalInput").ap()
    x = dt_in("x", [T, D])
    w_in = dt_in("w_in", [D, DIN])
    w_ba = dt_in("w_ba", [DR, D])
    w_bb = dt_in("w_bb", [D, D])
    w_out = dt_in("w_out", [D, D])
    w_up = dt_in("w_up", [D, DFF])
    w_dn = dt_in("w_dn", [DFF, D])
    wg_in = dt_in("wg", [BW, 64 * BW])
    vecs_in = dt_in("vecs", [128, NV])
    rows_in = dt_in("rows", [8, 128, D])
    ident_in = dt_in("ident", [128, 128])
    out = nc.dram_tensor("out", [T, D], F32, kind="ExternalOutput").ap()
    kw = {"kind": "ExternalOutput"} if debug else {}
    hT_d = nc.dram_tensor("hT_d", [D, T], BF16, **kw).ap()
    xr_d = nc.dram_tensor("xr_d", [DR, T], F32, **kw).ap()
    gy_d = nc.dram_tensor("gy_d", [DR, T], BF16, **kw).ap()
    c0_d = nc.dram_tensor("c0_d", [D, T], BF16, **kw).ap()
    c1_d = nc.dram_tensor("c1_d", [D, T], F32, **kw).ap()
    v_d = nc.dram_tensor("v_d", [DR, T], BF16, **kw).ap()
    xn1_d = nc.dram_tensor("xn1_d", [T, D], F32, **kw).ap()
    h1T_d = nc.dram_tensor("h1T_d", [D, T], BF16, **kw).ap()
    wup_bf = nc.dram_tensor("wup_bf", [D, DFF], BF16).ap()
    wdn_bf = nc.dram_tensor("wdn_bf", [DFF, D], BF16).ap()
    cst_d = nc.dram_tensor("cst_d", [2, 128, T], F32, **kw).ap()

    with contextlib.ExitStack() as st:
        S = Sched(nc, st)
        ARENA_F32 = 53000
        arena = st.enter_context(nc.sbuf_tensor("arena", [128, ARENA_F32], F32))
        base = nc.lookup_mloc(arena).addr
        ps_all = st.enter_context(nc.psum_tensor("ps_all", [128, 4096], F32))
        Rps = [Res("ps%d" % i) for i in range(8)]

        def bank(i, n=512, p=128):
            return ps_all[0:p, i * 512:i * 512 + n]

        cur = [0]
        cnt = [0]

        class Buf:
            def __init__(self, name, shape, dt, lane=False):
                nbytes = int(np.prod(shape[1:])) * (4 if dt == F32 else 2)
                nbytes = (nbytes + 63) // 64 * 64
                assert cur[0] + nbytes <= ARENA_F32 * 4, (name, cur[0], nbytes)
                cnt[0] += 1
                self.t = nc.alloc_sbuf_tensor_at("%s_%d" % (name, cnt[0]), list(shape), dt,
                                                 offset=base + cur[0])
                cur[0] += nbytes
                self.res = Res(name)
                self.lane = S.dma_lane() if lane else None

            def __getitem__(self, k):
                return self.t[k]

        def mm(o, l, r, start, stop, R, W):
            S.op("pe", lambda E: E.matmul(o, l, r, start=start, stop=stop), R, W)

        def trp(o, i, idn, R, W):
            S.op("pe", lambda E: E.transpose(o, i, idn), R, W)

        def act(o, i, func, R, W, bias=0.0, scale=1.0, self_sync=False):
            S.op("act", lambda E: E.activation(out=o, in_=i, func=func, bias=bias, scale=scale), R, W, self_sync)

        def ts(eng, o, i, s1, s2, op0, op1, R, W):
            if s2 is None:
                S.op(eng, lambda E: E.tensor_scalar(out=o, in0=i, scalar1=s1, scalar2=None, op0=op0), R, W)
            else:
                S.op(eng, lambda E: E.tensor_scalar(out=o, in0=i, scalar1=s1, scalar2=s2, op0=op0, op1=op1), R, W)

        def stt(o, i0, sc, i1, op0, op1, R, W):
            S.op("dve", lambda E: E.scalar_tensor_tensor(out=o, in0=i0, scalar=sc, in1=i1, op0=op0, op1=op1), R, W)

        def tt(eng, o, i0, i1, op, R, W):
            S.op(eng, lambda E: E.tensor_tensor(out=o, in0=i0, in1=i1, op=op), R, W)

        def cp(eng, o, i, R, W):
            S.op(eng, lambda E: E.tensor_copy(out=o, in_=i), R, W)

        def dma(eng, lane, o, i, R=(), W=(), WF=()):
            S.dma(eng, lane, lambda E: E.dma_start(out=o, in_=i), R, W, WF)

        R_hT, R_xr, R_gy, R_c0, R_c1, R_v, R_xn1, R_h1T, R_out = [Res(n) for n in
            ("hT_d", "xr_d", "gy_d", "c0_d", "c1_d", "v_d", "xn1_d", "h1T_d", "out")]
        R_cst = Res("cst_d")

        vecs = Buf("vecs", [128, NV], F32, lane=True)
        ident_f = Buf("ident_f", [128, 128], F32, lane=True)
        ident_b = Buf("ident_b", [128, 128], BF16)
        ones_f = Buf("ones_f", [128, 128], F32)
        hb = Buf("hb", [128, 64], F32)
        hc = Buf("hc", [128, 32], F32)
        tmpc = Buf("tmpc", [128, 32], F32)
        rs_all = Buf("rs_all", [128, 32], F32)
        nm_all = Buf("nm_all", [128, 32], F32)
        persist_end = cur[0]

        dma("sp", vecs.lane, vecs[:, :], vecs_in, W=[vecs.res])
        dma("sp", ident_f.lane, ident_f[:, :], ident_in, W=[ident_f.res])
        cp("dve", ident_b[:, :], ident_f[:, :], [ident_f.res], [ident_b.res])
        S.op("pool", lambda E: E.memset(ones_f[:, :], 1.0), [], [ones_f.res])
        V = lambda nm, j=0, n=1, p=128: vecs[0:p, _V[nm] + j:_V[nm] + j + n]
        ts("dve", hb[0:96, :], V("gb", 0, 64, 96), 0.5, None, ALU.mult, None, [vecs.res], [hb.res])
        act(tmpc[0:96, :], V("ap", 0, 32, 96), AF.Exp, [vecs.res], [tmpc.res], scale=-1.0)
        act(tmpc[0:96, :], tmpc[0:96, :], AF.Ln, [tmpc.res], [tmpc.res], bias=1.0, self_sync=True)
        ts("dve", hc[0:96, :], tmpc[0:96, :], -4.0, None, ALU.mult, None, [tmpc.res], [hc.res])

        def layer_norm_stats(xsrc, nsub, Rsrc, stb, mvb, rstd, nmr):
            for s in range(nsub):
                for hh in range(2):
                    S.op("dve", (lambda o=stb[:, s, hh, :], i=xsrc(s)[:, hh * 512:(hh + 1) * 512]:
                                 lambda E: E.bn_stats(out=o, in_=i))(), Rsrc, [stb.res])
                S.op("dve", (lambda s=s: lambda E: E.bn_aggr(
                    out=mvb[:, s, :], in_=stb[:, s, :, :].rearrange("p a b -> p (a b)")))(), [stb.res], [mvb.res])
            act(rstd[:, 0:nsub], mvb[:, 0:nsub, 1], AF.Sqrt, [mvb.res], [rstd.res], bias=EPS)
            S.op("dve", lambda E: E.reciprocal(out=rstd[:, 0:nsub], in_=rstd[:, 0:nsub]), [rstd.res], [rstd.res])
            stt(nmr[:, 0:nsub], mvb[:, 0:nsub, 0], -1.0, rstd[:, 0:nsub], ALU.mult, ALU.mult,
                [mvb.res, rstd.res], [nmr.res])

        def load_weight(buf, src, nk, ncols, c0, blk, reslist):
            srcv = src.rearrange("(kc p) c -> p kc c", p=128)
            for cb in range(ncols // blk):
                r = Res("wblk")
                reslist.append(r)
                dma("pool", S.dma_lane(), buf[:, 0:nk, cb * blk:(cb + 1) * blk],
                    srcv[:, :, c0 + cb * blk:c0 + (cb + 1) * blk], W=[r])

        psrot = [0]

        def next_bank(lo, hi):
            b = lo + psrot[0] % (hi - lo)
            psrot[0] += 1
            return b

        TT = 512
        w1 = Buf("w1", [128, 8, 5120], BF16)
        w1res = []
        load_weight(w1, w_in, 8, 5120, 0, 1024, w1res)
        xt = [Buf("xt%d" % i, [128, 4, D], F32, lane=True) for i in range(2)]
        xnb = Buf("xnb", [128, 4, D], BF16)
        hTt = [Buf("hTt%d" % i, [128, 8, TT], BF16, lane=True) for i in range(2)]
        xr_st = [Buf("xr_st%d" % i, [128, 6, TT], F32, lane=True) for i in range(2)]
        gy_st = [Buf("gy_st%d" % i, [128, 6, TT], BF16, lane=True) for i in range(2)]
        c0_st = [Buf("c0_st%d" % i, [128, 4, TT], BF16, lane=True) for i in range(2)]
        sg = [Buf("sg%d" % i, [128, TT], F32) for i in range(2)]
        stb = Buf("stb", [128, 4, 2, 6], F32)
        mvb = Buf("mvb", [128, 4, 2], F32)
        rstd = Buf("rstd", [128, 4], F32)
        nmr = Buf("nmr", [128, 4], F32)

        xv = x.rearrange("(n s p) f -> n p s f", s=4, p=128)
        NT1 = T // TT
        def p1_prepA(it):
            xb = xt[it % 2]
            layer_norm_stats(lambda s: xb[:, s, :], 4, [xb.res], stb, mvb, rstd, nmr)
            cp("dve", rs_all[:, it * 4:it * 4 + 4], rstd[:, 0:4], [rstd.res], [rs_all.res])
            cp("dve", nm_all[:, it * 4:it * 4 + 4], nmr[:, 0:4], [nmr.res], [nm_all.res])
            for s in range(4):
                act(xnb[:, s, :], xb[:, s, :], AF.Identity, [xb.res, rstd.res, nmr.res], [xnb.res],
                    bias=nmr[:, s:s + 1], scale=rstd[:, s:s + 1])

        def p1_prepB(it):
            hp = hTt[it % 2]
            for kc in range(8):
                b = next_bank(0, 2)
                pst = bank(b).bitcast(BF16)
                for s in range(4):
                    trp(pst[:, s * 128:(s + 1) * 128], xnb[:, s, kc * 128:(kc + 1) * 128], ident_b[:, :],
                        [xnb.res, ident_b.res], [Rps[b]])
                act(hp[:, kc, :], pst[:, 0:TT], AF.Identity, [Rps[b], vecs.res], [hp.res],
                    bias=V("emb_b", kc), scale=V("emb_g", kc))
            dma("sp", hp.lane, hT_d.rearrange("(kc p) t -> p kc t", p=128)[:, :, it * TT:(it + 1) * TT],
                hp[:, :, :], R=[hp.res], WF=[R_hT])

        dma("sp", xt[0].lane, xt[0][:, :, :], xv[0], W=[xt[0].res])
        dma("sp", xt[1].lane, xt[1][:, :, :], xv[1], W=[xt[1].res])
        p1_prepA(0)
        p1_prepB(0)
        for it in range(NT1):
            hb_ = hTt[it % 2]
            if it + 1 < NT1:
                p1_prepA(it + 1)

            def proj(col0, b):
                for kc in range(8):
                    mm(bank(b), w1[:, kc, col0:col0 + 128], hb_[:, kc, :], kc == 0, kc == 7,
                       [w1res[col0 // 1024], hb_.res], [Rps[b]])
            for c in range(12):
                b = next_bank(2, 8)
                proj(c * 128, b)
                stg = xr_st[c // 6]
                ts("dve", stg[:, c % 6, :], bank(b), V("b_xr", c), None, ALU.add, None,
                   [Rps[b], vecs.res], [stg.res])
                if c % 6 == 5:
                    dma("sp", stg.lane,
                        xr_d.rearrange("(kc p) t -> p kc t", p=128)[:, c - 5:c + 1, it * TT:(it + 1) * TT],
                        stg[:, :, :], R=[stg.res], WF=[R_xr])
            if it + 1 < NT1:
                p1_prepB(it + 1)
            if it + 2 < NT1:
                nb_ = xt[it % 2]
                dma("sp", nb_.lane, nb_[:, :, :], xv[it + 2], W=[nb_.res])
            for c in range(12):
                b = next_bank(2, 8)
                proj(DR + c * 128, b)
                stg = gy_st[c // 6]
                act(stg[:, c % 6, :], bank(b), AF.Gelu_apprx_tanh, [Rps[b], vecs.res], [stg.res],
                    bias=V("b_yr", c))
                if c % 6 == 5:
                    dma("sp", stg.lane,
                        gy_d.rearrange("(kc p) t -> p kc t", p=128)[:, c - 5:c + 1, it * TT:(it + 1) * TT],
                        stg[:, :, :], R=[stg.res], WF=[R_gy])
            for j in range(8):
                ba = next_bank(2, 8)
                proj(2 * DR + j * 128, ba)
                bb = next_bank(2, 8)
                proj(2 * DR + D + j * 128, bb)
                sgb = sg[j % 2]
                act(sgb[:, :], bank(bb), AF.Sigmoid, [Rps[bb], vecs.res], [sgb.res], bias=V("b_xcb", j))
                stg = c0_st[j // 4]
                stt(stg[:, j % 4, :], bank(ba), V("b_xca", j), sgb[:, :], ALU.add, ALU.mult,
                    [Rps[ba], sgb.res, vecs.res], [stg.res])
                if j % 4 == 3:
                    dma("sp", stg.lane,
                        c0_d.rearrange("(kc p) t -> p kc t", p=128)[:, j - 3:j + 1, it * TT:(it + 1) * TT],
                        stg[:, :, :], R=[stg.res], WF=[R_c0])

        S.barrier()
        cur[0] = persist_end
        wgb = Buf("wgb", [128, 64 * BW], BF16, lane=True)
        dma("pool", wgb.lane, wgb[0:BW, :], wg_in, W=[wgb.res])
        xrs = [Buf("xrs0", [128, T + 4], F32, lane=True)]
        gyb = Buf("gyb", [128, T], BF16, lane=True)
        ubs = [Buf("ub%d" % i, [128, T], F32) for i in range(2)]
        ubfs = [Buf("ubf%d" % i, [128, T], BF16) for i in range(2)]
        TRb = [Buf("TR%d" % d, [128, T], F32) for d in range(2)]
        TIb = [Buf("TI%d" % d, [128, T], F32) for d in range(2)]
        B0s = [Buf("B0_%d" % i, [128, T], F32) for i in range(2)]
        B1 = Buf("B1", [128, T], F32)
        D4 = [Buf("D4_%d" % i, [128, 4, BW], F32) for i in range(2)]
        P = BW
        xs = xrs[0]
        S.op("pool", lambda E: E.memset(xs[:, 0:2], 0.0), [], [xs.res])
        S.op("pool", lambda E: E.memset(xs[:, T + 2:T + 4], 0.0), [], [xs.res])

        def p2_loadx(n):
            dma("sp", xs.lane, xs[0:P, 2:T + 2], xr_d[n * BW:(n + 1) * BW, :], R=[R_xr], W=[xs.res])

        def p2_conv(n, half):
            ub, ubf, d4 = ubs[n % 2], ubfs[n % 2], D4[n % 2]
            if half == 0:
                for k in range(4):
                    act(d4[0:P, k, :], ident_f[0:P, 0:P], AF.Copy, [ident_f.res, vecs.res], [d4.res],
                        scale=V("w4", n * 4 + k, 1, P))
            for q in range(2 * half, 2 * half + 2):
                b2 = 2 * (next_bank(0, 4))
                for hh in range(2):
                    t0_ = q * 1024 + hh * 512
                    for k in range(4):
                        mm(bank(b2 + hh, 512, P), d4[0:P, k, :], xs[0:P, t0_ + k:t0_ + k + 512], k == 0, k == 3,
                           [d4.res, xs.res], [Rps[b2 + hh]])
                src = ps_all[0:P, b2 * 512:b2 * 512 + 1024]
                ts("dve", ub[0:P, q * 1024:(q + 1) * 1024], src, V("b4", n, 1, P), None, ALU.add, None,
                   [Rps[b2], Rps[b2 + 1], vecs.res], [ub.res])
                ts("dve", ubf[0:P, q * 1024:(q + 1) * 1024], src, V("b4", n, 1, P), None, ALU.add, None,
                   [Rps[b2], Rps[b2 + 1], vecs.res], [ubf.res])
            if half == 1 and n + 1 < NB:
                p2_loadx(n + 1)

        def p2_gates(n, d):
            ubf = ubfs[n % 2]
            for q in range(4):
                for g in range(2):
                    b2 = 2 * (next_bank(0, 4))
                    idx = (d * 2 + g) * NB + n
                    for hh in range(2):
                        mm(bank(b2 + hh, 512, P), wgb[0:P, idx * BW:(idx + 1) * BW],
                           ubf[0:P, q * 1024 + hh * 512:q * 1024 + (hh + 1) * 512], True, True,
                           [wgb.res, ubf.res], [Rps[b2 + hh]])
                    dst = (TRb if g == 0 else TIb)[d]
                    act(dst[0:P, q * 1024:(q + 1) * 1024], ps_all[0:P, b2 * 512:b2 * 512 + 1024], AF.Tanh,
                        [Rps[b2], Rps[b2 + 1], hb.res], [dst.res], bias=hb[0:P, idx:idx + 1], scale=0.5)

        def p2_dir(n, d):
            Bd = B0s[n % 2] if d == 0 else B1
            ub = ubs[n % 2]
            hcv = hc[0:P, d * NB + n:d * NB + n + 1]
            act(TRb[d][0:P, :], TRb[d][0:P, :], AF.Exp, [TRb[d].res, hc.res], [TRb[d].res], bias=hcv, scale=hcv)
            act(Bd[0:P, :], TRb[d][0:P, :], AF.Square, [TRb[d].res], [Bd.res])
            act(Bd[0:P, :], Bd[0:P, :], AF.Sqrt, [Bd.res], [Bd.res], bias=0.25, scale=-0.25)
            stt(TIb[d][0:P, :], TIb[d][0:P, :], 1.0, ub[0:P, :], ALU.add, ALU.mult,
                [TIb[d].res, ub.res], [TIb[d].res])
            tt("dve", TIb[d][0:P, :], TIb[d][0:P, :], Bd[0:P, :], ALU.mult,
               [TIb[d].res, Bd.res], [TIb[d].res])
            f = (lambda a: a) if d == 0 else rev
            S.op("dve", (lambda o=f(Bd[0:P, 0:T]), a_=f(TRb[d][0:P, 0:T]), b_=f(TIb[d][0:P, 0:T]):
                         lambda E: E.tensor_tensor_scan(out=o, data0=a_, data1=b_, initial=0.0,
                                                        op0=ALU.mult, op1=ALU.add))(),
                 [TRb[d].res, TIb[d].res, Bd.res], [Bd.res])

        p2_loadx(0)
        p2_conv(0, 0)
        p2_conv(0, 1)
        for n in range(NB):
            dma("sp", gyb.lane, gyb[0:P, :], gy_d[n * BW:(n + 1) * BW, :], R=[R_gy], W=[gyb.res])
            p2_gates(n, 0)
            if n + 1 < NB:
                p2_conv(n + 1, 0)
            p2_dir(n, 0)
            p2_gates(n, 1)
            if n + 1 < NB:
                p2_conv(n + 1, 1)
            p2_dir(n, 1)
            tt("pool", TIb[1][0:P, :], B0s[n % 2][0:P, :], B1[0:P, :], ALU.add,
               [B0s[n % 2].res, B1.res], [TIb[1].res])
            tt("pool", gyb[0:P, :], TIb[1][0:P, :], gyb[0:P, :], ALU.mult, [TIb[1].res, gyb.res], [gyb.res])
            dma("sp", gyb.lane, v_d[n * BW:(n + 1) * BW, :], gyb[0:P, :], R=[gyb.res], WF=[R_v])

        S.barrier()
        cur[0] = persist_end
        wou = Buf("wou", [128, 8, D], BF16)
        dead_base = cur[0]
        wgl = Buf("wgl", [128, 8, 2048], BF16)
        wbb = Buf("wbb", [128, 8, D], BF16)
        wba = Buf("wba", [128, 12, D], BF16)
        wgl_r, wbb_r, wba_r, wou_r = [], [], [], []
        load_weight(wgl, w_in, 8, 2048, 5120, 1024, wgl_r)
        load_weight(wbb, w_bb, 8, D, 0, 1024, wbb_r)
        load_weight(wba, w_ba, 12, D, 0, 1024, wba_r)
        load_weight(wou, w_out, 8, D, 0, 512, wou_r)
        w3_end = cur[0]
        R_wbf = Res("w_bf16")
        for rb in range(4):
            dma("pool", S.dma_lane(), wup_bf[rb * 256:(rb + 1) * 256, :], w_up[rb * 256:(rb + 1) * 256, :], WF=[R_wbf])
        for rb in range(4):
            dma("pool", S.dma_lane(), wdn_bf[rb * 1024:(rb + 1) * 1024, :], w_dn[rb * 1024:(rb + 1) * 1024, :], WF=[R_wbf])
        cxb = [Buf("cx%d" % i, [128, T + 32], BF16, lane=True) for i in range(2)]
        Dg = [Buf("Dg%d" % i, [128, 31, 128], BF16) for i in range(2)]
        c1s = [Buf("c1s%d" % i, [128, T], F32, lane=True) for i in range(2)]
        acc1 = Buf("acc1", [128, T], F32, lane=True)
        acc2 = Buf("acc2", [128, T], F32, lane=True)
        sqc = Buf("sqc", [128, T], F32)
        for i in range(2):
            S.op("pool", (lambda i=i: lambda E: E.memset(cxb[i][:, 0:15], 0.0))(), [], [cxb[i].res])
            S.op("pool", (lambda i=i: lambda E: E.memset(cxb[i][:, T + 15:T + 32], 0.0))(), [], [cxb[i].res])

        NPE = 28

        def build_diag(j):
            dg_ = Dg[j % 2]
            ia = ident_f[:, :]
            wa = V("convw", j * 31, NPE)
            pstep_i = ia.ap[0][0]
            pstep_w = wa.ap[0][0]
            in0 = AP(ia.tensor, ia.offset, [[pstep_i, 128], [0, NPE], [1, 128]])
            in1 = AP(wa.tensor, wa.offset, [[pstep_w, 128], [1, NPE], [0, 128]])
            tt("dve", dg_[:, 0:NPE, :], in0, in1, ALU.mult, [ident_f.res, vecs.res], [dg_.res])

        def p2c_load(j):
            cb_ = cxb[j % 2]
            dma("sp", cb_.lane, cb_[:, 15:T + 15], c0_d[j * 128:(j + 1) * 128, :], R=[R_c0], W=[cb_.res])
        p2c_load(0)
        build_diag(0)
        for j in range(8):
            cb_ = cxb[j % 2]
            dg = Dg[j % 2]
            co = c1s[j % 2]
            if j + 1 < 8:
                p2c_load(j + 1)
                build_diag(j + 1)
            ts("dve", co[:, :], cb_[:, NPE:NPE + T], V("convw", j * 31 + NPE), None, ALU.mult, None,
               [cb_.res, vecs.res], [co.res])
            for k in range(NPE + 1, 31):
                stt(co[:, :], cb_[:, k:k + T], V("convw", j * 31 + k), co[:, :], ALU.mult, ALU.add,
                    [cb_.res, co.res, vecs.res], [co.res])
            for tt_ in range(8):
                b = next_bank(0, 8)
                for k in range(NPE):
                    mm(bank(b), dg[:, k, :], cb_[:, tt_ * 512 + k:tt_ * 512 + k + 512], k == 0, k == NPE - 1,
                       [dg.res, cb_.res], [Rps[b]])
                tsl = slice(tt_ * 512, (tt_ + 1) * 512)
                stt(co[:, tsl], bank(b), V("conv_b", j), co[:, tsl], ALU.add, ALU.add,
                    [Rps[b], co.res, vecs.res], [co.res])
            dma("sp", co.lane, c1_d[j * 128:(j + 1) * 128, :], co[:, :], R=[co.res], WF=[R_c1])
            if j == 0:
                cp("pool", acc1[:, :], co[:, :], [co.res], [acc1.res])
                act(acc2[:, :], co[:, :], AF.Square, [co.res], [acc2.res])
            else:
                tt("pool", acc1[:, :], acc1[:, :], co[:, :], ALU.add, [acc1.res, co.res], [acc1.res])
                act(sqc[:, :], co[:, :], AF.Square, [co.res], [sqc.res])
                tt("pool", acc2[:, :], acc2[:, :], sqc[:, :], ALU.add, [acc2.res, sqc.res], [acc2.res])
        for tt_ in range(8):
            tsl = slice(tt_ * 512, (tt_ + 1) * 512)
            b1_ = next_bank(0, 8)
            b2_ = next_bank(0, 8)
            mm(bank(b1_), ones_f[:, :], acc1[:, tsl], True, True, [ones_f.res, acc1.res], [Rps[b1_]])
            mm(bank(b2_), ones_f[:, :], acc2[:, tsl], True, True, [ones_f.res, acc2.res], [Rps[b2_]])
            ts("dve", sqc[:, tsl], bank(b1_), 1.0 / D, None, ALU.mult, None, [Rps[b1_], acc1.res], [sqc.res])
            tt("dve", c1s[1][:, tsl], sqc[:, tsl], sqc[:, tsl], ALU.mult, [sqc.res], [c1s[1].res])
            stt(c1s[0][:, tsl], bank(b2_), 1.0 / D, c1s[1][:, tsl], ALU.mult, ALU.subtract,
                [Rps[b2_], c1s[1].res, acc2.res], [c1s[0].res])
        act(c1s[0][:, :], c1s[0][:, :], AF.Sqrt, [c1s[0].res], [c1s[0].res], bias=EPS)
        S.op("dve", lambda E: E.reciprocal(out=c1s[0][:, :], in_=c1s[0][:, :]), [c1s[0].res], [c1s[0].res])
        dma("sp", acc1.lane, cst_d[0], sqc[:, :], R=[sqc.res], WF=[R_cst])
        dma("sp", acc2.lane, cst_d[1], c1s[0][:, :], R=[c1s[0].res], WF=[R_cst])

        S.barrier()
        cur[0] = w3_end
        TT3 = 256
        NS = TT3 // 128
        hT3s = [Buf("hT3_%d" % i, [128, 8, TT3], BF16, lane=True) for i in range(2)]
        v3s = [Buf("v3_%d" % i, [128, 12, TT3], BF16, lane=True) for i in range(2)]
        c13 = Buf("c13", [128, 8, TT3], F32, lane=True)
        mean3 = Buf("mean3", [128, TT3], F32, lane=True)
        rstd3 = Buf("rstd3", [128, TT3], F32, lane=True)
        xnt = [Buf("xnt%d" % i, [128, TT3], F32) for i in range(2)]
        yt = [Buf("yt%d" % i, [128, TT3], F32) for i in range(2)]
        tht = [Buf("tht%d" % i, [128, TT3], F32) for i in range(2)]
        cBs = [Buf("cB%d" % i, [128, 8, TT3], BF16) for i in range(2)]
        t0b = [Buf("t0_%d" % i, [128, TT3], F32) for i in range(2)]
        t1b = [Buf("t1_%d" % i, [128, TT3], F32) for i in range(2)]
        u1b = [Buf("u1_%d" % i, [128, TT3], F32) for i in range(2)]
        u2b = [Buf("u2_%d" % i, [128, TT3], F32) for i in range(2)]
        zTs = [Buf("zT%d" % i, [128, 8, TT3], BF16) for i in range(2)]
        assert cur[0] - 8 * TT3 * 2 == dead_base + 2 * 8 * DFF * 2, (cur[0], dead_base)
        rows = {}
        for nm in ("G1", "B1"):
            rows[nm] = Buf("row_" + nm, [128, D], F32, lane=True)
        rtmp = Buf("xn0", [128, D], F32, lane=True)
        ri = {n: i for i, n in enumerate(ROWS)}

        def make_rows(gn, bn, gsrc, bsrc, addsrc):
            dma("sp", rows[gn].lane, rows[gn][:, :], rows_in[ri[gsrc]], W=[rows[gn].res])
            ts("dve", rows[gn][:, :], rows[gn][:, :], ALPHA, None, ALU.mult, None, [rows[gn].res], [rows[gn].res])
            dma("sp", rows[bn].lane, rows[bn][:, :], rows_in[ri[bsrc]], W=[rows[bn].res])
            dma("sp", rtmp.lane, rtmp[:, :], rows_in[ri[addsrc]], W=[rtmp.res])
            stt(rows[bn][:, :], rows[bn][:, :], ALPHA, rtmp[:, :], ALU.mult, ALU.add,
                [rows[bn].res, rtmp.res], [rows[bn].res])

        halfv = Buf("halfv", [128, 32], F32)
        ts("dve", halfv[:, 0:16], V("cln_g", 0, 16), 0.5, None, ALU.mult, None, [vecs.res], [halfv.res])
        ts("dve", halfv[:, 16:32], V("b_g0", 0, 16), 0.5, None, ALU.mult, None, [vecs.res], [halfv.res])
        mhalf = Buf("mhalf", [128, TT3], F32)
        S.op("pool", lambda E: E.memset(mhalf[:, :], -0.5), [], [mhalf.res])
        xC = Buf("xC", [128, NS, D], F32, lane=True)
        xn0 = rtmp
        tbss = [Buf("tbs%d" % i, [128, D], F32) for i in range(2)]
        xn1 = [Buf("xn1_%d" % i, [128, D], F32, lane=True) for i in range(2)]
        xn1bs = [Buf("xn1b%d" % i, [128, NS, D], BF16) for i in range(2)]
        h1T = [Buf("h1T%d" % i, [128, 8, TT3], BF16, lane=True) for i in range(2)]
        stb4 = Buf("stb4", [128, 4, 2, 6], F32)
        mvb4 = Buf("mvb4", [128, 4, 2], F32)
        vt4 = Buf("vt4", [128, 4], F32)
        rs1 = Buf("rs1", [128, 4], F32)
        nm1 = Buf("nm1", [128, 4], F32)

        xv3 = x.rearrange("(n s p) f -> n p s f", s=NS, p=128)
        hTv = hT_d.rearrange("(kc p) t -> p kc t", p=128)
        vv = v_d.rearrange("(kc p) t -> p kc t", p=128)
        c1v = c1_d.rearrange("(kc p) t -> p kc t", p=128)
        h1Tv = h1T_d.rearrange("(kc p) t -> p kc t", p=128)
        NT3 = T // TT3
        BS, BQ = 4, 5

        def rstd_pow(dst, var_ap, tmp, n, Rin, Rtmp, Rdst):
            ts("dve", tmp, var_ap, EPS, None, ALU.add, None, Rin, [Rtmp])
            tt("pool", dst, tmp, mhalf[:, 0:n], ALU.pow, [Rtmp, mhalf.res], [Rdst])

        def bcast_row(row, it):
            return cst_d[row][:, it * TT3:(it + 1) * TT3]

        def load_A(it):
            tsl = slice(it * TT3, (it + 1) * TT3)
            dma("sp", c13.lane, c13[:, :, :], c1v[:, :, tsl], R=[R_c1], W=[c13.res])
            dma("sp", mean3.lane, mean3[:, :], bcast_row(0, it), R=[R_cst], W=[mean3.res])
            dma("sp", rstd3.lane, rstd3[:, :], bcast_row(1, it), R=[R_cst], W=[rstd3.res])

        def load_B(it):
            tsl = slice(it * TT3, (it + 1) * TT3)
            h_, v_ = hT3s[it % 2], v3s[it % 2]
            dma("sp", h_.lane, h_[:, :, :], hTv[:, :, tsl], R=[R_hT], W=[h_.res])
            dma("sp", v_.lane, v_[:, :, :], vv[:, :, tsl], R=[R_v], W=[v_.res])

        def stage_A(it):
            cB = cBs[it % 2]
            for i in range(8 + 2):
                if i < 8:
                    kc = i
                    xb_ = xnt[kc % 2]
                    tt("dve", xb_[:, :], c13[:, kc, :], mean3[:, :], ALU.subtract, [c13.res, mean3.res], [xb_.res])
                    tt("dve", xb_[:, :], xb_[:, :], rstd3[:, :], ALU.mult, [xb_.res, rstd3.res], [xb_.res])
                if 1 <= i < 9:
                    kc = i - 1
                    xb_, yb_, th_ = xnt[kc % 2], yt[kc % 2], tht[kc % 2]
                    act(th_[:, :], xb_[:, :], AF.Tanh, [xb_.res, halfv.res], [th_.res],
                        bias=halfv[:, 8 + kc:9 + kc], scale=halfv[:, kc:kc + 1])
                    ts("dve", yb_[:, :], xb_[:, :], V("cln_g", kc), V("cln_b", kc), ALU.mult, ALU.add,
                       [xb_.res, vecs.res], [yb_.res])
                if 2 <= i:
                    kc = i - 2
                    yb_, th_ = yt[kc % 2], tht[kc % 2]
                    stt(cB[:, kc, :], th_[:, :], 1.0, yb_[:, :], ALU.add, ALU.mult, [th_.res, yb_.res], [cB.res])
                yield
            if it + 1 < NT3:
                load_A(it + 1)

        def stage_B(it):
            cB, zT = cBs[it % 2], zTs[it % 2]
            hT3, v3 = hT3s[it % 2], v3s[it % 2]
            if it + 1 < NT3:
                load_B(it + 1)

            def merge(oc):
                Y = 2 * (oc % 2) + 1
                RY = Rps[Y]
                s0, s1, u1, u2 = t0b[oc % 2], t1b[oc % 2], u1b[oc % 2], u2b[oc % 2]
                stt(u1[:, :], s0[:, :], 1.0, bank(Y)[:, 0:TT3], ALU.add, ALU.mult, [s0.res, RY], [u1.res])
                stt(u2[:, :], s1[:, :], 1.0, bank(Y)[:, TT3:2 * TT3], ALU.add, ALU.mult, [s1.res, RY], [u2.res])
                stt(zT[:, oc, :], u2[:, :], 0.5, u1[:, :], ALU.mult, ALU.add, [u1.res, u2.res], [zT.res])

            for oc in range(8):
                X = 2 * (oc % 2)
                Y = X + 1
                RX, RY = Rps[X], Rps[Y]
                for kc in range(8):
                    mm(bank(X)[:, 0:TT3], wgl[:, kc, oc * 128:(oc + 1) * 128], hT3[:, kc, :], kc == 0, kc == 7,
                       [wgl_r[0], hT3.res], [RX])
                for kc in range(8):
                    mm(bank(X)[:, TT3:2 * TT3], wgl[:, kc, D + oc * 128:D + (oc + 1) * 128], hT3[:, kc, :],
                       kc == 0, kc == 7, [wgl_r[1], hT3.res], [RX])
                for kc in range(12):
                    mm(bank(Y)[:, 0:TT3], wba[:, kc, oc * 128:(oc + 1) * 128], v3[:, kc, :], kc == 0, kc == 11,
                       [wba_r[0], v3.res], [RY])
                for kc in range(8):
                    mm(bank(Y)[:, TT3:2 * TT3], wbb[:, kc, oc * 128:(oc + 1) * 128], cB[:, kc, :], kc == 0, kc == 7,
                       [wbb_r[0], cB.res], [RY])
                s0, s1 = t0b[oc % 2], t1b[oc % 2]
                act(s0[:, :], bank(X)[:, 0:TT3], AF.Tanh, [RX, halfv.res], [s0.res],
                    bias=halfv[:, 16 + oc:17 + oc], scale=0.5)
                act(s1[:, :], bank(X)[:, TT3:2 * TT3], AF.Tanh, [RX, halfv.res], [s1.res],
                    bias=halfv[:, 24 + oc:25 + oc], scale=0.5)
                if oc >= 1:
                    merge(oc - 1)
                yield
            merge(7)
            yield

        def stage_C(it, use_pool=True):
            zT = zTs[it % 2]
            xn1b = xn1bs[it % 2]
            meng = "pool" if use_pool else "dve"
            for s in range(NS):
                tbs = tbss[s]
                col = it * NS + s
                act(xn0[:, :], xC[:, s, :], AF.Identity, [xC.res, rs_all.res, nm_all.res], [xn0.res],
                    bias=nm_all[:, col:col + 1], scale=rs_all[:, col:col + 1])
                tt(meng, tbs[:, :], xn0[:, :], rows["G1"][:, :], ALU.mult, [xn0.res, rows["G1"].res], [tbs.res])
                tt("dve", tbs[:, :], tbs[:, :], rows["B1"][:, :], ALU.add, [tbs.res, rows["B1"].res], [tbs.res])
                yield
            if it + 1 < NT3:
                dma("sp", xC.lane, xC[:, :, :], xv3[it + 1], W=[xC.res])
            for s in range(NS):
                tbs = tbss[s]
                for hh in range(2):
                    b = 4 + 2 * s + hh
                    for kc in range(8):
                        mm(bank(b), zT[:, kc, s * 128:(s + 1) * 128], wou[:, kc, hh * 512:(hh + 1) * 512],
                           kc == 0, kc == 7, [zT.res, wou_r[hh]], [Rps[b]])
                    stt(tbs[:, hh * 512:(hh + 1) * 512], bank(b), 0.5, tbs[:, hh * 512:(hh + 1) * 512],
                        ALU.mult, ALU.add, [tbs.res, Rps[b]], [tbs.res])
                for hh in range(2):
                    S.op("dve", (lambda o=stb4[:, s, hh, :], i=tbs[:, hh * 512:(hh + 1) * 512]:
                                 lambda E: E.bn_stats(out=o, in_=i))(), [tbs.res], [stb4.res])
                S.op("dve", (lambda o=mvb4[:, s, :], i=stb4[:, s, :, :].rearrange("p a b -> p (a b)"):
                             lambda E: E.bn_aggr(out=o, in_=i))(), [stb4.res], [mvb4.res])
                yield
            if use_pool:
                rstd_pow(rs1[:, 0:NS], mvb4[:, 0:NS, 1], vt4[:, 0:NS], NS, [mvb4.res], vt4.res, rs1.res)
            else:
                act(rs1[:, 0:NS], mvb4[:, 0:NS, 1], AF.Sqrt, [mvb4.res], [rs1.res], bias=EPS)
                S.op("dve", lambda E: E.reciprocal(out=rs1[:, 0:NS], in_=rs1[:, 0:NS]), [rs1.res], [rs1.res])
            stt(nm1[:, 0:NS], mvb4[:, 0:NS, 0], -1.0, rs1[:, 0:NS], ALU.mult, ALU.mult,
                [mvb4.res, rs1.res], [nm1.res])
            yield
            for s in range(NS):
                tbs = tbss[s]
                x1 = xn1[s % 2]
                act(xn1b[:, s, :], tbs[:, :], AF.Identity, [tbs.res, rs1.res, nm1.res], [xn1b.res],
                    bias=nm1[:, s:s + 1], scale=rs1[:, s:s + 1])
                act(x1[:, :], tbs[:, :], AF.Identity, [tbs.res, rs1.res, nm1.res], [x1.res],
                    bias=nm1[:, s:s + 1], scale=rs1[:, s:s + 1])
                dma("sp", x1.lane, xn1_d[it * TT3 + s * 128:it * TT3 + (s + 1) * 128, :], x1[:, :],
                    R=[x1.res], WF=[R_xn1])
                yield

        def stage_D(it):
            xn1b = xn1bs[it % 2]
            ho = h1T[it % 2]
            for kc in range(8):
                b = 4 + kc % 4
                pst = bank(b).bitcast(BF16)
                for s in range(NS):
                    trp(pst[:, s * 128:(s + 1) * 128], xn1b[:, s, kc * 128:(kc + 1) * 128], ident_b[:, :],
                        [xn1b.res, ident_b.res], [Rps[b]])
                act(ho[:, kc, :], pst[:, 0:TT3], AF.Identity, [Rps[b], vecs.res], [ho.res],
                    bias=V("ln1_b", kc), scale=V("ln1_g", kc))
                if kc % 2 == 1:
                    yield
            dma("sp", ho.lane, h1Tv[:, :, it * TT3:(it + 1) * TT3], ho[:, :, :], R=[ho.res], WF=[R_h1T])

        def drive(primary, others):
            live = [g for g in others if g is not None]
            for _ in primary:
                for g in list(live):
                    for _k in range(2):
                        try:
                            next(g)
                        except StopIteration:
                            if g in live:
                                live.remove(g)
                            break
            for g in live:
                for _ in g:
                    pass

        load_A(0)
        load_B(0)
        dma("sp", xC.lane, xC[:, :, :], xv3[0], W=[xC.res])
        make_rows("G1", "B1", "emb_g", "emb_b", "b_out")
        for _ in stage_A(0):
            pass
        for it in range(NT3):
            gA = stage_A(it + 1) if it + 1 < NT3 else None
            gC = stage_C(it - 1) if it >= 1 else None
            gD = stage_D(it - 2) if it >= 2 else None
            drive(stage_B(it), [gC, gD, gA])
        assert (NT3 - 1) % 2 == 1
        p3b_mark = cur[0]
        cur[0] = dead_base
        wup = Buf("wup", [128, 8, DFF], BF16)
        wdn = Buf("wdn", [128, 32, D], BF16)
        ffn_w_end = cur[0]
        cur[0] = p3b_mark
        dead = [wgl_r[0], wgl_r[1], wbb_r[0], wba_r[0], c13.res, mean3.res, rstd3.res, zTs[0].res]
        for lst in (hT3s, v3s, xnt, yt, tht, cBs, t0b, t1b, u1b, u2b):
            dead += [b_.res for b_ in lst]
        wup_r, wdn_r = [], []
        srcv_up = wup_bf.rearrange("(kc p) c -> p kc c", p=128)
        for cb in range(4):
            r = Res("wupblk")
            wup_r.append(r)
            dma("sp", S.dma_lane(), wup[:, 0:8, cb * 1024:(cb + 1) * 1024],
                srcv_up[:, :, cb * 1024:(cb + 1) * 1024], R=[R_wbf], W=[r] + dead)
        srcv_dn = wdn_bf.rearrange("(kc p) c -> p kc c", p=128)
        for cb in range(2):
            r = Res("wdnblk")
            wdn_r.append(r)
            dma("sp", S.dma_lane(), wdn[:, 0:32, cb * 512:(cb + 1) * 512],
                srcv_dn[:, :, cb * 512:(cb + 1) * 512], R=[R_wbf], W=[r] + dead)
        drive(stage_C(NT3 - 1), [stage_D(NT3 - 2)])
        for _ in stage_D(NT3 - 1):
            pass

        S.barrier()
        cur[0] = persist_end
        rows = {}
        for nm in ("G2", "B2", "ln2_g", "ln2_b"):
            rows[nm] = Buf("row_" + nm, [128, D], F32, lane=True)
        assert cur[0] <= dead_base
        cur[0] = ffn_w_end
        rtmp = Buf("rtmp2", [128, D], F32, lane=True)
        h4 = [Buf("h4_%d" % i, [128, 8, TT3], BF16, lane=True) for i in range(2)]
        x4 = [Buf("x4_%d" % i, [128, NS, D], F32, lane=True) for i in range(1)]
        mT = Buf("mT", [128, 32, TT3], BF16)
        rl = [Buf("rl%d" % i, [128, TT3], F32) for i in range(2)]
        t4 = [Buf("t4_%d" % i, [128, D], F32) for i in range(1)]
        o4 = [Buf("o4_%d" % i, [128, D], F32, lane=True) for i in range(2)]
        stb5 = Buf("stb5", [128, 4, 2, 6], F32)
        mvb5 = Buf("mvb5", [128, 4, 2], F32)
        rs2 = Buf("rs2", [128, 4], F32)
        nm2 = Buf("nm2", [128, 4], F32)
        xn1v = xn1_d.rearrange("(n s p) f -> n p s f", s=NS, p=128)

        def p4_load_early(it):
            sl = it % 2
            dma("sp", h4[sl].lane, h4[sl][:, :, :], h1Tv[:, :, it * TT3:(it + 1) * TT3], R=[R_h1T], W=[h4[sl].res])

        def p4_load_late(it):
            dma("sp", x4[0].lane, x4[0][:, :, :], xn1v[it], R=[R_xn1], W=[x4[0].res])
        p4_load_early(0)
        p4_load_late(0)
        make_rows("G2", "B2", "ln1_g", "ln1_b", "b_down")
        dma("sp", rows["ln2_g"].lane, rows["ln2_g"][:, :], rows_in[ri["ln2_g"]], W=[rows["ln2_g"].res])
        dma("sp", rows["ln2_b"].lane, rows["ln2_b"][:, :], rows_in[ri["ln2_b"]], W=[rows["ln2_b"].res])
        oi = 0
        for it in range(NT3):
            sl = it % 2
            if it + 1 < NT3:
                p4_load_early(it + 1)
            ht, xt4 = h4[sl], x4[0]
            for fc in range(32):
                b = next_bank(0, 8)
                for kc in range(8):
                    mm(bank(b, TT3), wup[:, kc, fc * 128:(fc + 1) * 128], ht[:, kc, :], kc == 0, kc == 7,
                       [wup_r[(fc * 128) // 1024], ht.res], [Rps[b]])
                r_ = rl[fc % 2]
                ts("dve", r_[:, :], bank(b, TT3), V("b_up", fc), 0.0, ALU.add, ALU.max, [Rps[b], vecs.res], [r_.res])
                act(mT[:, fc, :], r_[:, :], AF.Square, [r_.res], [mT.res])
            for s in range(NS):
                tbb = t4[0]
                tt("pool", tbb[:, :], xt4[:, s, :], rows["G2"][:, :], ALU.mult, [xt4.res, rows["G2"].res], [tbb.res])
                tt("pool", tbb[:, :], tbb[:, :], rows["B2"][:, :], ALU.add, [tbb.res, rows["B2"].res], [tbb.res])
                for hh in range(2):
                    b = next_bank(0, 8)
                    for fc in range(32):
                        mm(bank(b), mT[:, fc, s * 128:(s + 1) * 128], wdn[:, fc, hh * 512:(hh + 1) * 512],
                           fc == 0, fc == 31, [mT.res, wdn_r[hh]], [Rps[b]])
                    tt("dve", tbb[:, hh * 512:(hh + 1) * 512], tbb[:, hh * 512:(hh + 1) * 512], bank(b), ALU.add,
                       [tbb.res, Rps[b]], [tbb.res])
                for hh in range(2):
                    S.op("dve", (lambda s=s, hh=hh, tbb=tbb: lambda E: E.bn_stats(
                        out=stb5[:, s, hh, :], in_=tbb[:, hh * 512:(hh + 1) * 512]))(), [tbb.res], [stb5.res])
                S.op("dve", (lambda s=s: lambda E: E.bn_aggr(
                    out=mvb5[:, s, :], in_=stb5[:, s, :, :].rearrange("p a b -> p (a b)")))(), [stb5.res], [mvb5.res])
                act(rs2[:, s:s + 1], mvb5[:, s, 1:2], AF.Sqrt, [mvb5.res], [rs2.res], bias=EPS)
                S.op("dve", (lambda s=s: lambda E: E.reciprocal(out=rs2[:, s:s + 1], in_=rs2[:, s:s + 1]))(),
                     [rs2.res], [rs2.res])
                stt(nm2[:, s:s + 1], mvb5[:, s, 0:1], -1.0, rs2[:, s:s + 1], ALU.mult, ALU.mult,
                    [mvb5.res, rs2.res], [nm2.res])
                ob = o4[oi % 2]
                oi += 1
                act(ob[:, :], tbb[:, :], AF.Identity, [tbb.res, rs2.res, nm2.res], [ob.res],
                    bias=nm2[:, s:s + 1], scale=rs2[:, s:s + 1])
                tt("pool", ob[:, :], ob[:, :], rows["ln2_g"][:, :], ALU.mult, [ob.res, rows["ln2_g"].res], [ob.res])
                tt("dve", ob[:, :], ob[:, :], rows["ln2_b"][:, :], ALU.add, [ob.res, rows["ln2_b"].res], [ob.res])
                dma("sp", ob.lane, out[it * TT3 + s * 128:it * TT3 + (s + 1) * 128, :], ob[:, :],
                    R=[ob.res], WF=[R_out])
            if it + 1 < NT3:
                p4_load_late(it + 1)
        S.barrier()
        S.run()
    return nc


_NC_CACHE = {}


def _cols(v, p):
    v = np.asarray(v, np.float32).reshape(-1, p).T
    o = np.zeros((128, v.shape[1]), np.float32)
    o[:p] = v
    return o


def kernel(x, emb_ln_g, emb_ln_b, w_in, b_in, rnn_conv_w, rnn_conv_b, rg_gate_w, rg_gate_b,
           rg_a_param, w_branch_a, conv_w, conv_b, conv_ln_g, conv_ln_b, w_branch_b,
           w_out, b_out, ln1_g, ln1_b, w_up, b_up, w_down, b_down, ln2_g, ln2_b):
    f = lambda a: np.ascontiguousarray(np.asarray(a, np.float32))
    x = f(x)
    b_in0 = f(b_in)[0]
    parts = {
        "b_xr": _cols(b_in0[0:DR], 128), "b_yr": _cols(b_in0[DR:2 * DR], 128),
        "b_xca": _cols(b_in0[2 * DR:2 * DR + D], 128), "b_xcb": _cols(b_in0[2 * DR + D:2 * DR + 2 * D], 128),
        "b_g0": _cols(b_in0[5120:5120 + D], 128), "b_g1": _cols(b_in0[5120 + D:], 128),
        "emb_g": _cols(emb_ln_g, 128), "emb_b": _cols(emb_ln_b, 128),
        "conv_b": _cols(f(conv_b)[0], 128), "cln_g": _cols(f(conv_ln_g)[0], 128), "cln_b": _cols(f(conv_ln_b)[0], 128),
        "ln1_g": _cols(f(ln1_g)[0], 128), "ln1_b": _cols(f(ln1_b)[0], 128), "b_up": _cols(f(b_up)[0], 128),
        "convw": np.ascontiguousarray(f(conv_w)[0].reshape(31, 8, 128).transpose(2, 1, 0)).reshape(128, 8 * 31),
    }
    w4 = np.zeros((128, 64), np.float32)
    w4[:BW] = f(rnn_conv_w)[0].reshape(4, NB, BW).transpose(2, 1, 0).reshape(BW, 64)
    parts["w4"] = w4
    parts["b4"] = _cols(f(rnn_conv_b)[0], BW)
    gb = np.zeros((128, 64), np.float32)
    gb[:BW] = f(rg_gate_b)[0].reshape(64, BW).T
    parts["gb"] = gb
    ap_ = np.zeros((128, 32), np.float32)
    ap_[:BW] = f(rg_a_param)[0].reshape(2 * NB, BW).T
    parts["ap"] = ap_
    vecs = np.zeros((128, NV), np.float32)
    for nm, off in _V.items():
        a = parts[nm]
        vecs[:, off:off + a.shape[1]] = a
    rowsrc = {"emb_g": emb_ln_g, "emb_b": emb_ln_b, "b_out": f(b_out)[0], "ln1_g": f(ln1_g)[0],
              "ln1_b": f(ln1_b)[0], "b_down": f(b_down)[0], "ln2_g": f(ln2_g)[0], "ln2_b": f(ln2_b)[0]}
    rows = np.ascontiguousarray(np.stack(
        [np.broadcast_to(f(rowsrc[n]).reshape(1, D), (128, D)) for n in ROWS]))
    wg = np.ascontiguousarray(f(rg_gate_w)[0].reshape(64, BW, BW).transpose(1, 0, 2)).reshape(BW, 64 * BW)
    shared = {
        "w_in": f(w_in)[0], "w_ba": f(w_branch_a)[0], "w_bb": f(w_branch_b)[0], "w_out": f(w_out)[0],
        "w_up": f(w_up)[0], "w_dn": f(w_down)[0], "wg": wg, "vecs": vecs, "rows": rows,
        "ident": np.eye(128, dtype=np.float32),
    }
    if "nc" not in _NC_CACHE:
        _NC_CACHE["nc"] = build_nc()
    nc = _NC_CACHE["nc"]
    in_maps = [dict(shared, x=x[b]) for b in range(8)]
    res = run_bass_kernel_spmd(nc, in_maps, core_ids=list(range(8)))
    return np.stack([np.asarray(r["out"], np.float32) for r in res.results], axis=0)
```

```python
import contextlib
import numpy as np
import concourse.bass as bass
import concourse.mybir as mybir
from concourse.ap import AP
from concourse.bass_utils import run_bass_kernel_spmd

F32 = mybir.dt.float32
BF16 = mybir.dt.bfloat16
AF = mybir.ActivationFunctionType
ALU = mybir.AluOpType

T = 4096
D = 1024
DR = 1536
DIN = 7168
DFF = 4096
NB = 16
BW = 96
ALPHA = 2.0 ** 0.25
EPS = 1e-5
SAME_ENGINE_SYNC = ("pool", "dve", "act")


class Res:
    __slots__ = ("name", "w", "r")

    def __init__(self, name):
        self.name = name
        self.w = {}
        self.r = {}


class Sched:
    ENGS = ("pe", "act", "dve", "pool", "sp")

    def __init__(self, nc, stack):
        self.nc = nc
        self.stack = stack
        self.q = {e: [] for e in self.ENGS}
        self.lane_sem = {}
        self.lane_cnt = {}
        self.seen = {e: {} for e in self.ENGS}
        for e in self.ENGS:
            self.new_lane(e)
        self.n_dma_lanes = 0

    def new_lane(self, name):
        sem = self.stack.enter_context(self.nc.semaphore("s_" + name))
        self.lane_sem[name] = sem
        self.lane_cnt[name] = 0
        return name

    def dma_lane(self):
        self.n_dma_lanes += 1
        return self.new_lane("d%d" % self.n_dma_lanes)

    def _deps(self, eng, reads, writes, force=False):
        deps = {}

        def need(lane, v):
            if deps.get(lane, 0) < v:
                deps[lane] = v
        for b in reads:
            for lane, v in b.w.items():
                need(lane, v)
        for b in writes:
            for lane, v in b.w.items():
                need(lane, v)
            for lane, v in b.r.items():
                need(lane, v)
        waits = []
        for lane, v in deps.items():
            if lane == eng and eng not in SAME_ENGINE_SYNC and not force:
                continue
            if self.seen[eng].get(lane, 0) >= v:
                continue
            self.seen[eng][lane] = v
            waits.append((self.lane_sem[lane], v))
        return waits

    @staticmethod
    def _commit(ticket, reads, writes):
        lane, v = ticket
        for b in reads:
            if b.r.get(lane, 0) < v:
                b.r[lane] = v
        for b in writes:
            if b.w.get(lane, 0) < v:
                b.w[lane] = v

    def op(self, eng, fn, reads=(), writes=(), self_sync=False):
        waits = self._deps(eng, reads, writes, self_sync)
        self.lane_cnt[eng] += 1
        ticket = (eng, self.lane_cnt[eng])
        sem = self.lane_sem[eng]

        def emit(E, fn=fn, waits=waits, sem=sem):
            for s, v in waits:
                E.wait_ge(s, v)
            fn(E).then_inc(sem, 1)
        self.q[eng].append(emit)
        self._commit(ticket, reads, writes)

    def dma(self, eng, lane, fn, reads=(), writes=(), writes_free=()):
        waits = self._deps(eng, reads, writes)
        self.lane_cnt[lane] += 16
        ticket = (lane, self.lane_cnt[lane])
        sem = self.lane_sem[lane]

        def emit(E, fn=fn, waits=waits, sem=sem):
            for s, v in waits:
                E.wait_ge(s, v)
            fn(E).then_inc(sem, 16)
        self.q[eng].append(emit)
        self._commit(ticket, reads, tuple(writes) + tuple(writes_free))

    def barrier(self, engs=None):
        for eng in (engs or self.ENGS):
            waits = []
            for lane, v in self.lane_cnt.items():
                if v == 0 or lane == eng:
                    continue
                if self.seen[eng].get(lane, 0) >= v:
                    continue
                self.seen[eng][lane] = v
                waits.append((self.lane_sem[lane], v))

            def emit(E, waits=waits):
                for s, v in waits:
                    E.wait_ge(s, v)
            self.q[eng].append(emit)

    def run(self):
        with self.nc.Block() as block:
            @block.tensor
            def _(E):
                for f in self.q["pe"]:
                    f(E)

            @block.scalar
            def _(E):
                for f in self.q["act"]:
                    f(E)

            @block.vector
            def _(E):
                for f in self.q["dve"]:
                    f(E)

            @block.gpsimd
            def _(E):
                for f in self.q["pool"]:
                    f(E)

            @block.sync
            def _(E):
                for f in self.q["sp"]:
                    f(E)


def rev(ap):
    steps = [list(x) for x in ap.ap]
    st, cnt = steps[-1]
    off = ap.offset + st * (cnt - 1)
    steps[-1] = [-st, cnt]
    return AP(ap.tensor, off, steps)


_V = {}
_off = 0
for _nm, _n in [("b_xr", 12), ("b_yr", 12), ("b_xca", 8), ("b_xcb", 8), ("b_g0", 8), ("b_g1", 8),
                ("emb_g", 8), ("emb_b", 8), ("conv_b", 8), ("cln_g", 8), ("cln_b", 8),
                ("ln1_g", 8), ("ln1_b", 8), ("b_up", 32), ("convw", 8 * 31),
                ("w4", 64), ("b4", 16), ("gb", 64), ("ap", 32)]:
    _V[_nm] = _off
    _off += _n
NV = _off
ROWS = ["emb_g", "emb_b", "b_out", "ln1_g", "ln1_b", "b_down", "ln2_g", "ln2_b"]


def build_nc(debug=False):
    nc = bass.Bass("TRN2", target_bir_lowering=False)
    dt_in = lambda nm, shp: nc.dram_tensor(nm, shp, F32, kind="ExternalInput").ap()
    x = dt_in("x", [T, D])
    w_in = dt_in("w_in", [D, DIN])
    w_ba = dt_in("w_ba", [DR, D])
    w_bb = dt_in("w_bb", [D, D])
    w_out = dt_in("w_out", [D, D])
    w_up = dt_in("w_up", [D, DFF])
    w_dn = dt_in("w_dn", [DFF, D])
    wg_in = dt_in("wg", [BW, 64 * BW])
    vecs_in = dt_in("vecs", [128, NV])
    rows_in = dt_in("rows", [8, 128, D])
    ident_in = dt_in("ident", [128, 128])
    out = nc.dram_tensor("out", [T, D], F32, kind="ExternalOutput").ap()
    kw = {"kind": "ExternalOutput"} if debug else {}
    hT_d = nc.dram_tensor("hT_d", [D, T], BF16, **kw).ap()
    xr_d = nc.dram_tensor("xr_d", [DR, T], F32, **kw).ap()
    gy_d = nc.dram_tensor("gy_d", [DR, T], BF16, **kw).ap()
    c0_d = nc.dram_tensor("c0_d", [D, T], BF16, **kw).ap()
    c1_d = nc.dram_tensor("c1_d", [D, T], F32, **kw).ap()
    v_d = nc.dram_tensor("v_d", [DR, T], BF16, **kw).ap()
    xn1_d = nc.dram_tensor("xn1_d", [T, D], F32, **kw).ap()
    h1T_d = nc.dram_tensor("h1T_d", [D, T], BF16, **kw).ap()
    wup_bf = nc.dram_tensor("wup_bf", [D, DFF], BF16).ap()
    wdn_bf = nc.dram_tensor("wdn_bf", [DFF, D], BF16).ap()
    cst_d = nc.dram_tensor("cst_d", [2, 128, T], F32, **kw).ap()

    with contextlib.ExitStack() as st:
        S = Sched(nc, st)
        ARENA_F32 = 53000
        arena = st.enter_context(nc.sbuf_tensor("arena", [128, ARENA_F32], F32))
        base = nc.lookup_mloc(arena).addr
        ps_all = st.enter_context(nc.psum_tensor("ps_all", [128, 4096], F32))
        Rps = [Res("ps%d" % i) for i in range(8)]

        def bank(i, n=512, p=128):
            return ps_all[0:p, i * 512:i * 512 + n]

        cur = [0]
        cnt = [0]

        class Buf:
            def __init__(self, name, shape, dt, lane=False):
                nbytes = int(np.prod(shape[1:])) * (4 if dt == F32 else 2)
                nbytes = (nbytes + 63) // 64 * 64
                assert cur[0] + nbytes <= ARENA_F32 * 4, (name, cur[0], nbytes)
                cnt[0] += 1
                self.t = nc.alloc_sbuf_tensor_at("%s_%d" % (name, cnt[0]), list(shape), dt,
                                                 offset=base + cur[0])
                cur[0] += nbytes
                self.res = Res(name)
                self.lane = S.dma_lane() if lane else None

            def __getitem__(self, k):
                return self.t[k]

        def mm(o, l, r, start, stop, R, W):
            S.op("pe", lambda E: E.matmul(o, l, r, start=start, stop=stop), R, W)

        def trp(o, i, idn, R, W):
            S.op("pe", lambda E: E.transpose(o, i, idn), R, W)

        def act(o, i, func, R, W, bias=0.0, scale=1.0, self_sync=False):
            S.op("act", lambda E: E.activation(out=o, in_=i, func=func, bias=bias, scale=scale), R, W, self_sync)

        def ts(eng, o, i, s1, s2, op0, op1, R, W):
            if s2 is None:
                S.op(eng, lambda E: E.tensor_scalar(out=o, in0=i, scalar1=s1, scalar2=None, op0=op0), R, W)
            else:
                S.op(eng, lambda E: E.tensor_scalar(out=o, in0=i, scalar1=s1, scalar2=s2, op0=op0, op1=op1), R, W)

        def stt(o, i0, sc, i1, op0, op1, R, W):
            S.op("dve", lambda E: E.scalar_tensor_tensor(out=o, in0=i0, scalar=sc, in1=i1, op0=op0, op1=op1), R, W)

        def tt(eng, o, i0, i1, op, R, W):
            S.op(eng, lambda E: E.tensor_tensor(out=o, in0=i0, in1=i1, op=op), R, W)

        def cp(eng, o, i, R, W):
            S.op(eng, lambda E: E.tensor_copy(out=o, in_=i), R, W)

        def dma(eng, lane, o, i, R=(), W=(), WF=()):
            S.dma(eng, lane, lambda E: E.dma_start(out=o, in_=i), R, W, WF)

        R_hT, R_xr, R_gy, R_c0, R_c1, R_v, R_xn1, R_h1T, R_out = [Res(n) for n in
            ("hT_d", "xr_d", "gy_d", "c0_d", "c1_d", "v_d", "xn1_d", "h1T_d", "out")]
        R_cst = Res("cst_d")

        vecs = Buf("vecs", [128, NV], F32, lane=True)
        ident_f = Buf("ident_f", [128, 128], F32, lane=True)
        ident_b = Buf("ident_b", [128, 128], BF16)
        ones_f = Buf("ones_f", [128, 128], F32)
        hb = Buf("hb", [128, 64], F32)
        hc = Buf("hc", [128, 32], F32)
        tmpc = Buf("tmpc", [128, 32], F32)
        rs_all = Buf("rs_all", [128, 32], F32)
        nm_all = Buf("nm_all", [128, 32], F32)
        persist_end = cur[0]

        dma("sp", vecs.lane, vecs[:, :], vecs_in, W=[vecs.res])
        dma("sp", ident_f.lane, ident_f[:, :], ident_in, W=[ident_f.res])
        cp("dve", ident_b[:, :], ident_f[:, :], [ident_f.res], [ident_b.res])
        S.op("pool", lambda E: E.memset(ones_f[:, :], 1.0), [], [ones_f.res])
        V = lambda nm, j=0, n=1, p=128: vecs[0:p, _V[nm] + j:_V[nm] + j + n]
        ts("dve", hb[0:96, :], V("gb", 0, 64, 96), 0.5, None, ALU.mult, None, [vecs.res], [hb.res])
        act(tmpc[0:96, :], V("ap", 0, 32, 96), AF.Exp, [vecs.res], [tmpc.res], scale=-1.0)
        act(tmpc[0:96, :], tmpc[0:96, :], AF.Ln, [tmpc.res], [tmpc.res], bias=1.0, self_sync=True)
        ts("dve", hc[0:96, :], tmpc[0:96, :], -4.0, None, ALU.mult, None, [tmpc.res], [hc.res])

        def layer_norm_stats(xsrc, nsub, Rsrc, stb, mvb, rstd, nmr):
            for s in range(nsub):
                for hh in range(2):
                    S.op("dve", (lambda o=stb[:, s, hh, :], i=xsrc(s)[:, hh * 512:(hh + 1) * 512]:
                                 lambda E: E.bn_stats(out=o, in_=i))(), Rsrc, [stb.res])
                S.op("dve", (lambda s=s: lambda E: E.bn_aggr(
                    out=mvb[:, s, :], in_=stb[:, s, :, :].rearrange("p a b -> p (a b)")))(), [stb.res], [mvb.res])
            act(rstd[:, 0:nsub], mvb[:, 0:nsub, 1], AF.Sqrt, [mvb.res], [rstd.res], bias=EPS)
            S.op("dve", lambda E: E.reciprocal(out=rstd[:, 0:nsub], in_=rstd[:, 0:nsub]), [rstd.res], [rstd.res])
            stt(nmr[:, 0:nsub], mvb[:, 0:nsub, 0], -1.0, rstd[:, 0:nsub], ALU.mult, ALU.mult,
                [mvb.res, rstd.res], [nmr.res])

        def load_weight(buf, src, nk, ncols, c0, blk, reslist):
            srcv = src.rearrange("(kc p) c -> p kc c", p=128)
            for cb in range(ncols // blk):
                r = Res("wblk")
                reslist.append(r)
                dma("pool", S.dma_lane(), buf[:, 0:nk, cb * blk:(cb + 1) * blk],
                    srcv[:, :, c0 + cb * blk:c0 + (cb + 1) * blk], W=[r])

        psrot = [0]

        def next_bank(lo, hi):
            b = lo + psrot[0] % (hi - lo)
            psrot[0] += 1
            return b

        TT = 512
        w1 = Buf("w1", [128, 8, 5120], BF16)
        w1res = []
        load_weight(w1, w_in, 8, 5120, 0, 1024, w1res)
        xt = [Buf("xt%d" % i, [128, 4, D], F32, lane=True) for i in range(2)]
        xnb = Buf("xnb", [128, 4, D], BF16)
        hTt = [Buf("hTt%d" % i, [128, 8, TT], BF16, lane=True) for i in range(2)]
        xr_st = [Buf("xr_st%d" % i, [128, 6, TT], F32, lane=True) for i in range(2)]
        gy_st = [Buf("gy_st%d" % i, [128, 6, TT], BF16, lane=True) for i in range(2)]
        c0_st = [Buf("c0_st%d" % i, [128, 4, TT], BF16, lane=True) for i in range(2)]
        sg = [Buf("sg%d" % i, [128, TT], F32) for i in range(2)]
        stb = Buf("stb", [128, 4, 2, 6], F32)
        mvb = Buf("mvb", [128, 4, 2], F32)
        rstd = Buf("rstd", [128, 4], F32)
        nmr = Buf("nmr", [128, 4], F32)

        xv = x.rearrange("(n s p) f -> n p s f", s=4, p=128)
        NT1 = T // TT
        def p1_prepA(it):
            xb = xt[it % 2]
            layer_norm_stats(lambda s: xb[:, s, :], 4, [xb.res], stb, mvb, rstd, nmr)
            cp("dve", rs_all[:, it * 4:it * 4 + 4], rstd[:, 0:4], [rstd.res], [rs_all.res])
            cp("dve", nm_all[:, it * 4:it * 4 + 4], nmr[:, 0:4], [nmr.res], [nm_all.res])
            for s in range(4):
                act(xnb[:, s, :], xb[:, s, :], AF.Identity, [xb.res, rstd.res, nmr.res], [xnb.res],
                    bias=nmr[:, s:s + 1], scale=rstd[:, s:s + 1])

        def p1_prepB(it):
            hp = hTt[it % 2]
            for kc in range(8):
                b = next_bank(0, 2)
                pst = bank(b).bitcast(BF16)
                for s in range(4):
                    trp(pst[:, s * 128:(s + 1) * 128], xnb[:, s, kc * 128:(kc + 1) * 128], ident_b[:, :],
                        [xnb.res, ident_b.res], [Rps[b]])
                act(hp[:, kc, :], pst[:, 0:TT], AF.Identity, [Rps[b], vecs.res], [hp.res],
                    bias=V("emb_b", kc), scale=V("emb_g", kc))
            dma("sp", hp.lane, hT_d.rearrange("(kc p) t -> p kc t", p=128)[:, :, it * TT:(it + 1) * TT],
                hp[:, :, :], R=[hp.res], WF=[R_hT])

        dma("sp", xt[0].lane, xt[0][:, :, :], xv[0], W=[xt[0].res])
        dma("sp", xt[1].lane, xt[1][:, :, :], xv[1], W=[xt[1].res])
        p1_prepA(0)
        p1_prepB(0)
        for it in range(NT1):
            hb_ = hTt[it % 2]
            if it + 1 < NT1:
                p1_prepA(it + 1)

            def proj(col0, b):
                for kc in range(8):
                    mm(bank(b), w1[:, kc, col0:col0 + 128], hb_[:, kc, :], kc == 0, kc == 7,
                       [w1res[col0 // 1024], hb_.res], [Rps[b]])
            for c in range(12):
                b = next_bank(2, 8)
                proj(c * 128, b)
                stg = xr_st[c // 6]
                ts("dve", stg[:, c % 6, :], bank(b), V("b_xr", c), None, ALU.add, None,
                   [Rps[b], vecs.res], [stg.res])
                if c % 6 == 5:
                    dma("sp", stg.lane,
                        xr_d.rearrange("(kc p) t -> p kc t", p=128)[:, c - 5:c + 1, it * TT:(it + 1) * TT],
                        stg[:, :, :], R=[stg.res], WF=[R_xr])
            if it + 1 < NT1:
                p1_prepB(it + 1)
            if it + 2 < NT1:
                nb_ = xt[it % 2]
                dma("sp", nb_.lane, nb_[:, :, :], xv[it + 2], W=[nb_.res])
            for c in range(12):
                b = next_bank(2, 8)
                proj(DR + c * 128, b)
                stg = gy_st[c // 6]
                act(stg[:, c % 6, :], bank(b), AF.Gelu_apprx_tanh, [Rps[b], vecs.res], [stg.res],
                    bias=V("b_yr", c))
                if c % 6 == 5:
                    dma("sp", stg.lane,
                        gy_d.rearrange("(kc p) t -> p kc t", p=128)[:, c - 5:c + 1, it * TT:(it + 1) * TT],
                        stg[:, :, :], R=[stg.res], WF=[R_gy])
            for j in range(8):
                ba = next_bank(2, 8)
                proj(2 * DR + j * 128, ba)
                bb = next_bank(2, 8)
                proj(2 * DR + D + j * 128, bb)
                sgb = sg[j % 2]
                act(sgb[:, :], bank(bb), AF.Sigmoid, [Rps[bb], vecs.res], [sgb.res], bias=V("b_xcb", j))
                stg = c0_st[j // 4]
                stt(stg[:, j % 4, :], bank(ba), V("b_xca", j), sgb[:, :], ALU.add, ALU.mult,
                    [Rps[ba], sgb.res, vecs.res], [stg.res])
                if j % 4 == 3:
                    dma("sp", stg.lane,
                        c0_d.rearrange("(kc p) t -> p kc t", p=128)[:, j - 3:j + 1, it * TT:(it + 1) * TT],
                        stg[:, :, :], R=[stg.res], WF=[R_c0])

        S.barrier()
        cur[0] = persist_end
        wgb = Buf("wgb", [128, 64 * BW], BF16, lane=True)
        dma("pool", wgb.lane, wgb[0:BW, :], wg_in, W=[wgb.res])
        xrs = [Buf("xrs0", [128, T + 4], F32, lane=True)]
        gyb = Buf("gyb", [128, T], BF16, lane=True)
        ubs = [Buf("ub%d" % i, [128, T], F32) for i in range(2)]
        ubfs = [Buf("ubf%d" % i, [128, T], BF16) for i in range(2)]
        TRb = [Buf("TR%d" % d, [128, T], F32) for d in range(2)]
        TIb = [Buf("TI%d" % d, [128, T], F32) for d in range(2)]
        B0s = [Buf("B0_%d" % i, [128, T], F32) for i in range(2)]
        B1 = Buf("B1", [128, T], F32)
        D4 = [Buf("D4_%d" % i, [128, 4, BW], F32) for i in range(2)]
        P = BW
        xs = xrs[0]
        S.op("pool", lambda E: E.memset(xs[:, 0:2], 0.0), [], [xs.res])
        S.op("pool", lambda E: E.memset(xs[:, T + 2:T + 4], 0.0), [], [xs.res])

        def p2_loadx(n):
            dma("sp", xs.lane, xs[0:P, 2:T + 2], xr_d[n * BW:(n + 1) * BW, :], R=[R_xr], W=[xs.res])

        def p2_conv(n, half):
            ub, ubf, d4 = ubs[n % 2], ubfs[n % 2], D4[n % 2]
            if half == 0:
                for k in range(4):
                    act(d4[0:P, k, :], ident_f[0:P, 0:P], AF.Copy, [ident_f.res, vecs.res], [d4.res],
                        scale=V("w4", n * 4 + k, 1, P))
            for q in range(2 * half, 2 * half + 2):
                b2 = 2 * (next_bank(0, 4))
                for hh in range(2):
                    t0_ = q * 1024 + hh * 512
                    for k in range(4):
                        mm(bank(b2 + hh, 512, P), d4[0:P, k, :], xs[0:P, t0_ + k:t0_ + k + 512], k == 0, k == 3,
                           [d4.res, xs.res], [Rps[b2 + hh]])
                src = ps_all[0:P, b2 * 512:b2 * 512 + 1024]
                ts("dve", ub[0:P, q * 1024:(q + 1) * 1024], src, V("b4", n, 1, P), None, ALU.add, None,
                   [Rps[b2], Rps[b2 + 1], vecs.res], [ub.res])
                ts("dve", ubf[0:P, q * 1024:(q + 1) * 1024], src, V("b4", n, 1, P), None, ALU.add, None,
                   [Rps[b2], Rps[b2 + 1], vecs.res], [ubf.res])
            if half == 1 and n + 1 < NB:
                p2_loadx(n + 1)

        def p2_gates(n, d):
            ubf = ubfs[n % 2]
            for q in range(4):
                for g in range(2):
                    b2 = 2 * (next_bank(0, 4))
                    idx = (d * 2 + g) * NB + n
                    for hh in range(2):
                        mm(bank(b2 + hh, 512, P), wgb[0:P, idx * BW:(idx + 1) * BW],
                           ubf[0:P, q * 1024 + hh * 512:q * 1024 + (hh + 1) * 512], True, True,
                           [wgb.res, ubf.res], [Rps[b2 + hh]])
                    dst = (TRb if g == 0 else TIb)[d]
                    act(dst[0:P, q * 1024:(q + 1) * 1024], ps_all[0:P, b2 * 512:b2 * 512 + 1024], AF.Tanh,
                        [Rps[b2], Rps[b2 + 1], hb.res], [dst.res], bias=hb[0:P, idx:idx + 1], scale=0.5)

        def p2_dir(n, d):
            Bd = B0s[n % 2] if d == 0 else B1
            ub = ubs[n % 2]
            hcv = hc[0:P, d * NB + n:d * NB + n + 1]
            act(TRb[d][0:P, :], TRb[d][0:P, :], AF.Exp, [TRb[d].res, hc.res], [TRb[d].res], bias=hcv, scale=hcv)
            act(Bd[0:P, :], TRb[d][0:P, :], AF.Square, [TRb[d].res], [Bd.res])
            act(Bd[0:P, :], Bd[0:P, :], AF.Sqrt, [Bd.res], [Bd.res], bias=0.25, scale=-0.25)
            stt(TIb[d][0:P, :], TIb[d][0:P, :], 1.0, ub[0:P, :], ALU.add, ALU.mult,
                [TIb[d].res, ub.res], [TIb[d].res])
            tt("dve", TIb[d][0:P, :], TIb[d][0:P, :], Bd[0:P, :], ALU.mult,
               [TIb[d].res, Bd.res], [TIb[d].res])
            f = (lambda a: a) if d == 0 else rev
            S.op("dve", (lambda o=f(Bd[0:P, 0:T]), a_=f(TRb[d][0:P, 0:T]), b_=f(TIb[d][0:P, 0:T]):
                         lambda E: E.tensor_tensor_scan(out=o, data0=a_, data1=b_, initial=0.0,
                                                        op0=ALU.mult, op1=ALU.add))(),
                 [TRb[d].res, TIb[d].res, Bd.res], [Bd.res])

        p2_loadx(0)
        p2_conv(0, 0)
        p2_conv(0, 1)
        for n in range(NB):
            dma("sp", gyb.lane, gyb[0:P, :], gy_d[n * BW:(n + 1) * BW, :], R=[R_gy], W=[gyb.res])
            p2_gates(n, 0)
            if n + 1 < NB:
                p2_conv(n + 1, 0)
            p2_dir(n, 0)
            p2_gates(n, 1)
            if n + 1 < NB:
                p2_conv(n + 1, 1)
            p2_dir(n, 1)
            tt("pool", TIb[1][0:P, :], B0s[n % 2][0:P, :], B1[0:P, :], ALU.add,
               [B0s[n % 2].res, B1.res], [TIb[1].res])
            tt("pool", gyb[0:P, :], TIb[1][0:P, :], gyb[0:P, :], ALU.mult, [TIb[1].res, gyb.res], [gyb.res])
            dma("sp", gyb.lane, v_d[n * BW:(n + 1) * BW, :], gyb[0:P, :], R=[gyb.res], WF=[R_v])

        S.barrier()
        cur[0] = persist_end
        wou = Buf("wou", [128, 8, D], BF16)
        dead_base = cur[0]
        wgl = Buf("wgl", [128, 8, 2048], BF16)
        wbb = Buf("wbb", [128, 8, D], BF16)
        wba = Buf("wba", [128, 12, D], BF16)
        wgl_r, wbb_r, wba_r, wou_r = [], [], [], []
        load_weight(wgl, w_in, 8, 2048, 5120, 1024, wgl_r)
        load_weight(wbb, w_bb, 8, D, 0, 1024, wbb_r)
        load_weight(wba, w_ba, 12, D, 0, 1024, wba_r)
        load_weight(wou, w_out, 8, D, 0, 512, wou_r)
        w3_end = cur[0]
        R_wbf = Res("w_bf16")
        for rb in range(4):
            dma("pool", S.dma_lane(), wup_bf[rb * 256:(rb + 1) * 256, :], w_up[rb * 256:(rb + 1) * 256, :], WF=[R_wbf])
        for rb in range(4):
            dma("pool", S.dma_lane(), wdn_bf[rb * 1024:(rb + 1) * 1024, :], w_dn[rb * 1024:(rb + 1) * 1024, :], WF=[R_wbf])
        cxb = [Buf("cx%d" % i, [128, T + 32], BF16, lane=True) for i in range(2)]
        Dg = [Buf("Dg%d" % i, [128, 31, 128], BF16) for i in range(2)]
        c1s = [Buf("c1s%d" % i, [128, T], F32, lane=True) for i in range(2)]
        acc1 = Buf("acc1", [128, T], F32, lane=True)
        acc2 = Buf("acc2", [128, T], F32, lane=True)
        sqc = Buf("sqc", [128, T], F32)
        for i in range(2):
            S.op("pool", (lambda i=i: lambda E: E.memset(cxb[i][:, 0:15], 0.0))(), [], [cxb[i].res])
            S.op("pool", (lambda i=i: lambda E: E.memset(cxb[i][:, T + 15:T + 32], 0.0))(), [], [cxb[i].res])

        NPE = 28

        def build_diag(j):
            dg_ = Dg[j % 2]
            ia = ident_f[:, :]
            wa = V("convw", j * 31, NPE)
            pstep_i = ia.ap[0][0]
            pstep_w = wa.ap[0][0]
            in0 = AP(ia.tensor, ia.offset, [[pstep_i, 128], [0, NPE], [1, 128]])
            in1 = AP(wa.tensor, wa.offset, [[pstep_w, 128], [1, NPE], [0, 128]])
            tt("dve", dg_[:, 0:NPE, :], in0, in1, ALU.mult, [ident_f.res, vecs.res], [dg_.res])

        def p2c_load(j):
            cb_ = cxb[j % 2]
            dma("sp", cb_.lane, cb_[:, 15:T + 15], c0_d[j * 128:(j + 1) * 128, :], R=[R_c0], W=[cb_.res])
        p2c_load(0)
        build_diag(0)
        for j in range(8):
            cb_ = cxb[j % 2]
            dg = Dg[j % 2]
            co = c1s[j % 2]
            if j + 1 < 8:
                p2c_load(j + 1)
                build_diag(j + 1)
            ts("dve", co[:, :], cb_[:, NPE:NPE + T], V("convw", j * 31 + NPE), None, ALU.mult, None,
               [cb_.res, vecs.res], [co.res])
            for k in range(NPE + 1, 31):
                stt(co[:, :], cb_[:, k:k + T], V("convw", j * 31 + k), co[:, :], ALU.mult, ALU.add,
                    [cb_.res, co.res, vecs.res], [co.res])
            for tt_ in range(8):
                b = next_bank(0, 8)
                for k in range(NPE):
                    mm(bank(b), dg[:, k, :], cb_[:, tt_ * 512 + k:tt_ * 512 + k + 512], k == 0, k == NPE - 1,
                       [dg.res, cb_.res], [Rps[b]])
                tsl = slice(tt_ * 512, (tt_ + 1) * 512)
                stt(co[:, tsl], bank(b), V("conv_b", j), co[:, tsl], ALU.add, ALU.add,
                    [Rps[b], co.res, vecs.res], [co.res])
            dma("sp", co.lane, c1_d[j * 128:(j + 1) * 128, :], co[:, :], R=[co.res], WF=[R_c1])
            if j == 0:
                cp("pool", acc1[:, :], co[:, :], [co.res], [acc1.res])
                act(acc2[:, :], co[:, :], AF.Square, [co.res], [acc2.res])
            else:
                tt("pool", acc1[:, :], acc1[:, :], co[:, :], ALU.add, [acc1.res, co.res], [acc1.res])
                act(sqc[:, :], co[:, :], AF.Square, [co.res], [sqc.res])
                tt("pool", acc2[:, :], acc2[:, :], sqc[:, :], ALU.add, [acc2.res, sqc.res], [acc2.res])
        for tt_ in range(8):
            tsl = slice(tt_ * 512, (tt_ + 1) * 512)
            b1_ = next_bank(0, 8)
            b2_ = next_bank(0, 8)
            mm(bank(b1_), ones_f[:, :], acc1[:, tsl], True, True, [ones_f.res, acc1.res], [Rps[b1_]])
            mm(bank(b2_), ones_f[:, :], acc2[:, tsl], True, True, [ones_f.res, acc2.res], [Rps[b2_]])
            ts("dve", sqc[:, tsl], bank(b1_), 1.0 / D, None, ALU.mult, None, [Rps[b1_], acc1.res], [sqc.res])
            tt("dve", c1s[1][:, tsl], sqc[:, tsl], sqc[:, tsl], ALU.mult, [sqc.res], [c1s[1].res])
            stt(c1s[0][:, tsl], bank(b2_), 1.0 / D, c1s[1][:, tsl], ALU.mult, ALU.subtract,
                [Rps[b2_], c1s[1].res, acc2.res], [c1s[0].res])
        act(c1s[0][:, :], c1s[0][:, :], AF.Sqrt, [c1s[0].res], [c1s[0].res], bias=EPS)
        S.op("dve", lambda E: E.reciprocal(out=c1s[0][:, :], in_=c1s[0][:, :]), [c1s[0].res], [c1s[0].res])
        dma("sp", acc1.lane, cst_d[0], sqc[:, :], R=[sqc.res], WF=[R_cst])
        dma("sp", acc2.lane, cst_d[1], c1s[0][:, :], R=[c1s[0].res], WF=[R_cst])

        S.barrier()
        cur[0] = w3_end
        TT3 = 256
        NS = TT3 // 128
        hT3s = [Buf("hT3_%d" % i, [128, 8, TT3], BF16, lane=True) for i in range(2)]
        v3s = [Buf("v3_%d" % i, [128, 12, TT3], BF16, lane=True) for i in range(2)]
        c13 = Buf("c13", [128, 8, TT3], F32, lane=True)
        mean3 = Buf("mean3", [128, TT3], F32, lane=True)
        rstd3 = Buf("rstd3", [128, TT3], F32, lane=True)
        xnt = [Buf("xnt%d" % i, [128, TT3], F32) for i in range(2)]
        yt = [Buf("yt%d" % i, [128, TT3], F32) for i in range(2)]
        tht = [Buf("tht%d" % i, [128, TT3], F32) for i in range(2)]
        cBs = [Buf("cB%d" % i, [128, 8, TT3], BF16) for i in range(2)]
        t0b = [Buf("t0_%d" % i, [128, TT3], F32) for i in range(2)]
        t1b = [Buf("t1_%d" % i, [128, TT3], F32) for i in range(2)]
        u1b = [Buf("u1_%d" % i, [128, TT3], F32) for i in range(2)]
        u2b = [Buf("u2_%d" % i, [128, TT3], F32) for i in range(2)]
        zTs = [Buf("zT%d" % i, [128, 8, TT3], BF16) for i in range(2)]
        assert cur[0] - 8 * TT3 * 2 == dead_base + 2 * 8 * DFF * 2, (cur[0], dead_base)
        rows = {}
        for nm in ("G1", "B1"):
            rows[nm] = Buf("row_" + nm, [128, D], F32, lane=True)
        rtmp = Buf("xn0", [128, D], F32, lane=True)
        ri = {n: i for i, n in enumerate(ROWS)}

        def make_rows(gn, bn, gsrc, bsrc, addsrc):
            dma("sp", rows[gn].lane, rows[gn][:, :], rows_in[ri[gsrc]], W=[rows[gn].res])
            ts("dve", rows[gn][:, :], rows[gn][:, :], ALPHA, None, ALU.mult, None, [rows[gn].res], [rows[gn].res])
            dma("sp", rows[bn].lane, rows[bn][:, :], rows_in[ri[bsrc]], W=[rows[bn].res])
            dma("sp", rtmp.lane, rtmp[:, :], rows_in[ri[addsrc]], W=[rtmp.res])
            stt(rows[bn][:, :], rows[bn][:, :], ALPHA, rtmp[:, :], ALU.mult, ALU.add,
                [rows[bn].res, rtmp.res], [rows[bn].res])

        halfv = Buf("halfv", [128, 32], F32)
        ts("dve", halfv[:, 0:16], V("cln_g", 0, 16), 0.5, None, ALU.mult, None, [vecs.res], [halfv.res])
        ts("dve", halfv[:, 16:32], V("b_g0", 0, 16), 0.5, None, ALU.mult, None, [vecs.res], [halfv.res])
        mhalf = Buf("mhalf", [128, TT3], F32)
        S.op("pool", lambda E: E.memset(mhalf[:, :], -0.5), [], [mhalf.res])
        xC = Buf("xC", [128, NS, D], F32, lane=True)
        xn0 = rtmp
        tbss = [Buf("tbs%d" % i, [128, D], F32) for i in range(2)]
        xn1 = [Buf("xn1_%d" % i, [128, D], F32, lane=True) for i in range(2)]
        xn1bs = [Buf("xn1b%d" % i, [128, NS, D], BF16) for i in range(2)]
        h1T = [Buf("h1T%d" % i, [128, 8, TT3], BF16, lane=True) for i in range(2)]
        stb4 = Buf("stb4", [128, 4, 2, 6], F32)
        mvb4 = Buf("mvb4", [128, 4, 2], F32)
        vt4 = Buf("vt4", [128, 4], F32)
        rs1 = Buf("rs1", [128, 4], F32)
        nm1 = Buf("nm1", [128, 4], F32)

        xv3 = x.rearrange("(n s p) f -> n p s f", s=NS, p=128)
        hTv = hT_d.rearrange("(kc p) t -> p kc t", p=128)
        vv = v_d.rearrange("(kc p) t -> p kc t", p=128)
        c1v = c1_d.rearrange("(kc p) t -> p kc t", p=128)
        h1Tv = h1T_d.rearrange("(kc p) t -> p kc t", p=128)
        NT3 = T // TT3
        BS, BQ = 4, 5

        def rstd_pow(dst, var_ap, tmp, n, Rin, Rtmp, Rdst):
            ts("dve", tmp, var_ap, EPS, None, ALU.add, None, Rin, [Rtmp])
            tt("pool", dst, tmp, mhalf[:, 0:n], ALU.pow, [Rtmp, mhalf.res], [Rdst])

        def bcast_row(row, it):
            return cst_d[row][:, it * TT3:(it + 1) * TT3]

        def load_A(it):
            tsl = slice(it * TT3, (it + 1) * TT3)
            dma("sp", c13.lane, c13[:, :, :], c1v[:, :, tsl], R=[R_c1], W=[c13.res])
            dma("sp", mean3.lane, mean3[:, :], bcast_row(0, it), R=[R_cst], W=[mean3.res])
            dma("sp", rstd3.lane, rstd3[:, :], bcast_row(1, it), R=[R_cst], W=[rstd3.res])

        def load_B(it):
            tsl = slice(it * TT3, (it + 1) * TT3)
            h_, v_ = hT3s[it % 2], v3s[it % 2]
            dma("sp", h_.lane, h_[:, :, :], hTv[:, :, tsl], R=[R_hT], W=[h_.res])
            dma("sp", v_.lane, v_[:, :, :], vv[:, :, tsl], R=[R_v], W=[v_.res])

        def stage_A(it):
            cB = cBs[it % 2]
            for i in range(8 + 2):
                if i < 8:
                    kc = i
                    xb_ = xnt[kc % 2]
                    tt("dve", xb_[:, :], c13[:, kc, :], mean3[:, :], ALU.subtract, [c13.res, mean3.res], [xb_.res])
                    tt("dve", xb_[:, :], xb_[:, :], rstd3[:, :], ALU.mult, [xb_.res, rstd3.res], [xb_.res])
                if 1 <= i < 9:
                    kc = i - 1
                    xb_, yb_, th_ = xnt[kc % 2], yt[kc % 2], tht[kc % 2]
                    act(th_[:, :], xb_[:, :], AF.Tanh, [xb_.res, halfv.res], [th_.res],
                        bias=halfv[:, 8 + kc:9 + kc], scale=halfv[:, kc:kc + 1])
                    ts("dve", yb_[:, :], xb_[:, :], V("cln_g", kc), V("cln_b", kc), ALU.mult, ALU.add,
                       [xb_.res, vecs.res], [yb_.res])
                if 2 <= i:
                    kc = i - 2
                    yb_, th_ = yt[kc % 2], tht[kc % 2]
                    stt(cB[:, kc, :], th_[:, :], 1.0, yb_[:, :], ALU.add, ALU.mult, [th_.res, yb_.res], [cB.res])
                yield
            if it + 1 < NT3:
                load_A(it + 1)

        def stage_B(it):
            cB, zT = cBs[it % 2], zTs[it % 2]
            hT3, v3 = hT3s[it % 2], v3s[it % 2]
            if it + 1 < NT3:
                load_B(it + 1)

            def merge(oc):
                Y = 2 * (oc % 2) + 1
                RY = Rps[Y]
                s0, s1, u1, u2 = t0b[oc % 2], t1b[oc % 2], u1b[oc % 2], u2b[oc % 2]
                stt(u1[:, :], s0[:, :], 1.0, bank(Y)[:, 0:TT3], ALU.add, ALU.mult, [s0.res, RY], [u1.res])
                stt(u2[:, :], s1[:, :], 1.0, bank(Y)[:, TT3:2 * TT3], ALU.add, ALU.mult, [s1.res, RY], [u2.res])
                stt(zT[:, oc, :], u2[:, :], 0.5, u1[:, :], ALU.mult, ALU.add, [u1.res, u2.res], [zT.res])

            for oc in range(8):
                X = 2 * (oc % 2)
                Y = X + 1
                RX, RY = Rps[X], Rps[Y]
                for kc in range(8):
                    mm(bank(X)[:, 0:TT3], wgl[:, kc, oc * 128:(oc + 1) * 128], hT3[:, kc, :], kc == 0, kc == 7,
                       [wgl_r[0], hT3.res], [RX])
                for kc in range(8):
                    mm(bank(X)[:, TT3:2 * TT3], wgl[:, kc, D + oc * 128:D + (oc + 1) * 128], hT3[:, kc, :],
                       kc == 0, kc == 7, [wgl_r[1], hT3.res], [RX])
                for kc in range(12):
                    mm(bank(Y)[:, 0:TT3], wba[:, kc, oc * 128:(oc + 1) * 128], v3[:, kc, :], kc == 0, kc == 11,
                       [wba_r[0], v3.res], [RY])
                for kc in range(8):
                    mm(bank(Y)[:, TT3:2 * TT3], wbb[:, kc, oc * 128:(oc + 1) * 128], cB[:, kc, :], kc == 0, kc == 7,
                       [wbb_r[0], cB.res], [RY])
                s0, s1 = t0b[oc % 2], t1b[oc % 2]
                act(s0[:, :], bank(X)[:, 0:TT3], AF.Tanh, [RX, halfv.res], [s0.res],
                    bias=halfv[:, 16 + oc:17 + oc], scale=0.5)
                act(s1[:, :], bank(X)[:, TT3:2 * TT3], AF.Tanh, [RX, halfv.res], [s1.res],
                    bias=halfv[:, 24 + oc:25 + oc], scale=0.5)
                if oc >= 1:
                    merge(oc - 1)
                yield
            merge(7)
            yield

        def stage_C(it, use_pool=True):
            zT = zTs[it % 2]
            xn1b = xn1bs[it % 2]
            meng = "pool" if use_pool else "dve"
            for s in range(NS):
                tbs = tbss[s]
                col = it * NS + s
                act(xn0[:, :], xC[:, s, :], AF.Identity, [xC.res, rs_all.res, nm_all.res], [xn0.res],
                    bias=nm_all[:, col:col + 1], scale=rs_all[:, col:col + 1])
                tt(meng, tbs[:, :], xn0[:, :], rows["G1"][:, :], ALU.mult, [xn0.res, rows["G1"].res], [tbs.res])
                tt("dve", tbs[:, :], tbs[:, :], rows["B1"][:, :], ALU.add, [tbs.res, rows["B1"].res], [tbs.res])
                yield
            if it + 1 < NT3:
                dma("sp", xC.lane, xC[:, :, :], xv3[it + 1], W=[xC.res])
            for s in range(NS):
                tbs = tbss[s]
                for hh in range(2):
                    b = 4 + 2 * s + hh
                    for kc in range(8):
                        mm(bank(b), zT[:, kc, s * 128:(s + 1) * 128], wou[:, kc, hh * 512:(hh + 1) * 512],
                           kc == 0, kc == 7, [zT.res, wou_r[hh]], [Rps[b]])
                    stt(tbs[:, hh * 512:(hh + 1) * 512], bank(b), 0.5, tbs[:, hh * 512:(hh + 1) * 512],
                        ALU.mult, ALU.add, [tbs.res, Rps[b]], [tbs.res])
                for hh in range(2):
                    S.op("dve", (lambda o=stb4[:, s, hh, :], i=tbs[:, hh * 512:(hh + 1) * 512]:
                                 lambda E: E.bn_stats(out=o, in_=i))(), [tbs.res], [stb4.res])
                S.op("dve", (lambda o=mvb4[:, s, :], i=stb4[:, s, :, :].rearrange("p a b -> p (a b)"):
                             lambda E: E.bn_aggr(out=o, in_=i))(), [stb4.res], [mvb4.res])
                yield
            if use_pool:
                rstd_pow(rs1[:, 0:NS], mvb4[:, 0:NS, 1], vt4[:, 0:NS], NS, [mvb4.res], vt4.res, rs1.res)
            else:
                act(rs1[:, 0:NS], mvb4[:, 0:NS, 1], AF.Sqrt, [mvb4.res], [rs1.res], bias=EPS)
                S.op("dve", lambda E: E.reciprocal(out=rs1[:, 0:NS], in_=rs1[:, 0:NS]), [rs1.res], [rs1.res])
            stt(nm1[:, 0:NS], mvb4[:, 0:NS, 0], -1.0, rs1[:, 0:NS], ALU.mult, ALU.mult,
                [mvb4.res, rs1.res], [nm1.res])
            yield
            for s in range(NS):
                tbs = tbss[s]
                x1 = xn1[s % 2]
                act(xn1b[:, s, :], tbs[:, :], AF.Identity, [tbs.res, rs1.res, nm1.res], [xn1b.res],
                    bias=nm1[:, s:s + 1], scale=rs1[:, s:s + 1])
                act(x1[:, :], tbs[:, :], AF.Identity, [tbs.res, rs1.res, nm1.res], [x1.res],
                    bias=nm1[:, s:s + 1], scale=rs1[:, s:s + 1])
                dma("sp", x1.lane, xn1_d[it * TT3 + s * 128:it * TT3 + (s + 1) * 128, :], x1[:, :],
                    R=[x1.res], WF=[R_xn1])
                yield

        def stage_D(it):
            xn1b = xn1bs[it % 2]
            ho = h1T[it % 2]
            for kc in range(8):
                b = 4 + kc % 4
                pst = bank(b).bitcast(BF16)
                for s in range(NS):
                    trp(pst[:, s * 128:(s + 1) * 128], xn1b[:, s, kc * 128:(kc + 1) * 128], ident_b[:, :],
                        [xn1b.res, ident_b.res], [Rps[b]])
                act(ho[:, kc, :], pst[:, 0:TT3], AF.Identity, [Rps[b], vecs.res], [ho.res],
                    bias=V("ln1_b", kc), scale=V("ln1_g", kc))
                if kc % 2 == 1:
                    yield
            dma("sp", ho.lane, h1Tv[:, :, it * TT3:(it + 1) * TT3], ho[:, :, :], R=[ho.res], WF=[R_h1T])

        def drive(primary, others):
            live = [g for g in others if g is not None]
            for _ in primary:
                for g in list(live):
                    for _k in range(2):
                        try:
                            next(g)
                        except StopIteration:
                            if g in live:
                                live.remove(g)
                            break
            for g in live:
                for _ in g:
                    pass

        load_A(0)
        load_B(0)
        dma("sp", xC.lane, xC[:, :, :], xv3[0], W=[xC.res])
        make_rows("G1", "B1", "emb_g", "emb_b", "b_out")
        for _ in stage_A(0):
            pass
        for it in range(NT3):
            gA = stage_A(it + 1) if it + 1 < NT3 else None
            gC = stage_C(it - 1) if it >= 1 else None
            gD = stage_D(it - 2) if it >= 2 else None
            drive(stage_B(it), [gC, gD, gA])
        assert (NT3 - 1) % 2 == 1
        p3b_mark = cur[0]
        cur[0] = dead_base
        wup = Buf("wup", [128, 8, DFF], BF16)
        wdn = Buf("wdn", [128, 32, D], BF16)
        ffn_w_end = cur[0]
        cur[0] = p3b_mark
        dead = [wgl_r[0], wgl_r[1], wbb_r[0], wba_r[0], c13.res, mean3.res, rstd3.res, zTs[0].res]
        for lst in (hT3s, v3s, xnt, yt, tht, cBs, t0b, t1b, u1b, u2b):
            dead += [b_.res for b_ in lst]
        wup_r, wdn_r = [], []
        srcv_up = wup_bf.rearrange("(kc p) c -> p kc c", p=128)
        for cb in range(4):
            r = Res("wupblk")
            wup_r.append(r)
            dma("sp", S.dma_lane(), wup[:, 0:8, cb * 1024:(cb + 1) * 1024],
                srcv_up[:, :, cb * 1024:(cb + 1) * 1024], R=[R_wbf], W=[r] + dead)
        srcv_dn = wdn_bf.rearrange("(kc p) c -> p kc c", p=128)
        for cb in range(2):
            r = Res("wdnblk")
            wdn_r.append(r)
            dma("sp", S.dma_lane(), wdn[:, 0:32, cb * 512:(cb + 1) * 512],
                srcv_dn[:, :, cb * 512:(cb + 1) * 512], R=[R_wbf], W=[r] + dead)
        drive(stage_C(NT3 - 1), [stage_D(NT3 - 2)])
        for _ in stage_D(NT3 - 1):
            pass

        S.barrier()
        cur[0] = persist_end
        rows = {}
        for nm in ("G2", "B2", "ln2_g", "ln2_b"):
            rows[nm] = Buf("row_" + nm, [128, D], F32, lane=True)
        assert cur[0] <= dead_base
        cur[0] = ffn_w_end
        rtmp = Buf("rtmp2", [128, D], F32, lane=True)
        h4 = [Buf("h4_%d" % i, [128, 8, TT3], BF16, lane=True) for i in range(2)]
        x4 = [Buf("x4_%d" % i, [128, NS, D], F32, lane=True) for i in range(1)]
        mT = Buf("mT", [128, 32, TT3], BF16)
        mT_r = [Res("mT%d" % i) for i in range(32)]
        rl = [Buf("rl%d" % i, [128, TT3], F32) for i in range(2)]
        t4 = [Buf("t4_%d" % i, [128, D], F32) for i in range(1)]
        o4 = [Buf("o4_%d" % i, [128, D], F32, lane=True) for i in range(2)]
        stb5 = Buf("stb5", [128, 4, 2, 6], F32)
        mvb5 = Buf("mvb5", [128, 4, 2], F32)
        rs2 = Buf("rs2", [128, 4], F32)
        nm2 = Buf("nm2", [128, 4], F32)
        xn1v = xn1_d.rearrange("(n s p) f -> n p s f", s=NS, p=128)

        def p4_load_early(it):
            sl = it % 2
            dma("sp", h4[sl].lane, h4[sl][:, :, :], h1Tv[:, :, it * TT3:(it + 1) * TT3], R=[R_h1T], W=[h4[sl].res])

        def p4_load_late(it):
            dma("sp", x4[0].lane, x4[0][:, :, :], xn1v[it], R=[R_xn1], W=[x4[0].res])
        p4_load_early(0)
        p4_load_late(0)
        make_rows("G2", "B2", "ln1_g", "ln1_b", "b_down")
        dma("sp", rows["ln2_g"].lane, rows["ln2_g"][:, :], rows_in[ri["ln2_g"]], W=[rows["ln2_g"].res])
        dma("sp", rows["ln2_b"].lane, rows["ln2_b"][:, :], rows_in[ri["ln2_b"]], W=[rows["ln2_b"].res])
        oi = 0
        for it in range(NT3):
            sl = it % 2
            if it + 1 < NT3:
                p4_load_early(it + 1)
            ht, xt4 = h4[sl], x4[0]
            for fc in range(32):
                b = next_bank(0, 8)
                for kc in range(8):
                    mm(bank(b, TT3), wup[:, kc, fc * 128:(fc + 1) * 128], ht[:, kc, :], kc == 0, kc == 7,
                       [wup_r[(fc * 128) // 1024], ht.res], [Rps[b]])
                r_ = rl[fc % 2]
                ts("dve", r_[:, :], bank(b, TT3), V("b_up", fc), 0.0, ALU.add, ALU.max, [Rps[b], vecs.res], [r_.res])
                act(mT[:, fc, :], r_[:, :], AF.Square, [r_.res], [mT_r[fc]])
            for s in range(NS):
                tbb = t4[0]
                tt("pool", tbb[:, :], xt4[:, s, :], rows["G2"][:, :], ALU.mult, [xt4.res, rows["G2"].res], [tbb.res])
                tt("pool", tbb[:, :], tbb[:, :], rows["B2"][:, :], ALU.add, [tbb.res, rows["B2"].res], [tbb.res])
                for hh in range(2):
                    b = next_bank(0, 8)
                    for fc in range(32):
                        mm(bank(b), mT[:, fc, s * 128:(s + 1) * 128], wdn[:, fc, hh * 512:(hh + 1) * 512],
                           fc == 0, fc == 31, [mT_r[fc], wdn_r[hh]], [Rps[b]])
                    tt("dve", tbb[:, hh * 512:(hh + 1) * 512], tbb[:, hh * 512:(hh + 1) * 512], bank(b), ALU.add,
                       [tbb.res, Rps[b]], [tbb.res])
                for hh in range(2):
                    S.op("dve", (lambda s=s, hh=hh, tbb=tbb: lambda E: E.bn_stats(
                        out=stb5[:, s, hh, :], in_=tbb[:, hh * 512:(hh + 1) * 512]))(), [tbb.res], [stb5.res])
                S.op("dve", (lambda s=s: lambda E: E.bn_aggr(
                    out=mvb5[:, s, :], in_=stb5[:, s, :, :].rearrange("p a b -> p (a b)")))(), [stb5.res], [mvb5.res])
                act(rs2[:, s:s + 1], mvb5[:, s, 1:2], AF.Sqrt, [mvb5.res], [rs2.res], bias=EPS)
                S.op("dve", (lambda s=s: lambda E: E.reciprocal(out=rs2[:, s:s + 1], in_=rs2[:, s:s + 1]))(),
                     [rs2.res], [rs2.res])
                stt(nm2[:, s:s + 1], mvb5[:, s, 0:1], -1.0, rs2[:, s:s + 1], ALU.mult, ALU.mult,
                    [mvb5.res, rs2.res], [nm2.res])
                ob = o4[oi % 2]
                oi += 1
                act(ob[:, :], tbb[:, :], AF.Identity, [tbb.res, rs2.res, nm2.res], [ob.res],
                    bias=nm2[:, s:s + 1], scale=rs2[:, s:s + 1])
                tt("pool", ob[:, :], ob[:, :], rows["ln2_g"][:, :], ALU.mult, [ob.res, rows["ln2_g"].res], [ob.res])
                tt("dve", ob[:, :], ob[:, :], rows["ln2_b"][:, :], ALU.add, [ob.res, rows["ln2_b"].res], [ob.res])
                dma("sp", ob.lane, out[it * TT3 + s * 128:it * TT3 + (s + 1) * 128, :], ob[:, :],
                    R=[ob.res], WF=[R_out])
            if it + 1 < NT3:
                p4_load_late(it + 1)
        S.barrier()
        S.run()
    return nc


_NC_CACHE = {}


def _cols(v, p):
    v = np.asarray(v, np.float32).reshape(-1, p).T
    o = np.zeros((128, v.shape[1]), np.float32)
    o[:p] = v
    return o


def kernel(x, emb_ln_g, emb_ln_b, w_in, b_in, rnn_conv_w, rnn_conv_b, rg_gate_w, rg_gate_b,
           rg_a_param, w_branch_a, conv_w, conv_b, conv_ln_g, conv_ln_b, w_branch_b,
           w_out, b_out, ln1_g, ln1_b, w_up, b_up, w_down, b_down, ln2_g, ln2_b):
    f = lambda a: np.ascontiguousarray(np.asarray(a, np.float32))
    x = f(x)
    b_in0 = f(b_in)[0]
    parts = {
        "b_xr": _cols(b_in0[0:DR], 128), "b_yr": _cols(b_in0[DR:2 * DR], 128),
        "b_xca": _cols(b_in0[2 * DR:2 * DR + D], 128), "b_xcb": _cols(b_in0[2 * DR + D:2 * DR + 2 * D], 128),
        "b_g0": _cols(b_in0[5120:5120 + D], 128), "b_g1": _cols(b_in0[5120 + D:], 128),
        "emb_g": _cols(emb_ln_g, 128), "emb_b": _cols(emb_ln_b, 128),
        "conv_b": _cols(f(conv_b)[0], 128), "cln_g": _cols(f(conv_ln_g)[0], 128), "cln_b": _cols(f(conv_ln_b)[0], 128),
        "ln1_g": _cols(f(ln1_g)[0], 128), "ln1_b": _cols(f(ln1_b)[0], 128), "b_up": _cols(f(b_up)[0], 128),
        "convw": np.ascontiguousarray(f(conv_w)[0].reshape(31, 8, 128).transpose(2, 1, 0)).reshape(128, 8 * 31),
    }
    w4 = np.zeros((128, 64), np.float32)
    w4[:BW] = f(rnn_conv_w)[0].reshape(4, NB, BW).transpose(2, 1, 0).reshape(BW, 64)
    parts["w4"] = w4
    parts["b4"] = _cols(f(rnn_conv_b)[0], BW)
    gb = np.zeros((128, 64), np.float32)
    gb[:BW] = f(rg_gate_b)[0].reshape(64, BW).T
    parts["gb"] = gb
    ap_ = np.zeros((128, 32), np.float32)
    ap_[:BW] = f(rg_a_param)[0].reshape(2 * NB, BW).T
    parts["ap"] = ap_
    vecs = np.zeros((128, NV), np.float32)
    for nm, off in _V.items():
        a = parts[nm]
        vecs[:, off:off + a.shape[1]] = a
    rowsrc = {"emb_g": emb_ln_g, "emb_b": emb_ln_b, "b_out": f(b_out)[0], "ln1_g": f(ln1_g)[0],
              "ln1_b": f(ln1_b)[0], "b_down": f(b_down)[0], "ln2_g": f(ln2_g)[0], "ln2_b": f(ln2_b)[0]}
    rows = np.ascontiguousarray(np.stack(
        [np.broadcast_to(f(rowsrc[n]).reshape(1, D), (128, D)) for n in ROWS]))
    wg = np.ascontiguousarray(f(rg_gate_w)[0].reshape(64, BW, BW).transpose(1, 0, 2)).reshape(BW, 64 * BW)
    shared = {
        "w_in": f(w_in)[0], "w_ba": f(w_branch_a)[0], "w_bb": f(w_branch_b)[0], "w_out": f(w_out)[0],
        "w_up": f(w_up)[0], "w_dn": f(w_down)[0], "wg": wg, "vecs": vecs, "rows": rows,
        "ident": np.eye(128, dtype=np.float32),
    }
    if "nc" not in _NC_CACHE:
        _NC_CACHE["nc"] = build_nc()
    nc = _NC_CACHE["nc"]
    in_maps = [dict(shared, x=x[b]) for b in range(8)]
    res = run_bass_kernel_spmd(nc, in_maps, core_ids=list(range(8)))
    return np.stack([np.asarray(r["out"], np.float32) for r in res.results], axis=0)
```
